# Optimizing a Trainium2 kernel written in Bass

```python
import math
import jax
import jax.numpy as jnp
from jax import lax
import numpy as np

D_MODEL = 1024
BATCH = 2
SEQ = 8192
DEPTH = 2
DEC_BATCH = 32
DEC_SEQ = 8
PAST_LEN = 8192
PAGE_SIZE = 128

MIX_WIDTH = D_MODEL
S5_WIDTH = MIX_WIDTH // 2
S5_GROUP_CH = 16
S5_GROUPS = S5_WIDTH // S5_GROUP_CH
S5_STATE = 64
HGRN_DK = 128
HGRN_HEADS = (MIX_WIDTH // 2) // HGRN_DK
HGRN_DV = HGRN_DK
HGRN_CHUNK = 32
ATT_HEAD_DIM = 64
FOX_HEADS = (MIX_WIDTH // 2) // ATT_HEAD_DIM
SB_HEADS = FOX_HEADS
Q_BLOCK = 128
D_FF = 2816
N_ADA = 9
EPS = 1e-6
FOX_BIAS_INIT = 6.0
SB_BIAS_INIT = -5.0
AB_IN = S5_WIDTH + 4 * HGRN_HEADS * HGRN_DK
CD_IN = 3 * FOX_HEADS * ATT_HEAD_DIM + FOX_HEADS + 3 * SB_HEADS * ATT_HEAD_DIM

kernel_name = 'hybrid_s5_hgrn2_fox_stickbreak_step'


def rms_norm(x):
    xf = x.astype(jnp.float32)
    return (xf * lax.rsqrt(jnp.mean(xf * xf, axis=-1, keepdims=True) + EPS)).astype(x.dtype)


def modulate(x, shift, scale):
    return rms_norm(x) * (1.0 + scale) + shift


def swiglu(h, w_in, w_out):
    g, u = jnp.split(h @ w_in, 2, axis=-1)
    return (jax.nn.silu(g) * u) @ w_out


def cmul(ar, ai, br, bi):
    return ar * br - ai * bi, ar * bi + ai * br


def s5_mix(u, h0_re, h0_im, a_re, a_im, log_dt, b_re, b_im, c_re, c_im, d_skip, glu_w, glu_b):
    f32 = jnp.float32
    nb, L, _ = u.shape
    uf = u.astype(f32).reshape(nb, L, S5_GROUPS, S5_GROUP_CH)
    lam_re = a_re.astype(f32)
    lam_im = a_im.astype(f32)
    dt = jnp.exp(log_dt.astype(f32))[:, None]
    mag = jnp.exp(lam_re * dt)
    lbar_re = mag * jnp.cos(lam_im * dt)
    lbar_im = mag * jnp.sin(lam_im * dt)
    den = lam_re * lam_re + lam_im * lam_im
    nr = lbar_re - 1.0
    zr = (nr * lam_re + lbar_im * lam_im) / den
    zi = (lbar_im * lam_re - nr * lam_im) / den
    bb_re, bb_im = cmul(zr[..., None], zi[..., None], b_re.astype(f32), b_im.astype(f32))
    bu_re = jnp.einsum('blgh,gph->blgp', uf, bb_re)
    bu_im = jnp.einsum('blgh,gph->blgp', uf, bb_im)
    ar = jnp.broadcast_to(lbar_re, bu_re.shape)
    ai = jnp.broadcast_to(lbar_im, bu_im.shape)

    def combine(e1, e2):
        a1r, a1i, b1r, b1i = e1
        a2r, a2i, b2r, b2i = e2
        nar, nai = cmul(a2r, a2i, a1r, a1i)
        nbr, nbi = cmul(a2r, a2i, b1r, b1i)
        return nar, nai, nbr + b2r, nbi + b2i

    pr, pim, hr, hi = lax.associative_scan(combine, (ar, ai, bu_re, bu_im), axis=1)
    cr, ci = cmul(pr, pim, h0_re.astype(f32)[:, None], h0_im.astype(f32)[:, None])
    hr = hr + cr
    hi = hi + ci
    y = (jnp.einsum('blgp,ghp->blgh', hr, c_re.astype(f32))
         - jnp.einsum('blgp,ghp->blgh', hi, c_im.astype(f32))
         + d_skip.astype(f32) * uf)
    y = jax.nn.gelu(y.reshape(nb, L, S5_WIDTH))
    out = y * jax.nn.sigmoid(y @ glu_w.astype(f32) + glu_b.astype(f32))
    return out, hr[:, -1], hi[:, -1]


def hgrn2_scan(q, k, v, logf, S0):
    f32 = jnp.float32
    nb, L, nh, _ = q.shape
    dv = v.shape[-1]
    C = math.gcd(L, HGRN_CHUNK)
    n = L // C

    def chunks(t):
        return t.astype(f32).reshape(nb, n, C, nh, t.shape[-1]).transpose(1, 0, 3, 2, 4)

    causal = jnp.tril(jnp.ones((C, C), dtype=bool))

    def step(S, xs):
        qc, kc, vc, gc = xs
        b = jnp.cumsum(gc, axis=2)
        q_in = qc * jnp.exp(b)
        k_in = kc * jnp.exp(-b)
        scores = jnp.where(causal, jnp.einsum('bhtd,bhsd->bhts', q_in, k_in), 0.0)
        o = jnp.einsum('bhtd,bhde->bhte', q_in, S) + jnp.einsum('bhts,bhse->bhte', scores, vc)
        b_last = b[:, :, -1:, :]
        S_new = (jnp.exp(b_last[:, :, 0, :])[..., None] * S
                 + jnp.einsum('bhsd,bhse->bhde', kc * jnp.exp(b_last - b), vc))
        return S_new, o

    S_fin, o = lax.scan(step, S0.astype(f32), (chunks(q), chunks(k), chunks(v), chunks(logf)))
    o = o.transpose(1, 0, 3, 2, 4).reshape(nb, L, nh, dv)
    return o, S_fin


def ab_mixer(h, h0_re, h0_im, S0, layer_idx, ab_w_in, ab_w_out, s5_a_re, s5_a_im, s5_log_dt,
             s5_b_re, s5_b_im, s5_c_re, s5_c_im, s5_d, s5_glu_w, s5_glu_b, hgrn_lb_logits, hgrn_norm_w):
    f32 = jnp.float32
    nb, L, _ = h.shape
    proj = h @ ab_w_in
    u = proj[..., :S5_WIDTH]
    hw = HGRN_HEADS * HGRN_DK
    q_raw, f_raw, i_raw, g_raw = [
        proj[..., S5_WIDTH + j * hw:S5_WIDTH + (j + 1) * hw].reshape(nb, L, HGRN_HEADS, HGRN_DK)
        for j in range(4)]
    s5_out, s5_re, s5_im = s5_mix(u, h0_re, h0_im, s5_a_re, s5_a_im, s5_log_dt, s5_b_re, s5_b_im,
                                  s5_c_re, s5_c_im, s5_d, s5_glu_w, s5_glu_b)
    lb = jnp.cumsum(jax.nn.softmax(hgrn_lb_logits.astype(f32), axis=0), axis=0)[layer_idx]
    lb = lb.reshape(HGRN_HEADS, HGRN_DK)
    f = lb + (1.0 - lb) * jax.nn.sigmoid(f_raw.astype(f32))
    q = jax.nn.silu(q_raw.astype(f32))
    o, S_fin = hgrn2_scan(q, 1.0 - f, i_raw.astype(f32), jnp.log(f), S0)
    o = rms_norm(o) * hgrn_norm_w.astype(f32) * jax.nn.silu(g_raw.astype(f32))
    mixed = jnp.concatenate([s5_out, o.reshape(nb, L, hw)], axis=-1).astype(h.dtype)
    return mixed @ ab_w_out, (s5_re, s5_im, S_fin)


def fox_attend(q, k, v, fq_cum, fkT, q_pos, k_pos):
    s = jnp.einsum('bqhd,bkhd->bhqk', q, k).astype(jnp.float32) * ATT_HEAD_DIM ** -0.5
    s = s + jnp.swapaxes(fq_cum, 1, 2)[..., None] - fkT
    s = jnp.where(k_pos[None, :] <= q_pos[:, None], s, -jnp.inf)
    p = jax.nn.softmax(s, axis=-1)
    return jnp.einsum('bhqk,bkhd->bqhd', p.astype(v.dtype), v)


def sb_attend(q, k, v, sb_bias, q_pos, k_pos):
    z = jnp.einsum('bqhd,bkhd->bhqk', q, k).astype(jnp.float32) * ATT_HEAD_DIM ** -0.5
    z = z + sb_bias.astype(jnp.float32)[None, :, None, None]
    mask = k_pos[None, :] < q_pos[:, None]
    log_keep = jnp.where(mask, jax.nn.log_sigmoid(-z), 0.0)
    later = lax.cumsum(log_keep, axis=3, reverse=True) - log_keep
    w = jnp.where(mask, jnp.exp(jax.nn.log_sigmoid(z) + later), 0.0)
    return jnp.einsum('bhqk,bkhd->bqhd', w.astype(v.dtype), v)


def cd_attention(fq, fk, fv, fq_cum, fk_cum, sq, sk, sv, sb_bias, q_offset):
    nb, L, _, _ = fq.shape
    T = math.gcd(L, Q_BLOCK)
    n = L // T
    k_pos = jnp.arange(fk.shape[1])
    q_pos = q_offset + jnp.arange(L).reshape(n, T)
    fkT = jnp.swapaxes(fk_cum, 1, 2)[:, :, None, :]

    def blocks(t):
        return jnp.moveaxis(t.reshape((nb, n, T) + t.shape[2:]), 1, 0)

    def one(xs):
        fqb, fcb, sqb, qp = xs
        return (fox_attend(fqb, fk, fv, fcb, fkT, qp, k_pos), sb_attend(sqb, sk, sv, sb_bias, qp, k_pos))

    fo, so = lax.map(one, (blocks(fq), blocks(fq_cum), blocks(sq), q_pos))
    fo = jnp.moveaxis(fo, 0, 1).reshape(nb, L, FOX_HEADS * ATT_HEAD_DIM)
    so = jnp.moveaxis(so, 0, 1).reshape(nb, L, SB_HEADS * ATT_HEAD_DIM)
    return fo, so


def cd_mixer(h, past_fk, past_fv, past_flogf, past_sk, past_sv, cd_w_in, cd_b_f, cd_b_sb, cd_w_out):
    f32 = jnp.float32
    nb, L, _ = h.shape
    proj = h @ cd_w_in
    w = FOX_HEADS * ATT_HEAD_DIM
    fq, fk, fv = [proj[..., j * w:(j + 1) * w].reshape(nb, L, FOX_HEADS, ATT_HEAD_DIM) for j in range(3)]
    f_logit = proj[..., 3 * w:3 * w + FOX_HEADS]
    off = 3 * w + FOX_HEADS
    ws = SB_HEADS * ATT_HEAD_DIM
    sq, sk, sv = [proj[..., off + j * ws:off + (j + 1) * ws].reshape(nb, L, SB_HEADS, ATT_HEAD_DIM)
                  for j in range(3)]
    logf = jax.nn.log_sigmoid((f_logit + cd_b_f).astype(f32))
    f_new = jnp.cumsum(logf, axis=1)
    pl = past_flogf.astype(f32)
    f_past = -(jnp.flip(jnp.cumsum(jnp.flip(pl, 1), axis=1), 1) - pl)
    fk_cum = jnp.concatenate([f_past, f_new], axis=1)
    fk_all = jnp.concatenate([past_fk, fk], axis=1)
    fv_all = jnp.concatenate([past_fv, fv], axis=1)
    sk_all = jnp.concatenate([past_sk, sk], axis=1)
    sv_all = jnp.concatenate([past_sv, sv], axis=1)
    fo, so = cd_attention(fq, fk_all, fv_all, f_new, fk_cum, sq, sk_all, sv_all, cd_b_sb, past_fk.shape[1])
    mixed = jnp.concatenate([fo, so], axis=-1).astype(h.dtype)
    return mixed @ cd_w_out, (fk, fv, logf, sk, sv)


def gather_pages(pool, page_table):
    rows = pool[page_table]
    return rows.reshape((page_table.shape[0], -1) + pool.shape[2:])


def trunk(x, c, ab_state, cd_past, ab_weights, cd_weights, ada_w, ada_b, ffn_w_in, ffn_w_out, final_norm_w):
    new_states = []
    for l in range(DEPTH):
        mod = (jax.nn.silu(c) @ ada_w[l] + ada_b[l]).reshape(c.shape[0], N_ADA, 1, D_MODEL)
        h = modulate(x, mod[:, 0], mod[:, 1])
        x = x + 0.5 * mod[:, 2] * swiglu(h, ffn_w_in[l, 0], ffn_w_out[l, 0])
        h = modulate(x, mod[:, 3], mod[:, 4])
        if l % 2 == 0:
            out, st = ab_mixer(h, ab_state[0], ab_state[1], ab_state[2], l, *ab_weights)
        else:
            out, st = cd_mixer(h, cd_past[0], cd_past[1], cd_past[2], cd_past[3], cd_past[4], *cd_weights)
        x = x + mod[:, 5] * out
        h = modulate(x, mod[:, 6], mod[:, 7])
        x = x + 0.5 * mod[:, 8] * swiglu(h, ffn_w_in[l, 1], ffn_w_out[l, 1])
        new_states.append(st)
    return rms_norm(x) * final_norm_w, new_states


def setup_inputs(seed: int = 0) -> dict:
    key = jax.random.key(seed)
    ks = jax.random.split(key, 36)
    f32 = jnp.float32
    n_pages = PAST_LEN // PAGE_SIZE
    n_used = DEC_BATCH * n_pages
    n_phys = n_used + (n_used + 3) // 4

    def nrm(k, shape, s=1.0):
        return s * jax.random.normal(k, shape, f32)

    page_table = jax.random.permutation(ks[10], n_phys)[:n_used].reshape(DEC_BATCH, n_pages).astype(jnp.int32)
    return {
        'x_prompt': nrm(ks[0], (BATCH, SEQ, D_MODEL)),
        'x_sample': nrm(ks[1], (DEC_BATCH, DEC_SEQ, D_MODEL)),
        'state_s5_re': nrm(ks[2], (DEC_BATCH, S5_GROUPS, S5_STATE), 0.1),
        'state_s5_im': nrm(ks[3], (DEC_BATCH, S5_GROUPS, S5_STATE), 0.1),
        'state_hgrn': nrm(ks[4], (DEC_BATCH, HGRN_HEADS, HGRN_DK, HGRN_DV), 0.5),
        'cache_fox_k': nrm(ks[5], (n_phys, PAGE_SIZE, FOX_HEADS, ATT_HEAD_DIM)),
        'cache_fox_v': nrm(ks[6], (n_phys, PAGE_SIZE, FOX_HEADS, ATT_HEAD_DIM)),
        'cache_fox_logf': jax.nn.log_sigmoid(FOX_BIAS_INIT + nrm(ks[7], (n_phys, PAGE_SIZE, FOX_HEADS))),
        'cache_sb_k': nrm(ks[8], (n_phys, PAGE_SIZE, SB_HEADS, ATT_HEAD_DIM)),
        'cache_sb_v': nrm(ks[9], (n_phys, PAGE_SIZE, SB_HEADS, ATT_HEAD_DIM)),
        'page_table': page_table,
        'c_prompt': nrm(ks[11], (BATCH, D_MODEL)),
        'c_sample': nrm(ks[12], (DEC_BATCH, D_MODEL)),
        'ada_w': nrm(ks[13], (DEPTH, D_MODEL, N_ADA * D_MODEL), 0.5 * D_MODEL ** -0.5),
        'ada_b': nrm(ks[14], (DEPTH, N_ADA * D_MODEL), 0.01),
        'ffn_w_in': nrm(ks[15], (DEPTH, 2, D_MODEL, 2 * D_FF), D_MODEL ** -0.5),
        'ffn_w_out': nrm(ks[16], (DEPTH, 2, D_FF, D_MODEL), D_FF ** -0.5),
        'ab_w_in': nrm(ks[17], (D_MODEL, AB_IN), D_MODEL ** -0.5),
        'ab_w_out': nrm(ks[18], (MIX_WIDTH, D_MODEL), MIX_WIDTH ** -0.5),
        's5_a_re': -0.5 + nrm(ks[19], (S5_GROUPS, S5_STATE), 0.01),
        's5_a_im': math.pi * jnp.arange(S5_STATE, dtype=f32)[None, :] + nrm(ks[20], (S5_GROUPS, S5_STATE), 0.01),
        's5_log_dt': jax.random.uniform(ks[21], (S5_GROUPS,), f32, math.log(1e-3), math.log(1e-1)),
        's5_b_re': nrm(ks[22], (S5_GROUPS, S5_STATE, S5_GROUP_CH), S5_GROUP_CH ** -0.5),
        's5_b_im': nrm(ks[23], (S5_GROUPS, S5_STATE, S5_GROUP_CH), S5_GROUP_CH ** -0.5),
        's5_c_re': nrm(ks[24], (S5_GROUPS, S5_GROUP_CH, S5_STATE), S5_STATE ** -0.5),
        's5_c_im': nrm(ks[25], (S5_GROUPS, S5_GROUP_CH, S5_STATE), S5_STATE ** -0.5),
        's5_d': nrm(ks[26], (S5_GROUPS, S5_GROUP_CH)),
        's5_glu_w': nrm(ks[27], (S5_WIDTH, S5_WIDTH), S5_WIDTH ** -0.5),
        's5_glu_b': nrm(ks[28], (S5_WIDTH,), 0.01),
        'hgrn_lb_logits': nrm(ks[29], (DEPTH + 1, HGRN_HEADS * HGRN_DK), 0.1),
        'hgrn_norm_w': 1.0 + nrm(ks[30], (HGRN_DV,), 0.01),
        'cd_w_in': nrm(ks[31], (D_MODEL, CD_IN), D_MODEL ** -0.5),
        'cd_b_f': FOX_BIAS_INIT + nrm(ks[32], (FOX_HEADS,), 0.1),
        'cd_b_sb': SB_BIAS_INIT + nrm(ks[35], (SB_HEADS,), 0.1),
        'cd_w_out': nrm(ks[33], (MIX_WIDTH, D_MODEL), MIX_WIDTH ** -0.5),
        'final_norm_w': 1.0 + nrm(ks[34], (D_MODEL,), 0.01),
    }


def reference(x_prompt, x_sample, state_s5_re, state_s5_im, state_hgrn, cache_fox_k, cache_fox_v,
              cache_fox_logf, cache_sb_k, cache_sb_v, page_table, c_prompt, c_sample, ada_w, ada_b,
              ffn_w_in, ffn_w_out, ab_w_in, ab_w_out, s5_a_re, s5_a_im, s5_log_dt, s5_b_re, s5_b_im,
              s5_c_re, s5_c_im, s5_d, s5_glu_w, s5_glu_b, hgrn_lb_logits, hgrn_norm_w, cd_w_in, cd_b_f,
              cd_b_sb, cd_w_out, final_norm_w):
    f32 = jnp.float32
    ab_weights = (ab_w_in, ab_w_out, s5_a_re, s5_a_im, s5_log_dt, s5_b_re, s5_b_im, s5_c_re, s5_c_im,
                  s5_d, s5_glu_w, s5_glu_b, hgrn_lb_logits, hgrn_norm_w)
    cd_weights = (cd_w_in, cd_b_f, cd_b_sb, cd_w_out)

    bp = x_prompt.shape[0]
    ab_state_p = (jnp.zeros((bp, S5_GROUPS, S5_STATE), f32), jnp.zeros((bp, S5_GROUPS, S5_STATE), f32),
                  jnp.zeros((bp, HGRN_HEADS, HGRN_DK, HGRN_DV), f32))
    cd_past_p = (jnp.zeros((bp, 0, FOX_HEADS, ATT_HEAD_DIM), x_prompt.dtype),
                 jnp.zeros((bp, 0, FOX_HEADS, ATT_HEAD_DIM), x_prompt.dtype),
                 jnp.zeros((bp, 0, FOX_HEADS), f32),
                 jnp.zeros((bp, 0, SB_HEADS, ATT_HEAD_DIM), x_prompt.dtype),
                 jnp.zeros((bp, 0, SB_HEADS, ATT_HEAD_DIM), x_prompt.dtype))
    y_prompt, states_p = trunk(x_prompt, c_prompt, ab_state_p, cd_past_p, ab_weights, cd_weights,
                               ada_w, ada_b, ffn_w_in, ffn_w_out, final_norm_w)

    ab_state_s = (state_s5_re, state_s5_im, state_hgrn)
    cd_past_s = (gather_pages(cache_fox_k, page_table), gather_pages(cache_fox_v, page_table),
                 gather_pages(cache_fox_logf, page_table), gather_pages(cache_sb_k, page_table),
                 gather_pages(cache_sb_v, page_table))
    y_sample, states_s = trunk(x_sample, c_sample, ab_state_s, cd_past_s, ab_weights, cd_weights,
                               ada_w, ada_b, ffn_w_in, ffn_w_out, final_norm_w)

    (s5r_p, s5i_p, hg_p), (fk_p, fv_p, flf_p, sk_p, sv_p) = states_p
    (s5r_s, s5i_s, hg_s), (fk_s, fv_s, flf_s, sk_s, sv_s) = states_s
    return (y_prompt, y_sample,
            s5r_p, s5i_p, hg_p, fk_p, fv_p, flf_p, sk_p, sv_p,
            s5r_s, s5i_s, hg_s, fk_s, fv_s, flf_s, sk_s, sv_s)
```

```python
import contextlib
import numpy as np
import concourse.bass as bass
import concourse.mybir as mybir
from concourse.bass_utils import run_bass_kernel_spmd

F32 = mybir.dt.float32
BF16 = mybir.dt.bfloat16
I32 = mybir.dt.int32
AF = mybir.ActivationFunctionType
ALU = mybir.AluOpType
AX = mybir.AxisListType

D = 1024
KT = 8
DFF = 2816
HT = 22
NADA = 9
EPS = 1e-6
NS = 4
LS = 8
NSTOK = NS * LS


class Sched:
    EPOCH = 8000
    NDMA = 32

    def __init__(self, nc, es):
        self.nc = nc
        self.es = es
        self.engs = {"pe": nc.tensor, "act": nc.scalar, "dve": nc.vector, "pool": nc.gpsimd, "sp": nc.sync}
        self.q = {e: [] for e in self.engs}
        self.cnt = {e: 0 for e in self.engs}
        self.esems = {e: [] for e in self.engs}
        self.dsems = [es.enter_context(nc.semaphore(f"dma{i}")) for i in range(2 * self.NDMA)]
        self.duse = [0] * (2 * self.NDMA)
        self.drr = {"sp": 0, "pool": 0}
        self.lastw = {}
        self.readers = {}
        self.seen = {e: {} for e in self.engs}
        self.nops = 0
        self.pbar = {e: [] for e in self.engs}
        self.lasttok = {}

    def barrier(self):
        toks = list(self.lasttok.values())
        for i, s_ in enumerate(self.dsems):
            if self.duse[i]:
                toks.append((s_, self.duse[i] * 16, "dma"))
        for e in self.engs:
            self.pbar[e] = list(toks)

    def _esem(self, e, epoch):
        while len(self.esems[e]) <= epoch:
            self.esems[e].append(self.es.enter_context(self.nc.semaphore(f"s_{e}{len(self.esems[e])}")))
        return self.esems[e][epoch]

    def op(self, e, fn, r=(), w=(), dma=False):
        r = list(r)
        w = list(w)
        for k in list(r):
            if len(k) == 3 and k.startswith("ps") and k[2].isdigit():
                r.remove(k)
                if k not in w:
                    w.append(k)
        deps = []
        for k in list(r) + list(w):
            t = self.lastw.get(k)
            if t is not None:
                deps.append(t)
        for k in w:
            deps.extend(self.readers.get(k, ()))
        if self.pbar[e]:
            deps.extend(self.pbar[e])
            self.pbar[e] = []
        waits = []
        seen = self.seen[e]

        def need(tok):
            sem, val, eng = tok
            if eng == "pe" and e == "pe" and not dma:
                return
            if seen.get(id(sem), 0) >= val:
                return
            seen[id(sem)] = val
            waits.append((sem, val))

        for t in deps:
            need(t)
        if dma:
            qn = "pool" if e == "pool" else "sp"
            i = self.drr[qn] + (self.NDMA if qn == "pool" else 0)
            self.drr[qn] = (self.drr[qn] + 1) % self.NDMA
            pv = self.duse[i] * 16
            if pv > 0:
                need((self.dsems[i], pv, "dma"))
            self.duse[i] += 1
            tok = (self.dsems[i], pv + 16, "dma")
            inc = (self.dsems[i], 16)
        else:
            c = self.cnt[e]
            self.cnt[e] += 1
            ep, v = divmod(c, self.EPOCH)
            sem = self._esem(e, ep)
            tok = (sem, v + 1, e)
            inc = (sem, 1)
        self.q[e].append((waits, fn, inc))
        if not dma:
            self.lasttok[e] = tok
        for k in w:
            self.lastw[k] = tok
            self.readers[k] = []
        for k in r:
            self.readers.setdefault(k, []).append(tok)
        self.nops += 1
        return tok

    def emit(self):
        nc = self.nc
        fin = []
        for i, s in enumerate(self.dsems):
            if self.duse[i]:
                fin.append((s, self.duse[i] * 16))
        for e in self.engs:
            c = self.cnt[e]
            if c:
                ep, v = divmod(c - 1, self.EPOCH)
                fin.append((self.esems[e][ep], v + 1))
        qs = self.q

        def replay(e, eng):
            for waits, fn, inc in qs[e]:
                for sem, val in waits:
                    eng.wait_ge(sem, val)
                ins = fn(eng)
                ins.then_inc(inc[0], inc[1])

        with nc.Block() as block:
            @block.tensor
            def _(eng):
                replay("pe", eng)

            @block.scalar
            def _(eng):
                replay("act", eng)

            @block.vector
            def _(eng):
                replay("dve", eng)

            @block.gpsimd
            def _(eng):
                replay("pool", eng)

            @block.sync
            def _(eng):
                replay("sp", eng)
                for sem, val in fin:
                    eng.wait_ge(sem, val)


class Ctx:
    pass


DEBUG = False
DBG5 = False
ASTOP = 99
SBCUT = 99
STAGE = 99


def build(T, PAST, NPHYS=2560):
    nc = bass.Bass("TRN2", target_bir_lowering=False)
    es = contextlib.ExitStack()
    S = Sched(nc, es)
    NB = T // 512
    NTOK = T + NSTOK
    blocks = [(i * 512, 512) for i in range(NB)] + [(T, NSTOK)]

    def dram_in(name, shape, dt=F32):
        return nc.dram_tensor(name, list(shape), dt, kind="ExternalInput").ap()

    def dram_out(name, shape, dt=F32):
        return nc.dram_tensor(name, list(shape), dt, kind="ExternalOutput").ap()

    def dram_tmp(name, shape, dt):
        if DEBUG:
            return nc.dram_tensor(name, list(shape), dt, kind="ExternalOutput").ap()
        return nc.dram_tensor(name, list(shape), dt).ap()

    def sb(name, shape, dt=F32):
        return es.enter_context(nc.sbuf_tensor(name, list(shape), dt))

    xT = dram_in("xT", [D, NTOK])
    cT = dram_in("cT", [D, 1 + NS])
    ada_w = dram_in("ada_w", [2, D, NADA * D])
    ada_bT = dram_in("ada_bT", [2, 128, NADA * KT])
    ffn_w_in = dram_in("ffn_w_in", [2, 2, D, 2 * DFF])
    ffn_w_out = dram_in("ffn_w_out", [2, 2, DFF, D])
    fnw = dram_in("fnw", [128, KT])
    yT = dram_out("yT", [D, NTOK])
    xs = dram_tmp("xs", [D, NTOK], F32)
    hbuf = dram_tmp("hbuf", [D, NTOK], BF16)
    hid = dram_tmp("hid", [DFF, NTOK], BF16)

    ps = [es.enter_context(nc.psum_tensor(f"ps{i}", [128, 512], F32)) for i in range(8)]
    psk = [f"ps{i}" for i in range(8)]
    psrr = [0]

    def nextps():
        i = psrr[0]
        psrr[0] = (i + 1) % 8
        return i

    ones_bf = sb("ones_bf", [128, 128], BF16)
    S.op("pool", lambda e: e.memset(ones_bf[:], 1.0), w=["ones_bf"])

    ARENA_W = 47104
    arena = sb("arena", [128, ARENA_W], F32)
    apos = [0]

    def aalloc(shape, dt=F32):
        n = int(np.prod(shape))
        words = n if dt == F32 or dt == I32 else (n + 1) // 2
        words = (words + 7) // 8 * 8
        a = apos[0]
        assert a + words <= ARENA_W, ("arena overflow", a, words)
        apos[0] = a + words
        v = arena[:, a:a + words]
        if dt != F32:
            v = v.bitcast(dt)
        v = v[:, 0:n]
        if len(shape) == 2:
            v = v.rearrange("p (a b) -> p a b", a=shape[0])
        elif len(shape) == 3:
            v = v.rearrange("p (a b c) -> p a b c", a=shape[0], b=shape[1])
        return v

    def areset():
        S.barrier()
        apos[0] = 0

    def token_bufs():
        areset()
        g = {}
        g["wbig"] = aalloc([25600], BF16)
        g["xb"] = [aalloc([KT, 512], F32) for i in range(2)]
        g["hb"] = [aalloc([HT, 512], BF16) for i in range(2)]
        g["ob"] = [aalloc([HT, 512], BF16)] * 2
        g["sq"] = aalloc([KT, 512], BF16)
        g["t1"] = [aalloc([512], F32) for i in range(2)]
        g["t2"] = [aalloc([512], F32) for i in range(2)]
        g["rstd"] = aalloc([512], F32)
        return g

    tb = token_bufs()
    wbig, xb, hb, ob, sq, t1, t2, rstd = (tb[k] for k in ["wbig", "xb", "hb", "ob", "sq", "t1", "t2", "rstd"])
    modv = [sb(f"modv{l}", [128, NADA * KT, 1 + NS], F32) for l in range(2)]
    s1v = [sb(f"s1v{l}", [128, NADA * KT, 1 + NS], F32) for l in range(2)]
    ghv = [sb(f"ghv{l}", [128, NADA * KT, 1 + NS], F32) for l in range(2)]
    modS = sb("modS", [128, 3, KT, NSTOK], F32)
    fnw_sb = sb("fnw_sb", [128, KT], F32)
    S.op("sp", lambda e: e.dma_start(out=fnw_sb[:], in_=fnw[:, :]), w=["fnw_sb"], dma=True)

    def ada_phase():
        csb = sb("csb", [128, KT, 1 + NS], F32)
        scs = sb("scs", [128, KT, 1 + NS], F32)
        adab = sb("adab", [128, NADA * KT], F32)
        S.op("sp", lambda e: e.dma_start(out=csb[:], in_=cT.rearrange("(k p) j -> p k j", p=128)), w=["csb"], dma=True)
        S.op("act", lambda e: e.activation(out=scs[:], in_=csb[:], func=AF.Silu), r=["csb"], w=["scs"])
        wst = wbig.bitcast(F32).rearrange("p (s k c) -> p s k c", s=2, k=KT)
        CW = 768
        ci = 0
        for l in range(2):
            S.op("sp", lambda e, l=l: e.dma_start(out=adab[:], in_=ada_bT[l]), w=["adab"], dma=True)
            for ch in range(NADA * D // CW):
                slot = ci % 2
                ci += 1
                S.op("sp", lambda e, l=l, ch=ch, slot=slot: e.dma_start(
                    out=wst[:, slot, :, 0:CW],
                    in_=ada_w[l, :, ch * CW:(ch + 1) * CW].rearrange("(k p) c -> p k c", p=128)),
                    w=[f"wst{slot}"], dma=True)
                for fl in range(CW // 128):
                    ft = ch * (CW // 128) + fl
                    pi = nextps()
                    for kt in range(KT):
                        S.op("pe", lambda e, pi=pi, slot=slot, kt=kt, fl=fl: e.matmul(
                            ps[pi][:, 0:1 + NS], lhsT=wst[:, slot, kt, fl * 128:(fl + 1) * 128],
                            rhs=scs[:, kt, :], start=(kt == 0), stop=(kt == KT - 1)),
                            r=[f"wst{slot}", "scs"], w=[psk[pi]])
                    S.op("dve", lambda e, pi=pi, ft=ft, l=l: e.tensor_scalar(
                        out=modv[l][:, ft, :], in0=ps[pi][:, 0:1 + NS], scalar1=adab[:, ft:ft + 1], scalar2=None,
                        op0=ALU.add), r=[psk[pi], "adab"], w=[f"modv{l}"])
            S.op("dve", lambda e, l=l: e.tensor_scalar(out=s1v[l][:], in0=modv[l][:], scalar1=1.0, scalar2=None,
                                                       op0=ALU.add), r=[f"modv{l}"], w=[f"s1v{l}"])
            S.op("dve", lambda e, l=l: e.tensor_scalar(out=ghv[l][:], in0=modv[l][:], scalar1=0.5, scalar2=None,
                                                       op0=ALU.mult), r=[f"modv{l}"], w=[f"ghv{l}"])

    def set_modS(which, src, chunk):
        for j in range(NS):
            S.op("dve", lambda e, j=j: e.tensor_copy(
                out=modS[:, which, :, j * LS:(j + 1) * LS],
                in_=src[:, chunk * KT:(chunk + 1) * KT, 1 + j:2 + j].to_broadcast([128, KT, LS])),
                r=[src.name if hasattr(src, "name") else "modsrc"], w=["modS"])

    def load_w(w_ap, K, cols, key="wbig"):
        kt_n = K // 128
        F = sum(c1 - c0 for c0, c1 in cols)
        view = wbig[:, 0:kt_n * F].rearrange("p (k f) -> p k f", k=kt_n)
        for kt in range(kt_n):
            o = 0
            for (c0, c1) in cols:
                for f0 in range(c0, c1, 2048):
                    f1 = min(c1, f0 + 2048)
                    S.op("pool", lambda e, kt=kt, f0=f0, f1=f1, o=o: e.dma_start(
                        out=view[:, kt, o:o + f1 - f0], in_=w_ap[kt * 128:(kt + 1) * 128, f0:f1]),
                        w=[key, "wst0", "wst1"], dma=True)
                    o += f1 - f0
        return view

    def norm_mod_block(xblk, xkey, bi, l, chunk, final=False):
        c0, n = blocks[bi]
        slot = bi % 2
        S.op("act", lambda e: e.activation(out=sq[:, :, 0:n], in_=xblk[:, :, 0:n], func=AF.Square),
             r=[xkey], w=["sq"])
        pi = nextps()
        for kt in range(KT):
            S.op("pe", lambda e, kt=kt: e.matmul(ps[pi][:, 0:n], lhsT=ones_bf[:], rhs=sq[:, kt, 0:n],
                                                 start=(kt == 0), stop=(kt == KT - 1)),
                 r=["ones_bf", "sq"], w=[psk[pi]])
        S.op("act", lambda e: e.activation(out=rstd[:, 0:n], in_=ps[pi][:, 0:n], func=AF.Sqrt, scale=1.0 / D,
                                           bias=epsb[:, 0:1]), r=[psk[pi], "epsb"], w=["rstd"])
        S.op("dve", lambda e: e.reciprocal(out=rstd[:, 0:n], in_=rstd[:, 0:n]), r=["rstd"], w=["rstd"])
        if final:
            for kt in range(KT):
                S.op("dve", lambda e, kt=kt: e.scalar_tensor_tensor(
                    out=xblk[:, kt, 0:n], in0=xblk[:, kt, 0:n], scalar=fnw_sb[:, kt:kt + 1], in1=rstd[:, 0:n],
                    op0=ALU.mult, op1=ALU.mult), r=[xkey, "rstd", "fnw_sb"], w=[xkey])
            S.op("sp", lambda e: e.dma_start(out=yT[:, c0:c0 + n].rearrange("(k p) t -> p k t", p=128),
                                             in_=xblk[:, :, 0:n]), r=[xkey], w=["yT"], dma=True)
            return
        o = ob[slot]
        okey = "ob0"
        prompt = n == 512
        for kt in range(KT):
            tt = t1[kt % 2]
            S.op("dve", lambda e, kt=kt, tt=tt: e.tensor_tensor(out=tt[:, 0:n], in0=xblk[:, kt, 0:n],
                                                                 in1=rstd[:, 0:n], op=ALU.mult),
                 r=[xkey, "rstd"], w=[f"t1_{kt % 2}"])
            ft = chunk * KT + kt
            if prompt:
                S.op("act", lambda e, kt=kt, tt=tt, ft=ft: e.activation(
                    out=o[:, kt, 0:n], in_=tt[:, 0:n], func=AF.Identity,
                    scale=s1v[l][:, ft + KT, 0:1], bias=modv[l][:, ft, 0:1]),
                    r=[f"t1_{kt % 2}", f"s1v{l}", f"modv{l}"], w=[okey])
            else:
                S.op("dve", lambda e, kt=kt, tt=tt: e.tensor_tensor(out=tt[:, 0:n], in0=tt[:, 0:n],
                                                                     in1=modS[:, 0, kt, :], op=ALU.mult),
                     r=[f"t1_{kt % 2}", "modS"], w=[f"t1_{kt % 2}"])
                S.op("dve", lambda e, kt=kt, tt=tt: e.tensor_tensor(out=o[:, kt, 0:n], in0=tt[:, 0:n],
                                                                     in1=modS[:, 1, kt, :], op=ALU.add),
                     r=[f"t1_{kt % 2}", "modS"], w=[okey])
        S.op("sp", lambda e: e.dma_start(out=hbuf[:, c0:c0 + n].rearrange("(k p) t -> p k t", p=128),
                                         in_=o[:, 0:KT, 0:n]), r=[okey], w=[f"hbuf{bi}"], dma=True)

    epsb = sb("epsb", [128, 1], F32)
    S.op("pool", lambda e: e.memset(epsb[:], EPS), w=["epsb"])

    def prep_modS(l, chunk):
        set_modS(0, s1v[l], chunk + 1)
        set_modS(1, modv[l], chunk)

    def pass_mod(src, l, chunk, store_x):
        prep_modS(l, chunk)
        for bi, (c0, n) in enumerate(blocks):
            slot = bi % 2
            S.op("sp", lambda e, c0=c0, n=n, slot=slot: e.dma_start(
                out=xb[slot][:, :, 0:n], in_=src[:, c0:c0 + n].rearrange("(k p) t -> p k t", p=128)),
                r=[f"{src.name}{bi}"], w=[f"xb{slot}"], dma=True)
            if store_x:
                S.op("sp", lambda e, c0=c0, n=n, slot=slot: e.dma_start(
                    out=xs[:, c0:c0 + n].rearrange("(k p) t -> p k t", p=128), in_=xb[slot][:, :, 0:n]),
                    r=[f"xb{slot}"], w=[f"xs{bi}"], dma=True)
            norm_mod_block(xb[slot], f"xb{slot}", bi, l, chunk)

    def pass_ffn_in(l, j):
      HH = HT // 2
      HW = HH * 128
      for hh in range(2):
        wv = load_w(ffn_w_in[l, j], D, [(hh * HW, (hh + 1) * HW), (DFF + hh * HW, DFF + (hh + 1) * HW)])
        for bi, (c0, n) in enumerate(blocks):
            slot = bi % 2
            h = hb[slot]
            S.op("sp", lambda e, c0=c0, n=n, h=h: e.dma_start(
                out=h[:, 0:KT, 0:n], in_=hbuf[:, c0:c0 + n].rearrange("(k p) t -> p k t", p=128)),
                r=[f"hbuf{bi}"], w=[f"hb{slot}"], dma=True)
            o = ob[slot]
            for i in range(HH):
                pg = nextps()
                pu = nextps()
                for kt in range(KT):
                    S.op("pe", lambda e, kt=kt, pg=pg, i=i, h=h, n=n: e.matmul(
                        ps[pg][:, 0:n], lhsT=wv[:, kt, i * 128:(i + 1) * 128], rhs=h[:, kt, 0:n],
                        start=(kt == 0), stop=(kt == KT - 1)), r=["wbig", f"hb{slot}"], w=[psk[pg]])
                for kt in range(KT):
                    S.op("pe", lambda e, kt=kt, pu=pu, i=i, h=h, n=n: e.matmul(
                        ps[pu][:, 0:n], lhsT=wv[:, kt, HW + i * 128:HW + (i + 1) * 128], rhs=h[:, kt, 0:n],
                        start=(kt == 0), stop=(kt == KT - 1)), r=["wbig", f"hb{slot}"], w=[psk[pu]])
                tt = t2[i % 2]
                S.op("act", lambda e, pg=pg, tt=tt, n=n: e.activation(out=tt[:, 0:n], in_=ps[pg][:, 0:n], func=AF.Silu),
                     r=[psk[pg]], w=[f"t2_{i % 2}"])
                S.op("dve", lambda e, pu=pu, tt=tt, i=i, o=o, n=n: e.tensor_tensor(
                    out=o[:, i, 0:n], in0=tt[:, 0:n], in1=ps[pu][:, 0:n], op=ALU.mult),
                    r=[psk[pu], f"t2_{i % 2}"], w=["ob0"])
            S.op("sp", lambda e, c0=c0, n=n, o=o, hh=hh: e.dma_start(
                out=hid[hh * HW:(hh + 1) * HW, c0:c0 + n].rearrange("(k p) t -> p k t", p=128), in_=o[:, 0:HH, 0:n]),
                r=["ob0"], w=[f"hid{bi}"], dma=True)

    def pass_out(src, srckey, K, w_ap, l, gchunk, half, nxt):
        ktn = K // 128
        wv = load_w(w_ap, K, [(0, D)])
        gsrc = ghv[l] if half else modv[l]
        set_modS(2, gsrc, gchunk)
        if nxt not in (None, "final"):
            prep_modS(nxt[0], nxt[1])
        for bi, (c0, n) in enumerate(blocks):
            slot = bi % 2
            h = hb[slot]
            S.op("sp", lambda e, c0=c0, n=n, h=h: e.dma_start(
                out=h[:, 0:ktn, 0:n], in_=src[:, c0:c0 + n].rearrange("(k p) t -> p k t", p=128)),
                r=[f"{srckey}{bi}"], w=[f"hb{slot}"], dma=True)
            S.op("sp", lambda e, c0=c0, n=n, slot=slot: e.dma_start(
                out=xb[slot][:, :, 0:n], in_=xs[:, c0:c0 + n].rearrange("(k p) t -> p k t", p=128)),
                r=[f"xs{bi}"], w=[f"xb{slot}"], dma=True)
            x = xb[slot]
            for ft in range(KT):
                pi = nextps()
                for kt in range(ktn):
                    S.op("pe", lambda e, kt=kt, pi=pi, ft=ft, h=h, n=n: e.matmul(
                        ps[pi][:, 0:n], lhsT=wv[:, kt, ft * 128:(ft + 1) * 128], rhs=h[:, kt, 0:n],
                        start=(kt == 0), stop=(kt == ktn - 1)), r=["wbig", f"hb{slot}"], w=[psk[pi]])
                if n == 512:
                    gi = gchunk * KT + ft
                    S.op("dve", lambda e, pi=pi, ft=ft, x=x, gi=gi, n=n: e.scalar_tensor_tensor(
                        out=x[:, ft, 0:n], in0=ps[pi][:, 0:n], scalar=gsrc[:, gi:gi + 1, 0:1].rearrange("p a b -> p (a b)"),
                        in1=x[:, ft, 0:n], op0=ALU.mult, op1=ALU.add),
                        r=[psk[pi], f"xb{slot}", f"modv{l}", f"ghv{l}"], w=[f"xb{slot}"])
                else:
                    tt = t1[ft % 2]
                    S.op("dve", lambda e, pi=pi, ft=ft, tt=tt, n=n: e.tensor_tensor(
                        out=tt[:, 0:n], in0=ps[pi][:, 0:n], in1=modS[:, 2, ft, :], op=ALU.mult),
                        r=[psk[pi], "modS"], w=[f"t1_{ft % 2}"])
                    S.op("dve", lambda e, ft=ft, tt=tt, x=x, n=n: e.tensor_tensor(
                        out=x[:, ft, 0:n], in0=x[:, ft, 0:n], in1=tt[:, 0:n], op=ALU.add),
                        r=[f"t1_{ft % 2}", f"xb{slot}"], w=[f"xb{slot}"])
            if nxt == "final":
                norm_mod_block(x, f"xb{slot}", bi, 0, 0, final=True)
            else:
                S.op("sp", lambda e, c0=c0, n=n, x=x: e.dma_start(
                    out=xs[:, c0:c0 + n].rearrange("(k p) t -> p k t", p=128), in_=x[:, :, 0:n]),
                    r=[f"xb{slot}"], w=[f"xs{bi}"], dma=True)
                if nxt is not None:
                    norm_mod_block(x, f"xb{slot}", bi, nxt[0], nxt[1])


    def TT(eng, out, a, b, op, r, w):
        S.op(eng, lambda e: e.tensor_tensor(out=out, in0=a, in1=b, op=op), r=r, w=w)

    def TS(eng, out, a, s1, op0, r, w, s2=None, op1=None):
        if op1 is None:
            S.op(eng, lambda e: e.tensor_scalar(out=out, in0=a, scalar1=s1, scalar2=None, op0=op0), r=r, w=w)
        else:
            S.op(eng, lambda e: e.tensor_scalar(out=out, in0=a, scalar1=s1, scalar2=s2, op0=op0, op1=op1), r=r, w=w)

    def STT(out, a, sc, b, op0, op1, r, w):
        S.op("dve", lambda e: e.scalar_tensor_tensor(out=out, in0=a, scalar=sc, in1=b, op0=op0, op1=op1), r=r, w=w)

    def ACT(out, in_, func, r, w, scale=1.0, bias=None):
        if bias is None:
            S.op("act", lambda e: e.activation(out=out, in_=in_, func=func, scale=scale), r=r, w=w)
        else:
            S.op("act", lambda e: e.activation(out=out, in_=in_, func=func, scale=scale, bias=bias), r=r, w=w)

    def MM(out, lhsT, rhs, start, stop, r, w):
        S.op("pe", lambda e: e.matmul(out, lhsT=lhsT, rhs=rhs, start=start, stop=stop), r=r, w=w)

    def TR(out, in_, ident, r, w):
        S.op("pe", lambda e: e.transpose(out, in_, ident), r=r, w=w)

    def DMA(eng, out, in_, r, w):
        S.op(eng, lambda e: e.dma_start(out=out, in_=in_), r=r, w=w, dma=True)

    def DMAX(eng, out, in_, r, w):
        S.op(eng, lambda e: e.dma_start(out=out, in_=in_, allow_slow_non_contiguous=True), r=r, w=w, dma=True)

    def CP(eng, out, in_, r, w):
        S.op(eng, lambda e: e.tensor_copy(out=out, in_=in_), r=r, w=w)

    def MS(eng, out, val, w):
        S.op(eng, lambda e: e.memset(out, val), w=w)

    ones_f = sb("ones_f", [128, 128], F32)
    MS("pool", ones_f[:], 1.0, ["ones_f"])
    ident = sb("ident", [128, 128], F32)
    S.op("pool", lambda e: e.affine_select(out=ident[:], in_=ones_f[:], pattern=[[-1, 128]], compare_op=ALU.is_equal,
                                           fill=0.0, base=0, channel_multiplier=1), r=["ones_f"], w=["ident"])
    halfpi = sb("halfpi", [128, 1], F32)
    MS("pool", halfpi[:], float(np.pi / 2), ["halfpi"])

    stg = [sb(f"stg{i}", [128, 512], F32) for i in range(4)]
    stgrr = [0]

    def evac(ps_ap, pkey, n, np_, scale=None):
        i = stgrr[0]
        stgrr[0] = (i + 1) % 4
        o = stg[i][0:np_, 0:n]
        if i % 2 == 0:
            ACT(o, ps_ap, AF.Copy, r=[pkey], w=[f"stg{i}"], scale=(1.0 if scale is None else scale))
        else:
            if scale is None:
                CP("dve", o, ps_ap, r=[pkey], w=[f"stg{i}"])
            else:
                TS("dve", o, ps_ap, float(scale), ALU.mult, r=[pkey], w=[f"stg{i}"])
        return o, f"stg{i}"

    def pass_proj(w_ap, F, fm, tm):
        wv = load_w(w_ap, D, [(0, F)])
        for bi, (c0, n) in enumerate(blocks):
            slot = bi % 2
            h = hb[slot]
            DMA("sp", h[:, 0:KT, 0:n], hbuf[:, c0:c0 + n].rearrange("(k p) t -> p k t", p=128),
                r=[f"hbuf{bi}"], w=[f"hb{slot}"])
            for (col0, ncols, dst, row0, scale) in fm:
                for ft in range(ncols // 128):
                    pi = nextps()
                    for kt in range(KT):
                        MM(ps[pi][:, 0:n], wv[:, kt, col0 + ft * 128:col0 + (ft + 1) * 128], h[:, kt, 0:n],
                           kt == 0, kt == KT - 1, r=["wbig", f"hb{slot}"], w=[psk[pi]])
                    o, ok = evac(ps[pi][:, 0:n], psk[pi], n, 128, scale)
                    for dst_ in (dst if isinstance(dst, (list, tuple)) else [dst]):
                        DMA("sp", dst_[row0 + ft * 128:row0 + (ft + 1) * 128, c0:c0 + n], o, r=[ok], w=[f"{dst_.name}{bi}"])
            for (col0, ncols, handler) in tm:
                for sub in range((n + 127) // 128):
                    nt = min(128, n - sub * 128)
                    pi = nextps()
                    for kt in range(KT):
                        MM(ps[pi][0:nt, 0:ncols], h[:, kt, sub * 128:sub * 128 + nt], wv[:, kt, col0:col0 + ncols],
                           kt == 0, kt == KT - 1, r=["wbig", f"hb{slot}"], w=[psk[pi]])
                    handler(pi, nt, c0 + sub * 128, bi)


    ab_w_in = dram_in("ab_w_in", [D, 2560])
    ab_w_out = dram_in("ab_w_out", [D, D])
    s5_st = dram_in("s5_st", [3, 128, 16])
    s5_ch = dram_in("s5_ch", [3, 4, 128, 512])
    s5_bblk = dram_in("s5_bblk", [2, 4, 128, 512])
    s5_cblk = dram_in("s5_cblk", [2, 16, 128, 128])
    s5_dfm = dram_in("s5_dfm", [128, 4])
    glu_w = dram_in("glu_w", [512, 512])
    glu_bfm = dram_in("glu_bfm", [128, 4])
    lbl_fm = dram_in("lbl_fm", [128, 4, 3])
    hnw = dram_in("hnw", [128, 1])
    s5s_in = dram_in("s5s_in", [128, 2, 16, NS])
    hgs_in = dram_in("hgs_in", [NS * 4, 128, 128])
    s5p_out = dram_out("s5p_out", [128, 2, 16])
    s5s_out = dram_out("s5s_out", [128, 2, 16, NS])
    hgp_out = dram_out("hgp_out", [4, 128, 128])
    hgs_out = dram_out("hgs_out", [NS * 4, 128, 128])
    pr0 = dram_tmp("pr0", [2560, NTOK], F32)
    vtok0 = dram_tmp("vtok0", [NTOK, 512], F32)
    mixed = dram_tmp("mixed", [D, NTOK], BF16)

    def proj0():
        def vh(pi, nt, tok0, bi):
            o, ok = evac(ps[pi][0:nt, 0:512], psk[pi], 512, nt)
            DMA("sp", vtok0[tok0:tok0 + nt, :], o, r=[ok], w=[f"vtok0{bi}"])
        pass_proj(ab_w_in, 2560, [(0, 2560, pr0, 0, None)], [(1536, 512, vh)])

    def s5_phase():
        areset()
        K5 = ["s5p"]
        scr = aalloc([8192])
        st = aalloc([3, 16])
        DMA("sp", st, s5_st.rearrange("a p j -> p a j"), r=[], w=K5)
        dt = aalloc([16]); mag = aalloc([16]); th = aalloc([16]); cs = aalloc([16]); sn = aalloc([16])
        ta16 = aalloc([16]); tb16 = aalloc([16])
        ACT(dt, st[:, 2, :], AF.Exp, r=K5, w=K5)
        TT("dve", ta16, st[:, 0, :], dt, ALU.mult, r=K5, w=K5)
        ACT(mag, ta16, AF.Exp, r=K5, w=K5)
        TT("dve", th, st[:, 1, :], dt, ALU.mult, r=K5, w=K5)
        ACT(cs, th, AF.Sin, r=K5 + ["halfpi"], w=K5, scale=1.0 / 64, bias=halfpi[:, 0:1])
        ACT(sn, th, AF.Sin, r=K5, w=K5, scale=1.0 / 64)

        def sq_cs(c_, s_, a_, b_):
            TT("dve", a_, c_, c_, ALU.mult, r=K5, w=K5)
            TT("dve", b_, s_, s_, ALU.mult, r=K5, w=K5)
            TT("dve", b_, a_, b_, ALU.subtract, r=K5, w=K5)
            TT("dve", a_, c_, s_, ALU.mult, r=K5, w=K5)
            TS("dve", s_, a_, 2.0, ALU.mult, r=K5, w=K5)
            CP("dve", c_, b_, r=K5, w=K5)
        for _ in range(6):
            sq_cs(cs, sn, ta16, tb16)
        Wbu = aalloc([2, 4, 512], BF16)
        chp = aalloc([3, 512]); bbl = aalloc([2, 512])
        w = [scr[:, i * 512:(i + 1) * 512] for i in range(8)]
        for c in range(4):
            DMA("sp", chp, s5_ch[:, c].rearrange("a p x -> p a x"), r=K5, w=K5)
            DMA("sp", bbl, s5_bblk[:, c].rearrange("a p x -> p a x"), r=K5, w=K5)
            dtc, lrd, magc, thc, cc, sc, t0, t1_ = w
            ACT(dtc, chp[:, 2, :], AF.Exp, r=K5, w=K5)
            TT("dve", lrd, chp[:, 0, :], dtc, ALU.mult, r=K5, w=K5)
            ACT(magc, lrd, AF.Exp, r=K5, w=K5)
            TT("dve", thc, chp[:, 1, :], dtc, ALU.mult, r=K5, w=K5)
            ACT(cc, thc, AF.Sin, r=K5 + ["halfpi"], w=K5, scale=1.0 / 64, bias=halfpi[:, 0:1])
            ACT(sc, thc, AF.Sin, r=K5, w=K5, scale=1.0 / 64)
            for _ in range(6):
                sq_cs(cc, sc, t0, t1_)
            TT("dve", cc, cc, magc, ALU.mult, r=K5, w=K5)
            TT("dve", sc, sc, magc, ALU.mult, r=K5, w=K5)
            TS("dve", cc, cc, -1.0, ALU.add, r=K5, w=K5)
            lr, li = chp[:, 0, :], chp[:, 1, :]
            TT("dve", t0, lr, lr, ALU.mult, r=K5, w=K5)
            TT("dve", t1_, li, li, ALU.mult, r=K5, w=K5)
            TT("dve", t0, t0, t1_, ALU.add, r=K5, w=K5)
            S.op("dve", lambda e, t0=t0: e.reciprocal(out=t0, in_=t0), r=K5, w=K5)
            TT("dve", dtc, cc, lr, ALU.mult, r=K5, w=K5)
            TT("dve", lrd, sc, li, ALU.mult, r=K5, w=K5)
            TT("dve", dtc, dtc, lrd, ALU.add, r=K5, w=K5)
            TT("dve", dtc, dtc, t0, ALU.mult, r=K5, w=K5)
            TT("dve", lrd, sc, lr, ALU.mult, r=K5, w=K5)
            TT("dve", magc, cc, li, ALU.mult, r=K5, w=K5)
            TT("dve", lrd, lrd, magc, ALU.subtract, r=K5, w=K5)
            TT("dve", lrd, lrd, t0, ALU.mult, r=K5, w=K5)
            TT("dve", magc, dtc, bbl[:, 0, :], ALU.mult, r=K5, w=K5)
            TT("dve", thc, lrd, bbl[:, 1, :], ALU.mult, r=K5, w=K5)
            TT("dve", Wbu[:, 0, c, :], magc, thc, ALU.subtract, r=K5, w=K5)
            TT("dve", magc, dtc, bbl[:, 1, :], ALU.mult, r=K5, w=K5)
            TT("dve", thc, lrd, bbl[:, 0, :], ALU.mult, r=K5, w=K5)
            TT("dve", Wbu[:, 1, c, :], magc, thc, ALU.add, r=K5, w=K5)
        Wc = aalloc([2, 16, 128])
        DMA("sp", Wc, s5_cblk.rearrange("a j p x -> p a j x"), r=[], w=K5)
        TS("dve", Wc[:, 1], Wc[:, 1], -1.0, ALU.mult, r=K5, w=K5)
        dfm = aalloc([4]); gbf = aalloc([4])
        DMA("sp", dfm, s5_dfm[:, :], r=[], w=K5)
        DMA("sp", gbf, glu_bfm[:, :], r=[], w=K5)
        Wg = aalloc([4, 512], BF16)
        DMA("pool", Wg, glu_w.rearrange("(k p) f -> p k f", p=128), r=[], w=K5)
        Ec = aalloc([16, 512]); Es = aalloc([16, 512])
        pc = aalloc([16]); psn = aalloc([16])
        tA = scr[:, 0:4096].rearrange("p (a b) -> p a b", a=16)
        tB = scr[:, 4096:8192].rearrange("p (a b) -> p a b", a=16)
        CP("dve", pc, cs, r=K5, w=K5)
        CP("dve", psn, sn, r=K5, w=K5)
        MS("dve", Ec[:, :, 0:1], 1.0, K5)
        MS("dve", Es[:, :, 0:1], 0.0, K5)
        L = 1
        while L < 512:
            pcB = pc.unsqueeze(2).to_broadcast([128, 16, L])
            psB = psn.unsqueeze(2).to_broadcast([128, 16, L])
            TT("dve", tA[:, :, 0:L], Ec[:, :, 0:L], pcB, ALU.mult, r=K5, w=K5)
            TT("dve", tB[:, :, 0:L], Es[:, :, 0:L], psB, ALU.mult, r=K5, w=K5)
            TT("dve", Ec[:, :, L:2 * L], tA[:, :, 0:L], tB[:, :, 0:L], ALU.subtract, r=K5, w=K5)
            TT("dve", tA[:, :, 0:L], Ec[:, :, 0:L], psB, ALU.mult, r=K5, w=K5)
            TT("dve", tB[:, :, 0:L], Es[:, :, 0:L], pcB, ALU.mult, r=K5, w=K5)
            TT("dve", Es[:, :, L:2 * L], tA[:, :, 0:L], tB[:, :, 0:L], ALU.add, r=K5, w=K5)
            sq_cs(pc, psn, ta16, tb16)
            L *= 2
        if DBG5:
            dbgE = dram_out("dbgE", [128, 2, 16, 512])
            DMA("sp", dbgE[:, 0], Ec, r=K5, w=["dbgE"])
            DMA("sp", dbgE[:, 1], Es, r=K5, w=["dbgE"])
            dbgP = dram_out("dbgP", [128, 3, 16])
            DMA("sp", dbgP[:, 0], mag, r=K5, w=["dbgP"])
            DMA("sp", dbgP[:, 1], cs, r=K5, w=["dbgP"])
            DMA("sp", dbgP[:, 2], sn, r=K5, w=["dbgP"])
        S.barrier()
        lbr = aalloc([16]); lbi = aalloc([16])
        TT("dve", lbr, mag, cs, ALU.mult, r=K5, w=K5)
        TT("dve", lbi, mag, sn, ALU.mult, r=K5, w=K5)

        ini = aalloc([2, 16])
        MS("dve", ini, 0.0, K5)
        hlast = aalloc([2, 16])
        ub = [aalloc([4, 512], BF16) for _ in range(2)]
        uf = [aalloc([4, 512]) for _ in range(2)]
        wk = [[scr[:, (a * 8 + i) * 512:(a * 8 + i + 1) * 512] for i in range(8)] for a in range(2)]
        yf = aalloc([4, 512]); ygb = aalloc([4, 512], BF16); outb = aalloc([4, 512], BF16)
        g1 = aalloc([512]); g2 = aalloc([512])
        Hs = scr[:, 0:1024].rearrange("p (a j s t) -> p a j s t", a=2, j=16, s=NS)
        bus = scr[:, 1024:2048].rearrange("p (a j s t) -> p a j s t", a=2, j=16, s=NS)
        h0 = scr[:, 2048:2176].rearrange("p (a j s) -> p a j s", a=2, j=16)

        def epilogue(bi, n, usl, uk):
            c0 = blocks[bi][0]
            for c in range(4):
                y = yf[:, c, 0:n]
                TT("dve", g1[:, 0:n], y, y, ALU.mult, r=["yf"], w=["g1"])
                TS("dve", g1[:, 0:n], g1[:, 0:n], 0.044715, ALU.mult, r=["g1"], w=["g1"], s2=1.0, op1=ALU.add)
                TT("dve", g1[:, 0:n], g1[:, 0:n], y, ALU.mult, r=["g1", "yf"], w=["g1"])
                ACT(g2[:, 0:n], g1[:, 0:n], AF.Sigmoid, r=["g1"], w=["g2"], scale=1.5957691216057308)
                TT("dve", y, y, g2[:, 0:n], ALU.mult, r=["yf", "g2"], w=["yf"])
                ACT(ygb[:, c, 0:n], y, AF.Copy, r=["yf"], w=["ygb"])
            for co in range(4):
                pz = nextps()
                for ci in range(4):
                    MM(ps[pz][:, 0:n], Wg[:, ci, co * 128:(co + 1) * 128], ygb[:, ci, 0:n], ci == 0, ci == 3,
                       r=K5 + ["ygb"], w=[psk[pz]])
                ACT(g2[:, 0:n], ps[pz][:, 0:n], AF.Sigmoid, r=[psk[pz]] + K5, w=["g2"], bias=gbf[:, co:co + 1])
                TT("dve", outb[:, co, 0:n], yf[:, co, 0:n], g2[:, 0:n], ALU.mult, r=["yf", "g2"], w=["outb"])
            DMA("sp", mixed[0:512, c0:c0 + n].rearrange("(k p) t -> p k t", p=128), outb[:, :, 0:n],
                r=["outb"], w=[f"mixed{bi}"])

        for bi in range(NB):
            c0, n = blocks[bi]
            sl = bi % 2
            DMA("sp", uf[sl][:, :, 0:n], pr0[0:512, c0:c0 + n].rearrange("(k p) t -> p k t", p=128),
                r=[f"pr0{bi}"], w=[f"uf{sl}"])
            DMA("pool", ub[sl][:, :, 0:n], pr0[0:512, c0:c0 + n].rearrange("(k p) t -> p k t", p=128),
                r=[f"pr0{bi}"], w=[f"ub{sl}"])
            for c in range(4):
                py = c % 2
                for jj in range(4):
                    j = 4 * c + jj
                    ws = (j % 2)
                    d_re, d_im, g_re, g_im, h_re, h_im, ta, tb = wk[ws]
                    kk = f"wk{ws}"
                    pr_ = 2 + (2 * j) % 6
                    pi_ = 2 + (2 * j + 1) % 6
                    MM(ps[pr_][:, 0:n], Wbu[:, 0, c, jj * 128:(jj + 1) * 128], ub[sl][:, c, 0:n], True, True,
                       r=K5 + [f"ub{sl}"], w=[psk[pr_]])
                    MM(ps[pi_][:, 0:n], Wbu[:, 1, c, jj * 128:(jj + 1) * 128], ub[sl][:, c, 0:n], True, True,
                       r=K5 + [f"ub{sl}"], w=[psk[pi_]])
                    ec, es_ = Ec[:, j, 0:n], Es[:, j, 0:n]
                    TT("dve", ta[:, 0:n], ps[pr_][:, 0:n], ec, ALU.mult, r=[psk[pr_]] + K5, w=[kk])
                    TT("dve", tb[:, 0:n], ps[pi_][:, 0:n], es_, ALU.mult, r=[psk[pi_]] + K5, w=[kk])
                    TT("dve", d_re[:, 0:n], ta[:, 0:n], tb[:, 0:n], ALU.add, r=[kk], w=[kk])
                    TT("dve", ta[:, 0:n], ps[pi_][:, 0:n], ec, ALU.mult, r=[psk[pi_]] + K5, w=[kk])
                    TT("dve", tb[:, 0:n], ps[pr_][:, 0:n], es_, ALU.mult, r=[psk[pr_]] + K5, w=[kk])
                    TT("dve", d_im[:, 0:n], ta[:, 0:n], tb[:, 0:n], ALU.subtract, r=[kk], w=[kk])
                    S.op("dve", lambda e, g_re=g_re, d_re=d_re, j=j, n=n: e.tensor_tensor_scan(
                        out=g_re[:, 0:n], data0=mag[:, j:j + 1].to_broadcast([128, n]), data1=d_re[:, 0:n], initial=ini[:, 0, j:j + 1],
                        op0=ALU.mult, op1=ALU.add), r=[kk, "ini"] + K5, w=[kk])
                    S.op("dve", lambda e, g_im=g_im, d_im=d_im, j=j, n=n: e.tensor_tensor_scan(
                        out=g_im[:, 0:n], data0=mag[:, j:j + 1].to_broadcast([128, n]), data1=d_im[:, 0:n], initial=ini[:, 1, j:j + 1],
                        op0=ALU.mult, op1=ALU.add), r=[kk, "ini"] + K5, w=[kk])
                    TT("dve", ta[:, 0:n], g_re[:, 0:n], ec, ALU.mult, r=[kk] + K5, w=[kk])
                    TT("dve", tb[:, 0:n], g_im[:, 0:n], es_, ALU.mult, r=[kk] + K5, w=[kk])
                    TT("dve", h_re[:, 0:n], ta[:, 0:n], tb[:, 0:n], ALU.subtract, r=[kk], w=[kk])
                    TT("dve", ta[:, 0:n], g_re[:, 0:n], es_, ALU.mult, r=[kk] + K5, w=[kk])
                    TT("dve", tb[:, 0:n], g_im[:, 0:n], ec, ALU.mult, r=[kk] + K5, w=[kk])
                    TT("dve", h_im[:, 0:n], ta[:, 0:n], tb[:, 0:n], ALU.add, r=[kk], w=[kk])
                    if DBG5 and bi == 0 and j in (0, 5):
                        dbgH = dram_out(f"dbgH{j}", [128, 6, 512])
                        for ii, tt_ in enumerate((d_re, d_im, g_re, g_im, h_re, h_im)):
                            DMA("sp", dbgH[:, ii, :], tt_[:, 0:n], r=[kk], w=[f"dbgH{j}"])
                        dbgB = dram_out(f"dbgB{j}", [128, 2, 512])
                        o_, ok_ = evac(ps[pr_][:, 0:n], psk[pr_], n, 128)
                        DMA("sp", dbgB[:, 0, :], o_, r=[ok_], w=[f"dbgB{j}"])
                        o_, ok_ = evac(ps[pi_][:, 0:n], psk[pi_], n, 128)
                        DMA("sp", dbgB[:, 1, :], o_, r=[ok_], w=[f"dbgB{j}"])
                    hrl, hil = h_re[:, n - 1:n], h_im[:, n - 1:n]
                    TT("dve", ta[:, 0:1], hil, sn[:, j:j + 1], ALU.mult, r=[kk] + K5, w=[kk])
                    STT(ini[:, 0, j:j + 1], hrl, cs[:, j:j + 1], ta[:, 0:1], ALU.mult, ALU.subtract, r=[kk] + K5, w=["ini"])
                    TT("dve", ta[:, 0:1], hil, cs[:, j:j + 1], ALU.mult, r=[kk] + K5, w=[kk])
                    STT(ini[:, 1, j:j + 1], hrl, sn[:, j:j + 1], ta[:, 0:1], ALU.mult, ALU.add, r=[kk] + K5, w=["ini"])
                    if bi == NB - 1:
                        CP("dve", hlast[:, 0, j:j + 1], hrl, r=[kk], w=["hlast"])
                        CP("dve", hlast[:, 1, j:j + 1], hil, r=[kk], w=["hlast"])
                    MM(ps[py][:, 0:n], Wc[:, 0, j, :], h_re[:, 0:n], jj == 0, False, r=K5 + [kk], w=[psk[py]])
                    MM(ps[py][:, 0:n], Wc[:, 1, j, :], h_im[:, 0:n], False, jj == 3, r=K5 + [kk], w=[psk[py]])
                STT(yf[:, c, 0:n], uf[sl][:, c, 0:n], dfm[:, c:c + 1], ps[py][:, 0:n], ALU.mult, ALU.add,
                    r=[f"uf{sl}", psk[py]] + K5, w=["yf"])
            epilogue(bi, n, sl, None)
        DMA("sp", s5p_out[:, :, :], hlast, r=["hlast"], w=["s5p_out"])

        S.barrier()
        DMA("sp", h0, s5s_in[:, :, :, :], r=[], w=["h0"])
        bi = NB
        c0, n = blocks[bi]
        DMA("sp", uf[0][:, :, 0:n], pr0[0:512, c0:c0 + n].rearrange("(k p) t -> p k t", p=128),
            r=[f"pr0{bi}"], w=["uf0"])
        DMA("pool", ub[0][:, :, 0:n], pr0[0:512, c0:c0 + n].rearrange("(k p) t -> p k t", p=128),
            r=[f"pr0{bi}"], w=["ub0"])
        pr_, pi_ = nextps(), nextps()
        for j in range(16):
            c, jj = divmod(j, 4)
            MM(ps[pr_][:, j * n:(j + 1) * n], Wbu[:, 0, c, jj * 128:(jj + 1) * 128], ub[0][:, c, 0:n], True, True,
               r=K5 + ["ub0"], w=[psk[pr_]])
            MM(ps[pi_][:, j * n:(j + 1) * n], Wbu[:, 1, c, jj * 128:(jj + 1) * 128], ub[0][:, c, 0:n], True, True,
               r=K5 + ["ub0"], w=[psk[pi_]])
        CP("dve", bus[:, 0].rearrange("p j s t -> p (j s t)"), ps[pr_][:, 0:16 * n], r=[psk[pr_]], w=["bus"])
        CP("dve", bus[:, 1].rearrange("p j s t -> p (j s t)"), ps[pi_][:, 0:16 * n], r=[psk[pi_]], w=["bus"])
        sA = scr[:, 2176:2240].rearrange("p (j s) -> p j s", j=16)
        sB = scr[:, 2240:2304].rearrange("p (j s) -> p j s", j=16)
        lbrB = lbr.unsqueeze(2).to_broadcast([128, 16, NS])
        lbiB = lbi.unsqueeze(2).to_broadcast([128, 16, NS])
        KH = ["Hs"]
        for t in range(LS):
            pr_re = h0[:, 0] if t == 0 else Hs[:, 0, :, :, t - 1]
            pr_im = h0[:, 1] if t == 0 else Hs[:, 1, :, :, t - 1]
            rr = KH + ["h0", "bus"] + K5
            TT("dve", sA, pr_re, lbrB, ALU.mult, r=rr, w=["sA"])
            TT("dve", sB, pr_im, lbiB, ALU.mult, r=rr, w=["sB"])
            TT("dve", sA, sA, sB, ALU.subtract, r=["sA", "sB"], w=["sA"])
            TT("dve", Hs[:, 0, :, :, t], sA, bus[:, 0, :, :, t], ALU.add, r=["sA", "bus"], w=KH)
            TT("dve", sA, pr_re, lbiB, ALU.mult, r=rr, w=["sA"])
            TT("dve", sB, pr_im, lbrB, ALU.mult, r=rr, w=["sB"])
            TT("dve", sA, sA, sB, ALU.add, r=["sA", "sB"], w=["sA"])
            TT("dve", Hs[:, 1, :, :, t], sA, bus[:, 1, :, :, t], ALU.add, r=["sA", "bus"], w=KH)
        hs_fin = scr[:, 2304:2432].rearrange("p (a j s) -> p a j s", a=2, j=16)
        CP("dve", hs_fin, Hs[:, :, :, :, LS - 1], r=KH, w=["hs_fin"])
        DMA("sp", s5s_out[:, :, :, :], hs_fin, r=["hs_fin"], w=["s5s_out"])
        for c in range(4):
            py = nextps()
            for jj in range(4):
                j = 4 * c + jj
                MM(ps[py][:, 0:n], Wc[:, 0, j, :], Hs[:, 0, j].rearrange("p s t -> p (s t)"), jj == 0, False,
                   r=K5 + KH, w=[psk[py]])
                MM(ps[py][:, 0:n], Wc[:, 1, j, :], Hs[:, 1, j].rearrange("p s t -> p (s t)"), False, jj == 3,
                   r=K5 + KH, w=[psk[py]])
            STT(yf[:, c, 0:n], uf[0][:, c, 0:n], dfm[:, c:c + 1], ps[py][:, 0:n], ALU.mult, ALU.add,
                r=["uf0", psk[py]] + K5, w=["yf"])
        epilogue(bi, n, 0, None)


    def hgrn_phase():
        areset()
        KP = ["hgp"]
        lbl = aalloc([4, 3]); ssum = aalloc([4]); lb = aalloc([4]); oml = aalloc([4]); nw = aalloc([1])
        DMA("sp", lbl, lbl_fm[:, :, :], r=[], w=KP)
        DMA("sp", nw, hnw[:, :], r=[], w=KP)
        ACT(lbl, lbl, AF.Exp, r=KP, w=KP)
        S.op("dve", lambda e: e.reduce_sum(out=ssum, in_=lbl, axis=AX.X), r=KP, w=KP)
        S.op("dve", lambda e: e.reciprocal(out=ssum, in_=ssum), r=KP, w=KP)
        TT("dve", lb, lbl[:, :, 0], ssum, ALU.mult, r=KP, w=KP)
        TS("dve", oml, lb, -1.0, ALU.mult, r=KP, w=KP, s2=1.0, op1=ALU.add)
        maskLE = aalloc([64])
        S.op("pool", lambda e: e.affine_select(out=maskLE[0:64, :], in_=ones_f[0:64, 0:64], pattern=[[1, 64]],
                                               compare_op=ALU.is_ge, fill=0.0, base=0, channel_multiplier=-1),
             r=["ones_f"], w=KP)
        m01 = {}
        for C_, n_ in ((64, 512), (8, 8)):
            m = aalloc([n_])
            MS("dve", m, 1.0, KP)
            MS("dve", m.rearrange("p (a c) -> p a c", c=C_)[:, :, 0:1], 0.0, KP)
            m01[C_] = m
        NU = 4
        U = []
        for u in range(NU):
            d = {}
            for nm in ("qr", "fr", "gr", "f", "b", "qin", "kin", "kdec", "tmp"):
                d[nm] = aalloc([512])
            d["v"] = aalloc([8, 128])
            d["kdT"] = aalloc([8, 128])
            d["a"] = aalloc([8])
            d["S"] = [aalloc([128]), aalloc([128])]
            d["scm"] = [aalloc([64]), aalloc([64])]
            d["osq"] = aalloc([512], BF16)
            d["ob"] = aalloc([512], BF16)
            U.append(d)
        rot = [4]

        def rps():
            i = rot[0]
            rot[0] = 4 + (i - 4 + 1) % 4
            return i

        def hg_block(units, n, C, si):
            nch = n // C
            for (u, head, col0) in units:
                d = U[u]; k = f"hu{u}"
                DMA("sp", d["qr"][:, 0:n], pr0[512 + head * 128:512 + (head + 1) * 128, col0:col0 + n], r=["pr0all"], w=[k])
                DMA("sp", d["fr"][:, 0:n], pr0[1024 + head * 128:1024 + (head + 1) * 128, col0:col0 + n], r=["pr0all"], w=[k])
                DMA("sp", d["gr"][:, 0:n], pr0[2048 + head * 128:2048 + (head + 1) * 128, col0:col0 + n], r=["pr0all"], w=[k])
                vv = d["v"].rearrange("p a d -> p (a d)")[0:C, 0:nch * 128].rearrange("p (a d) -> p a d", d=128)
                DMA("sp", vv, vtok0[col0:col0 + n, head * 128:(head + 1) * 128].rearrange("(a c) d -> c a d", c=C),
                    r=["vtok0all"], w=[k])
            for (u, head, col0) in units:
                d = U[u]; k = f"hu{u}"
                ACT(d["f"][:, 0:n], d["fr"][:, 0:n], AF.Sigmoid, r=[k], w=[k])
                ACT(d["gr"][:, 0:n], d["gr"][:, 0:n], AF.Silu, r=[k], w=[k])
                ACT(d["qr"][:, 0:n], d["qr"][:, 0:n], AF.Silu, r=[k], w=[k])
            for (u, head, col0) in units:
                d = U[u]; k = f"hu{u}"
                TS("dve", d["f"][:, 0:n], d["f"][:, 0:n], oml[:, head:head + 1], ALU.mult, r=[k] + KP, w=[k],
                   s2=lb[:, head:head + 1], op1=ALU.add)
                ACT(d["tmp"][:, 0:n], d["f"][:, 0:n], AF.Ln, r=[k], w=[k])
                S.op("dve", lambda e, d=d: e.tensor_tensor_scan(
                    out=d["b"][:, 0:n], data0=m01[C][:, 0:n], data1=d["tmp"][:, 0:n], initial=0.0,
                    op0=ALU.mult, op1=ALU.add), r=[k] + KP, w=[k])
                TS("dve", d["f"][:, 0:n], d["f"][:, 0:n], -1.0, ALU.mult, r=[k], w=[k], s2=1.0, op1=ALU.add)
                ACT(d["tmp"][:, 0:n], d["b"][:, 0:n], AF.Exp, r=[k], w=[k])
                TT("dve", d["qin"][:, 0:n], d["qr"][:, 0:n], d["tmp"][:, 0:n], ALU.mult, r=[k], w=[k])
                ACT(d["tmp"][:, 0:n], d["b"][:, 0:n], AF.Exp, r=[k], w=[k], scale=-1.0)
                TT("dve", d["kin"][:, 0:n], d["f"][:, 0:n], d["tmp"][:, 0:n], ALU.mult, r=[k], w=[k])
                b3 = d["b"][:, 0:n].rearrange("p (a c) -> p a c", c=C)
                t3 = d["tmp"][:, 0:n].rearrange("p (a c) -> p a c", c=C)
                TT("dve", t3, b3[:, :, C - 1:C].to_broadcast([128, nch, C]), b3, ALU.subtract, r=[k], w=[k])
                ACT(d["tmp"][:, 0:n], d["tmp"][:, 0:n], AF.Exp, r=[k], w=[k])
                TT("dve", d["kdec"][:, 0:n], d["f"][:, 0:n], d["tmp"][:, 0:n], ALU.mult, r=[k], w=[k])
                ACT(d["a"][:, 0:nch], b3[:, :, C - 1], AF.Exp, r=[k], w=[k])
                for g0 in range(0, nch, 4):
                    g1_ = min(nch, g0 + 4)
                    pt = rps()
                    for ch in range(g0, g1_):
                        TR(ps[pt][0:C, (ch - g0) * 128:(ch - g0 + 1) * 128], d["kdec"][:, ch * C:(ch + 1) * C], ident[:],
                           r=[k, "ident"], w=[psk[pt]])
                    CP("dve", d["kdT"].rearrange("p a d -> p (a d)")[0:C, g0 * 128:g1_ * 128],
                       ps[pt][0:C, 0:(g1_ - g0) * 128], r=[psk[pt]], w=[k])
            for ch in range(nch):
                for (u, head, col0) in units:
                    d = U[u]; k = f"hu{u}"; sk = f"hS{u}"
                    cols = slice(ch * C, (ch + 1) * C)
                    Sp = d["S"][si[u]]; Sn = d["S"][1 - si[u]]
                    vch = d["v"][0:C, ch, :] if False else d["v"].rearrange("p a d -> p (a d)")[0:C, ch * 128:(ch + 1) * 128]
                    kdch = d["kdT"].rearrange("p a d -> p (a d)")[0:C, ch * 128:(ch + 1) * 128]
                    pS = rps()
                    MM(ps[pS][0:C, 0:C], d["kin"][:, cols], d["qin"][:, cols], True, True, r=[k], w=[psk[pS]])
                    scm = d["scm"][ch % 2]
                    TT("dve", scm[0:C, 0:C], ps[pS][0:C, 0:C], maskLE[0:C, 0:C], ALU.mult, r=[psk[pS]] + KP, w=[k + f"scm{ch % 2}"])
                    MM(ps[u][:, cols], Sp, d["qin"][:, cols], True, False, r=[k, sk], w=[psk[u]])
                    MM(ps[u][:, cols], vch, scm[0:C, 0:C], False, True, r=[k, k + f"scm{ch % 2}"], w=[psk[u]])
                    pU = rps()
                    MM(ps[pU][:, 0:128], kdch, vch, True, True, r=[k], w=[psk[pU]])
                    STT(Sn, Sp, d["a"][:, ch:ch + 1], ps[pU][:, 0:128], ALU.mult, ALU.add, r=[k, sk, psk[pU]], w=[sk])
                    si[u] = 1 - si[u]
            for (u, head, col0) in units:
                d = U[u]; k = f"hu{u}"
                ACT(d["osq"][:, 0:n], ps[u][:, 0:n], AF.Square, r=[psk[u]], w=[k])
                pr_ = rps()
                MM(ps[pr_][:, 0:n], ones_bf[:], d["osq"][:, 0:n], True, True, r=[k, "ones_bf"], w=[psk[pr_]])
                ACT(d["tmp"][:, 0:n], ps[pr_][:, 0:n], AF.Sqrt, r=[psk[pr_], "epsb"], w=[k], scale=1.0 / 128, bias=epsb[:, 0:1])
                S.op("dve", lambda e, d=d: e.reciprocal(out=d["tmp"][:, 0:n], in_=d["tmp"][:, 0:n]), r=[k], w=[k])
                TT("dve", d["tmp"][:, 0:n], ps[u][:, 0:n], d["tmp"][:, 0:n], ALU.mult, r=[k, psk[u]], w=[k])
                STT(d["ob"][:, 0:n], d["tmp"][:, 0:n], nw[:, 0:1], d["gr"][:, 0:n], ALU.mult, ALU.mult, r=[k] + KP, w=[k])
                DMA("sp", mixed[512 + head * 128:512 + (head + 1) * 128, col0:col0 + n], d["ob"][:, 0:n],
                    r=[k], w=["mixedall"])

        si = [0] * NU
        for u in range(NU):
            MS("dve", U[u]["S"][0], 0.0, [f"hS{u}"])
        for bi in range(NB):
            hg_block([(u, u, bi * 512) for u in range(NU)], 512, 64, si)
        for u in range(NU):
            DMA("sp", hgp_out[u], U[u]["S"][si[u]], r=[f"hS{u}"], w=["hgp_out"])
        for sq_ in range(NS):
            for u in range(NU):
                DMA("sp", U[u]["S"][si[u]], hgs_in[sq_ * 4 + u], r=[], w=[f"hS{u}"])
            hg_block([(u, u, T + sq_ * LS) for u in range(NU)], LS, LS, si)
            for u in range(NU):
                DMA("sp", hgs_out[sq_ * 4 + u], U[u]["S"][si[u]], r=[f"hS{u}"], w=["hgs_out"])


    NP = PAST // 128
    NTt = T // 128
    NROWS = None
    cd_w_in = dram_in("cd_w_in", [D, 3080])
    cd_w_out = dram_in("cd_w_out", [D, D])
    bfb_in = dram_in("bfb_in", [128, 8])
    bsb_in = dram_in("bsb_in", [128, 8])
    pt_rep = dram_in("pt_rep", [128, NS * NP], I32)
    qk1 = dram_out("qk1", [2048, NTOK])
    fv_out = dram_out("fv_out", [NTOK, 512])
    sv_out = dram_out("sv_out", [NTOK, 512])
    logf_out = dram_out("logf_out", [NTOK, 8])
    qk1s = dram_tmp("qk1s", [2048, NTOK], F32)
    fv_s = dram_tmp("fv_s", [NTOK, 512], F32)
    sv_s = dram_tmp("sv_s", [NTOK, 512], F32)
    logf_s = dram_tmp("logf_s", [NTOK, 8], F32)

    def cache_in(name, w):
        return nc.dram_tensor(name, [NPHYS * 128, w], F32, kind="ExternalInput").ap()
    c_fk = cache_in("c_fk", 512); c_fv = cache_in("c_fv", 512); c_lf = cache_in("c_lf", 8)
    c_sk = cache_in("c_sk", 512); c_sv = cache_in("c_sv", 512)

    oneb = sb("oneb", [128, 1], F32)
    MS("pool", oneb[:], 1.0, ["oneb"])
    bfb = sb("bfb", [128, 8], F32)
    bsb = sb("bsb", [128, 8], F32)
    DMA("sp", bfb[:], bfb_in[:, :], r=[], w=["bfb"])
    DMA("sp", bsb[:], bsb_in[:, :], r=[], w=["bsb"])
    triS_f = sb("triS_f", [128, 128], F32)
    triI_f = sb("triI_f", [128, 128], F32)
    triLE_f = sb("triLE_f", [128, 128], F32)
    triI_b = sb("triI_b", [128, 128], BF16)
    for (tile_, base_, cm_, st_) in ((triS_f, -1, 1, -1), (triI_f, 0, 1, -1), (triLE_f, 0, -1, 1)):
        S.op("pool", lambda e, tile_=tile_, base_=base_, cm_=cm_, st_=st_: e.affine_select(
            out=tile_[:], in_=ones_f[:], pattern=[[st_, 128]], compare_op=ALU.is_ge, fill=0.0,
            base=base_, channel_multiplier=cm_), r=["ones_f"], w=[tile_.name])
    CP("dve", triI_b[:], triI_f[:], r=["triI_f"], w=["triI_b"])
    mLTn = sb("mLTn", [LS, LS], F32)
    S.op("pool", lambda e: e.affine_select(out=mLTn[:], in_=ones_f[0:LS, 0:LS], pattern=[[1, LS]], compare_op=ALU.is_ge,
                                           fill=0.0, base=-1, channel_multiplier=-1), r=["ones_f"], w=["asp"])

    def proj1():
        def fvh(dst, dst2):
            def h_(pi, nt, tok0, bi):
                o, ok = evac(ps[pi][0:nt, 0:512], psk[pi], 512, nt)
                DMA("sp", dst[tok0:tok0 + nt, :], o, r=[ok], w=[f"{dst.name}{bi}"])
                DMA("sp", dst2[tok0:tok0 + nt, :], o, r=[ok], w=[f"{dst2.name}{bi}"])
            return h_

        def lgh(pi, nt, tok0, bi):
            i = stgrr[0]
            stgrr[0] = (i + 1) % 4
            o = stg[i][0:nt, 0:8]
            k = f"stg{i}"
            TT("dve", o, ps[pi][0:nt, 0:8], bfb[0:nt, :], ALU.add, r=[psk[pi], "bfb"], w=[k])
            ACT(o, o, AF.Exp, r=[k], w=[k], scale=-1.0)
            ACT(o, o, AF.Ln, r=[k, "oneb"], w=[k], bias=oneb[0:nt, 0:1])
            TS("dve", o, o, -1.0, ALU.mult, r=[k], w=[k])
            DMA("sp", logf_out[tok0:tok0 + nt, :], o, r=[k], w=[f"logf_out{bi}"])
            DMA("sp", logf_s[tok0:tok0 + nt, :], o, r=[k], w=[f"logf_s{bi}"])
        pass_proj(cd_w_in, 3080,
                  [(0, 512, [qk1, qk1s], 0, 0.125), (512, 512, [qk1, qk1s], 512, None),
                   (1544, 512, [qk1, qk1s], 1024, 0.125), (2056, 512, [qk1, qk1s], 1536, None)],
                  [(1024, 512, fvh(fv_out, fv_s)), (2568, 512, fvh(sv_out, sv_s)), (1536, 8, lgh)])

    def attn_phase():
        areset()
        KA = ["attp"]
        mLE = aalloc([4, 512], BF16)
        mLT = aalloc([4, 512], BF16)
        onesb512 = aalloc([512], BF16)
        MS("pool", onesb512, 1.0, KA)
        for m in range(4):
            S.op("pool", lambda e, m=m: e.affine_select(out=mLE[:, m, :], in_=onesb512, pattern=[[1, 512]],
                 compare_op=ALU.is_ge, fill=0.0, base=-128 * m, channel_multiplier=-1), r=KA, w=KA)
            S.op("pool", lambda e, m=m: e.affine_select(out=mLT[:, m, :], in_=onesb512, pattern=[[1, 512]],
                 compare_op=ALU.is_ge, fill=0.0, base=-128 * m - 1, channel_multiplier=-1), r=KA, w=KA)
        if ASTOP <= 1:
            return
        lfall = aalloc([NTt, 8]); Fneg = aalloc([NTt, 8]); carF = aalloc([8])
        for a0_ in range(0, NTt, 16):
            a1_ = min(NTt, a0_ + 16)
            DMA("sp", lfall[:, a0_:a1_, :], logf_s[a0_ * 128:a1_ * 128, :].rearrange("(a p) h -> p a h", p=128), r=[], w=KA)
        MS("dve", carF, 0.0, KA)
        for t in range(NTt - 1, -1, -1):
            p1, p2 = 2 + (2 * t) % 6, 2 + (2 * t + 1) % 6
            MM(ps[p1][:, 0:8], triS_f[:], lfall[:, t, :], True, True, r=KA + ["triS_f"], w=[psk[p1]])
            MM(ps[p2][:, 0:8], ones_f[:], lfall[:, t, :], True, True, r=KA + ["ones_f"], w=[psk[p2]])
            TT("dve", Fneg[:, t, :], ps[p1][:, 0:8], carF, ALU.add, r=[psk[p1]] + KA, w=KA)
            TT("dve", carF, carF, ps[p2][:, 0:8], ALU.add, r=[psk[p2]] + KA, w=KA)
        if ASTOP <= 2:
            return
        kT = aalloc([T], BF16); qT = aalloc([T], BF16)
        Vf = aalloc([NTt, 2, 65], BF16)
        MS("dve", Vf[:, :, :, 64:65], 1.0, ["Vf"])
        e_t = [aalloc([512]) for _ in range(2)]
        zc_t = [aalloc([512]) for _ in range(2)]
        spb_t = [aalloc([512], BF16) for _ in range(2)]
        w_t = [aalloc([512], BF16) for _ in range(2)]
        carry = aalloc([512]); dsb = aalloc([512]); rden = aalloc([512])
        ob_ = [aalloc([512], BF16) for _ in range(2)]
        rot = [2]

        def rps():
            i = rot[0]
            rot[0] = 2 + (i - 2 + 1) % 6
            return i
        cnt = [0]
        for kind in range(2):
            if ASTOP == 4 and kind == 1:
                break
            if ASTOP == 5 and kind == 0:
                continue
            qrow, krow, vsrc = (0, 512, fv_s) if kind == 0 else (1024, 1536, sv_s)
            for hp in range(4):
                for t0_ in range(0, T, 2048):
                    t1_ = min(T, t0_ + 2048)
                    DMA("pool", kT[:, t0_:t1_], qk1s[krow + hp * 128:krow + (hp + 1) * 128, t0_:t1_], r=[], w=["kT"])
                    DMA("pool", qT[:, t0_:t1_], qk1s[qrow + hp * 128:qrow + (hp + 1) * 128, t0_:t1_], r=[], w=["qT"])
                for hh_ in range(2):
                    for a0_ in range(0, NTt, 16):
                        a1_ = min(NTt, a0_ + 16)
                        DMA("pool", Vf[:, a0_:a1_, hh_, 0:64],
                            vsrc[a0_ * 128:a1_ * 128, hp * 128 + hh_ * 64:hp * 128 + (hh_ + 1) * 64].rearrange("(a p) d -> p a d", p=128),
                            r=[], w=["Vf"])
                if ASTOP == 3:
                    continue
                for hh in range(2):
                    h = 2 * hp + hh
                    pr0_ = slice(hh * 64, (hh + 1) * 64)
                    for i in range(NB):
                        po = cnt[0] % 2
                        cnt[0] += 1
                        jl = list(range(4 * i + 3, -1, -1))
                        if kind == 1:
                            MS("dve", carry, 0.0, ["carry"])
                        for idx, j in enumerate(jl):
                            a = idx % 2
                            first, last = idx == 0, idx == len(jl) - 1
                            pz = rps()
                            MM(ps[pz][:, 0:512], kT[pr0_, j * 128:(j + 1) * 128], qT[pr0_, i * 512:(i + 1) * 512],
                               True, True, r=["kT", "qT"], w=[psk[pz]])
                            m = j - 4 * i
                            if kind == 0:
                                P = w_t[a]
                                ACT(P, ps[pz][:, 0:512], AF.Exp, r=[psk[pz]] + KA, w=[f"w{a}"], bias=Fneg[:, j, h:h + 1])
                                if m >= 0:
                                    TT("pool", P, P, mLE[:, m, :], ALU.mult, r=[f"w{a}"] + KA, w=[f"w{a}"])
                                MM(ps[po][0:65, 0:512], Vf[:, j, hh, :], P, first, last, r=["Vf", f"w{a}"], w=[psk[po]])
                            else:
                                e_, zc_, spb, w_ = e_t[a], zc_t[a], spb_t[a], w_t[a]
                                ACT(e_, ps[pz][:, 0:512], AF.Exp, r=[psk[pz], "bsb"], w=[f"e{a}"], bias=bsb[:, h:h + 1])
                                ACT(spb, e_, AF.Ln, r=[f"e{a}", "oneb"], w=[f"spb{a}"], bias=oneb[:, 0:1])
                                if m >= 0:
                                    TT("pool", spb, spb, mLT[:, m, :], ALU.mult, r=[f"spb{a}"] + KA, w=[f"spb{a}"])
                                if SBCUT >= 2:
                                    pc_, pt_ = rps(), rps()
                                    MM(ps[pc_][:, 0:512], triI_b[:], spb, True, True, r=["triI_b", f"spb{a}"], w=[psk[pc_]])
                                    MM(ps[pt_][:, 0:512], ones_bf[:], spb, True, True, r=["ones_bf", f"spb{a}"], w=[psk[pt_]])
                                if SBCUT >= 2.5:
                                    TT("dve", zc_, ps[pz][:, 0:512], carry, ALU.subtract, r=[psk[pz], "carry", f"e{a}"], w=[f"zc{a}"])
                                if SBCUT >= 3:
                                    TT("dve", zc_, zc_, ps[pc_][:, 0:512], ALU.subtract, r=[f"zc{a}", psk[pc_]], w=[f"zc{a}"])
                                if SBCUT >= 4:
                                    ACT(w_, zc_, AF.Exp, r=[f"zc{a}", "bsb"], w=[f"w{a}"], bias=bsb[:, h:h + 1])
                                    if m >= 0:
                                        TT("pool", w_, w_, mLT[:, m, :], ALU.mult, r=[f"w{a}"] + KA, w=[f"w{a}"])
                                if SBCUT >= 5:
                                    TT("dve", carry, carry, ps[pt_][:, 0:512], ALU.add, r=["carry", psk[pt_]], w=["carry"])
                                if SBCUT >= 6:
                                    MM(ps[po][0:64, 0:512], Vf[:, j, hh, 0:64], w_, first, last, r=["Vf", f"w{a}"], w=[psk[po]])
                        if SBCUT < 6 and kind == 1:
                            continue
                        oo = ob_[po]
                        if kind == 0:
                            ACT(dsb[64:65, :], ps[po][64:65, 0:512], AF.Copy, r=[psk[po]], w=["dsb"])
                            pd = rps()
                            MM(ps[pd][0:64, 0:512], ones_f[64:65, 0:64], dsb[64:65, :], True, True,
                               r=["dsb", "ones_f"], w=[psk[pd]])
                            S.op("dve", lambda e, pd=pd: e.reciprocal(out=rden[0:64, :], in_=ps[pd][0:64, 0:512]),
                                 r=[psk[pd]], w=["rden"])
                            TT("dve", oo[0:64, :], ps[po][0:64, 0:512], rden[0:64, :], ALU.mult, r=[psk[po], "rden"], w=[f"ob_{po}"])
                        else:
                            ACT(oo[0:64, :], ps[po][0:64, 0:512], AF.Copy, r=[psk[po]], w=[f"ob_{po}"])
                        DMA("sp", mixed[kind * 512 + h * 64:kind * 512 + (h + 1) * 64, i * 512:(i + 1) * 512], oo[0:64, :],
                            r=[f"ob_{po}"], w=["mixedall"])


    def attn_sample_phase():
        areset()
        KS = ["asp"]
        ptf = aalloc([NS * NP]); idx = aalloc([NS * NP], I32); pti = aalloc([NS * NP], I32)
        iop_i = aalloc([1], I32); iop = aalloc([1])
        DMA("sp", pti, pt_rep[:, :], r=[], w=KS)
        S.op("pool", lambda e: e.iota(out=iop_i, pattern=[[0, 1]], base=0, channel_multiplier=1), w=KS)
        CP("dve", iop, iop_i, r=KS, w=KS)
        CP("dve", ptf, pti, r=KS, w=KS)
        TS("dve", ptf, ptf, 128.0, ALU.mult, r=KS, w=KS, s2=iop[:, 0:1], op1=ALU.add)
        CP("dve", idx, ptf, r=KS, w=KS)
        bm = aalloc([8])
        S.op("pool", lambda e: e.affine_select(out=bm[0:64, :], in_=ones_f[0:64, 0:8], pattern=[[-8, 8]],
             compare_op=ALU.is_ge, fill=0.0, base=0, channel_multiplier=1), r=["ones_f"], w=KS)
        S.op("pool", lambda e: e.affine_select(out=bm[0:64, :], in_=bm[0:64, :], pattern=[[8, 8]],
             compare_op=ALU.is_ge, fill=0.0, base=7, channel_multiplier=-1), r=KS, w=KS)
        Qb = [aalloc([4, 64]) for _ in range(2)]
        qs = aalloc([4, LS]); kTn = [aalloc([4, LS]) for _ in range(2)]
        Vn = [aalloc([512]) for _ in range(2)]
        lfn = aalloc([8]); biasn = aalloc([8])
        kpg = [aalloc([512]) for _ in range(2)]
        vpg = [aalloc([512]) for _ in range(2)]
        kTp = [aalloc([4, 128]) for _ in range(2)]
        lfp = [aalloc([8]) for _ in range(2)]
        Fb = aalloc([8]); carF = aalloc([8])
        zb = [aalloc([64]) for _ in range(2)]
        e_ = [aalloc([64]) for _ in range(2)]
        sp_ = [aalloc([64]) for _ in range(2)]
        Pw = [aalloc([64]) for _ in range(2)]
        carry = aalloc([64])
        tmpo = aalloc([8, 64]); red = aalloc([64]); rd = aalloc([2]); o16 = aalloc([64], BF16)
        rot = [3]

        def rps():
            i = rot[0]
            rot[0] = 3 + (i - 3 + 1) % 5
            return i
        for sq_ in range(NS):
            col = T + sq_ * LS
            for kind in range(2):
                qrow, krow, vsrc, ck, cv = (0, 512, fv_s, c_fk, c_fv) if kind == 0 else (1024, 1536, sv_s, c_sk, c_sv)
                po, pd = 0, 1
                Q = Qb[kind]
                MS("dve", Q, 0.0, [f"Q{kind}"])
                DMA("sp", qs, qk1s[qrow:qrow + 512, col:col + LS].rearrange("(k p) t -> p k t", p=128), r=[], w=["qs"])
                for pr in range(4):
                    for hh in range(2):
                        hd = 2 * pr + hh
                        CP("dve", Q[hh * 64:(hh + 1) * 64, pr, hd * LS:(hd + 1) * LS], qs[hh * 64:(hh + 1) * 64, pr, :],
                           r=["qs"], w=[f"Q{kind}"])
                DMA("sp", kTn[kind], qk1s[krow:krow + 512, col:col + LS].rearrange("(k p) t -> p k t", p=128), r=[], w=[f"kTn{kind}"])
                DMA("sp", Vn[kind][0:LS, :], vsrc[col:col + LS, :], r=[], w=[f"Vn{kind}"])
                if kind == 0:
                    DMA("sp", lfn[0:LS, :], logf_s[col:col + LS, :], r=[], w=["lfn"])
                    MS("dve", carF, 0.0, ["carF"])
                else:
                    MS("dve", carry, 0.0, ["carry"])
                ntiles = NP + 1
                for ti in range(ntiles):
                    a = ti % 2
                    first, last = ti == 0, ti == ntiles - 1
                    new = ti == 0
                    ns = LS if new else 128
                    pz = rps()
                    if new:
                        for pr in range(4):
                            MM(ps[pz][0:ns, 0:64], kTn[kind][:, pr, :], Q[:, pr, :], pr == 0, pr == 3,
                               r=[f"kTn{kind}", f"Q{kind}"], w=[psk[pz]])
                        V = Vn[kind]
                        vkey = f"Vn{kind}"
                    else:
                        pg = NP - ti
                        ic = sq_ * NP + pg
                        S.op("pool", lambda e, a=a, ic=ic, ck=ck: e.indirect_dma_start(
                            out=kpg[a], out_offset=None, in_=ck,
                            in_offset=bass.IndirectOffsetOnAxis(ap=idx[:, ic:ic + 1], axis=0)),
                            r=KS, w=[f"kpg{a}"], dma=True)
                        S.op("pool", lambda e, a=a, ic=ic, cv=cv: e.indirect_dma_start(
                            out=vpg[a], out_offset=None, in_=cv,
                            in_offset=bass.IndirectOffsetOnAxis(ap=idx[:, ic:ic + 1], axis=0)),
                            r=KS, w=[f"vpg{a}"], dma=True)
                        ptr = rps()
                        for pr in range(4):
                            TR(ps[ptr][:, pr * 128:(pr + 1) * 128], kpg[a][:, pr * 128:(pr + 1) * 128], ident[:],
                               r=[f"kpg{a}", "ident"], w=[psk[ptr]])
                        ACT(kTp[a].rearrange("p a b -> p (a b)"), ps[ptr][:, 0:512], AF.Copy, r=[psk[ptr]], w=[f"kTp{a}"])
                        for pr in range(4):
                            MM(ps[pz][:, 0:64], kTp[a][:, pr, :], Q[:, pr, :], pr == 0, pr == 3,
                               r=[f"kTp{a}", f"Q{kind}"], w=[psk[pz]])
                        V = vpg[a]
                        vkey = f"vpg{a}"
                    z3 = ps[pz][0:ns, 0:64].rearrange("p (h t) -> p h t", h=8)
                    if kind == 0:
                        if new:
                            pb = rps()
                            MM(ps[pb][0:LS, 0:8], triLE_f[0:LS, 0:LS], lfn[0:LS, :], True, True, r=["lfn", "triLE_f"], w=[psk[pb]])
                            TS("dve", Fb[0:LS, :], ps[pb][0:LS, 0:8], -1.0, ALU.mult, r=[psk[pb]], w=["Fb"])
                        else:
                            S.op("pool", lambda e, a=a, ic=ic: e.indirect_dma_start(
                                out=lfp[a], out_offset=None, in_=c_lf,
                                in_offset=bass.IndirectOffsetOnAxis(ap=idx[:, ic:ic + 1], axis=0)),
                                r=KS, w=[f"lfp{a}"], dma=True)
                            pb, pb2 = rps(), rps()
                            MM(ps[pb][:, 0:8], triS_f[:], lfp[a], True, True, r=[f"lfp{a}", "triS_f"], w=[psk[pb]])
                            MM(ps[pb2][:, 0:8], ones_f[:], lfp[a], True, True, r=[f"lfp{a}", "ones_f"], w=[psk[pb2]])
                            TT("dve", Fb, ps[pb][:, 0:8], carF, ALU.add, r=[psk[pb], "carF"], w=["Fb"])
                            TT("dve", carF, carF, ps[pb2][:, 0:8], ALU.add, r=[psk[pb2], "carF"], w=["carF"])
                        zz = zb[a][0:ns, :].rearrange("p (h t) -> p h t", h=8)
                        TT("dve", zz, z3, Fb[0:ns, :].unsqueeze(2).to_broadcast([ns, 8, LS]), ALU.add,
                           r=[psk[pz], "Fb"], w=[f"zb{a}"])
                        P = Pw[a]
                        ACT(P[0:ns, :], zb[a][0:ns, :], AF.Exp, r=[f"zb{a}"], w=[f"Pw{a}"])
                        if new:
                            P3 = P[0:ns, :].rearrange("p (h t) -> p h t", h=8)
                            TT("dve", P3, P3, triLE_f[0:LS, 0:LS].unsqueeze(1).to_broadcast([LS, 8, LS]), ALU.mult,
                               r=[f"Pw{a}", "triLE_f"], w=[f"Pw{a}"])
                        MM(ps[po][0:64, 0:512], P[0:ns, :], V[0:ns, :], first, last, r=[f"Pw{a}", vkey], w=[psk[po]])
                        MM(ps[pd][0:64, 0:2], P[0:ns, :], ones_f[0:ns, 0:2], first, last, r=[f"Pw{a}", "ones_f"], w=[psk[pd]])
                    else:
                        zz = zb[a][0:ns, :].rearrange("p (h t) -> p h t", h=8)
                        TT("dve", zz, z3, bsb[0:ns, :].unsqueeze(2).to_broadcast([ns, 8, LS]), ALU.add,
                           r=[psk[pz], "bsb"], w=[f"zb{a}"])
                        ACT(e_[a][0:ns, :], zb[a][0:ns, :], AF.Exp, r=[f"zb{a}"], w=[f"e{a}"])
                        ACT(sp_[a][0:ns, :], e_[a][0:ns, :], AF.Ln, r=[f"e{a}", "oneb"], w=[f"sp{a}"], bias=oneb[0:ns, 0:1])
                        if new:
                            s3 = sp_[a][0:ns, :].rearrange("p (h t) -> p h t", h=8)
                            TT("dve", s3, s3, mLTn[0:LS, :].unsqueeze(1).to_broadcast([LS, 8, LS]), ALU.mult,
                               r=[f"sp{a}"] + KS, w=[f"sp{a}"])
                        pc_, pt_ = rps(), rps()
                        MM(ps[pc_][0:ns, 0:64], triI_f[0:ns, 0:ns], sp_[a][0:ns, :], True, True, r=["triI_f", f"sp{a}"], w=[psk[pc_]])
                        MM(ps[pt_][:, 0:64], ones_f[0:ns, :], sp_[a][0:ns, :], True, True, r=["ones_f", f"sp{a}"], w=[psk[pt_]])
                        TT("dve", zb[a][0:ns, :], zb[a][0:ns, :], carry[0:ns, :], ALU.subtract, r=[f"zb{a}", "carry"], w=[f"zb{a}"])
                        TT("dve", zb[a][0:ns, :], zb[a][0:ns, :], ps[pc_][0:ns, 0:64], ALU.subtract, r=[f"zb{a}", psk[pc_]], w=[f"zb{a}"])
                        P = Pw[a]
                        ACT(P[0:ns, :], zb[a][0:ns, :], AF.Exp, r=[f"zb{a}"], w=[f"Pw{a}"])
                        if new:
                            P3 = P[0:ns, :].rearrange("p (h t) -> p h t", h=8)
                            TT("dve", P3, P3, mLTn[0:LS, :].unsqueeze(1).to_broadcast([LS, 8, LS]), ALU.mult,
                               r=[f"Pw{a}"] + KS, w=[f"Pw{a}"])
                        TT("dve", carry, carry, ps[pt_][:, 0:64], ALU.add, r=["carry", psk[pt_]], w=["carry"])
                        MM(ps[po][0:64, 0:512], P[0:ns, :], V[0:ns, :], first, last, r=[f"Pw{a}", vkey], w=[psk[po]])
                t3 = tmpo[0:64].rearrange("p a b -> p (a b)")
                TT("dve", tmpo[0:64], ps[po][0:64, 0:512].rearrange("p (h d) -> p h d", h=8),
                   bm[0:64, :].unsqueeze(2).to_broadcast([64, 8, 64]), ALU.mult, r=[psk[po]] + KS, w=["tmpo"])
                S.op("dve", lambda e: e.reduce_sum(out=red[0:64, :], in_=tmpo[0:64].rearrange("p h d -> p d h"), axis=AX.X),
                     r=["tmpo"], w=["red"])
                if kind == 0:
                    S.op("dve", lambda e: e.reciprocal(out=rd[0:64, :], in_=ps[pd][0:64, 0:2]), r=[psk[pd]], w=["rd"])
                    TS("dve", o16[0:64, :], red[0:64, :], rd[0:64, 0:1], ALU.mult, r=["red", "rd"], w=["o16"])
                else:
                    CP("dve", o16[0:64, :], red[0:64, :], r=["red"], w=["o16"])
                for hd in range(8):
                    r0 = kind * 512 + hd * 64
                    DMAX("sp", mixed[r0:r0 + 64, col:col + LS].rearrange("d t -> t d"), o16[hd * LS:(hd + 1) * LS, :],
                         r=["o16"], w=["mixedall"])

    ctx = Ctx()
    ctx.__dict__.update(locals())
    return ctx


def program(T, PAST, mixers=True, NPHYS=2560):
    c = build(T, PAST, NPHYS)
    c.ada_phase()
    for l in range(2):
        if l == 0:
            c.pass_mod(c.xT, 0, 0, store_x=True)
        c.pass_ffn_in(l, 0)
        c.pass_out(c.hid, "hid", DFF, c.ffn_w_out[l, 0], l, 2, True, (l, 3))
        if mixers and l == 0:
            c.proj0()
            c.s5_phase()
            c.hgrn_phase()
            c.token_bufs()
            c.pass_out(c.mixed, "mixed", D, c.ab_w_out, l, 5, False, (l, 6))
        elif mixers and l == 1:
            c.proj1()
            if STAGE >= 2:
                c.attn_phase()
            if STAGE >= 3:
                c.attn_sample_phase()
            c.token_bufs()
            c.pass_out(c.mixed, "mixed", D, c.cd_w_out, l, 5, False, (l, 6))
        else:
            c.pass_mod(c.xs, l, 6, store_x=False)
        c.pass_ffn_in(l, 1)
        c.pass_out(c.hid, "hid", DFF, c.ffn_w_out[l, 1], l, 8, True, (1, 0) if l == 0 else "final")
    c.S.emit()
    return c


def host_inputs(inp, T, ncores=8, layers=2):
    maps = []
    xp = inp["x_prompt"]
    f32 = np.float32
    a_re, a_im, ld = inp["s5_a_re"], inp["s5_a_im"], inp["s5_log_dt"]
    s5_st = np.stack([a_re.reshape(16, 128).T, a_im.reshape(16, 128).T,
                      np.repeat(ld.reshape(16, 2, 1), 64, axis=2).reshape(16, 128).T]).astype(f32)
    s5_ch = np.zeros((3, 4, 128, 512), f32)
    s5_bblk = np.zeros((2, 4, 128, 512), f32)
    for c in range(4):
        s5_ch[0, c] = a_re[8 * c:8 * c + 8].reshape(1, 512)
        s5_ch[1, c] = a_im[8 * c:8 * c + 8].reshape(1, 512)
        s5_ch[2, c] = np.repeat(ld[8 * c:8 * c + 8], 64).reshape(1, 512)
        for ri, b in enumerate((inp["s5_b_re"], inp["s5_b_im"])):
            blk = np.zeros((8, 16, 8, 64), f32)
            for g in range(8):
                blk[g, :, g, :] = b[8 * c + g].T
            s5_bblk[ri, c] = blk.reshape(128, 512)
    s5_cblk = np.zeros((2, 16, 128, 128), f32)
    for j in range(16):
        for ri, cc in enumerate((inp["s5_c_re"], inp["s5_c_im"])):
            blk = np.zeros((2, 64, 8, 16), f32)
            for gg in range(2):
                blk[gg, :, 2 * (j % 4) + gg, :] = cc[2 * j + gg].T
            s5_cblk[ri, j] = blk.reshape(128, 128)
    common = {
        "ada_w": inp["ada_w"],
        "ada_bT": np.ascontiguousarray(inp["ada_b"].reshape(2, NADA * KT, 128).transpose(0, 2, 1)),
        "ffn_w_in": inp["ffn_w_in"], "ffn_w_out": inp["ffn_w_out"],
        "fnw": np.ascontiguousarray(inp["final_norm_w"].reshape(KT, 128).T),
        "ab_w_in": inp["ab_w_in"], "ab_w_out": inp["ab_w_out"],
        "s5_st": s5_st, "s5_ch": s5_ch, "s5_bblk": s5_bblk, "s5_cblk": s5_cblk,
        "s5_dfm": np.ascontiguousarray(inp["s5_d"].reshape(4, 128).T),
        "glu_w": inp["s5_glu_w"], "glu_bfm": np.ascontiguousarray(inp["s5_glu_b"].reshape(4, 128).T),
        "lbl_fm": np.ascontiguousarray(inp["hgrn_lb_logits"].reshape(3, 4, 128).transpose(2, 1, 0)),
        "hnw": np.ascontiguousarray(inp["hgrn_norm_w"].reshape(128, 1)),
        "cd_w_in": inp["cd_w_in"], "cd_w_out": inp["cd_w_out"],
        "bfb_in": np.ascontiguousarray(np.broadcast_to(inp["cd_b_f"].reshape(1, 8), (128, 8))).astype(f32),
        "bsb_in": np.ascontiguousarray(np.broadcast_to(inp["cd_b_sb"].reshape(1, 8), (128, 8))).astype(f32),
        "c_fk": inp["cache_fox_k"].reshape(-1, 512), "c_fv": inp["cache_fox_v"].reshape(-1, 512),
        "c_lf": inp["cache_fox_logf"].reshape(-1, 8),
        "c_sk": inp["cache_sb_k"].reshape(-1, 512), "c_sv": inp["cache_sb_v"].reshape(-1, 512),
    }
    for c in range(ncores):
        b = c % xp.shape[0]
        sl = slice(NS * c, NS * (c + 1))
        xs_ = inp["x_sample"][sl].reshape(NSTOK, D)
        xT = np.ascontiguousarray(np.concatenate([xp[b, :T], xs_], axis=0).T)
        cT = np.ascontiguousarray(np.concatenate([inp["c_prompt"][b:b + 1], inp["c_sample"][sl]], axis=0).T)
        s5s = np.stack([inp["state_s5_re"][sl].reshape(NS, 16, 128).transpose(2, 1, 0),
                        inp["state_s5_im"][sl].reshape(NS, 16, 128).transpose(2, 1, 0)], axis=1)
        m = dict(common)
        m.update({
            "xT": xT, "cT": cT,
            "s5s_in": np.ascontiguousarray(s5s.astype(f32)),
            "hgs_in": np.ascontiguousarray(inp["state_hgrn"][sl].reshape(NS * 4, 128, 128)),
            "pt_rep": np.ascontiguousarray(np.broadcast_to(inp["page_table"][sl].reshape(1, -1), (128, NS * inp["page_table"].shape[1]))).astype(np.int32),
        })
        maps.append(m)
    return maps


def assemble(res, T, nb=2):
    R_ = res
    ncores = len(R_)
    def prm(f):
        return np.stack([f(R_[b]) for b in range(nb)])
    def smp(f):
        return np.concatenate([f(R_[c]) for c in range(ncores)], axis=0)
    y_p = prm(lambda r: r["yT"][:, :T].T)
    y_s = smp(lambda r: r["yT"][:, T:].T.reshape(NS, LS, D))
    s5r_p = prm(lambda r: r["s5p_out"][:, 0, :].T.reshape(32, 64))
    s5i_p = prm(lambda r: r["s5p_out"][:, 1, :].T.reshape(32, 64))
    hg_p = prm(lambda r: r["hgp_out"])
    fk_p = prm(lambda r: r["qk1"][512:1024, :T].T.reshape(T, 8, 64))
    fv_p = prm(lambda r: r["fv_out"][:T].reshape(T, 8, 64))
    flf_p = prm(lambda r: r["logf_out"][:T])
    sk_p = prm(lambda r: r["qk1"][1536:2048, :T].T.reshape(T, 8, 64))
    sv_p = prm(lambda r: r["sv_out"][:T].reshape(T, 8, 64))
    s5r_s = smp(lambda r: r["s5s_out"][:, 0].transpose(2, 1, 0).reshape(NS, 32, 64))
    s5i_s = smp(lambda r: r["s5s_out"][:, 1].transpose(2, 1, 0).reshape(NS, 32, 64))
    hg_s = smp(lambda r: r["hgs_out"].reshape(NS, 4, 128, 128))
    fk_s = smp(lambda r: r["qk1"][512:1024, T:].T.reshape(NS, LS, 8, 64))
    fv_s = smp(lambda r: r["fv_out"][T:].reshape(NS, LS, 8, 64))
    flf_s = smp(lambda r: r["logf_out"][T:].reshape(NS, LS, 8))
    sk_s = smp(lambda r: r["qk1"][1536:2048, T:].T.reshape(NS, LS, 8, 64))
    sv_s = smp(lambda r: r["sv_out"][T:].reshape(NS, LS, 8, 64))
    outs = (y_p, y_s, s5r_p, s5i_p, hg_p, fk_p, fv_p, flf_p, sk_p, sv_p,
            s5r_s, s5i_s, hg_s, fk_s, fv_s, flf_s, sk_s, sv_s)
    return tuple(np.ascontiguousarray(o, dtype=np.float32) for o in outs)


def kernel(**inputs):
    inp = {k: np.asarray(v) for k, v in inputs.items()}
    T = inp["x_prompt"].shape[1]
    PAST = inp["page_table"].shape[1] * 128
    NPHYS = inp["cache_fox_k"].shape[0]
    c = program(T, PAST, True, NPHYS)
    maps = host_inputs(inp, T)
    res = run_bass_kernel_spmd(c.nc, maps, core_ids=list(range(8)))
    return assemble(res.results, T, inp["x_prompt"].shape[0])
```

```python
import contextlib
import numpy as np
import concourse.bass as bass
import concourse.mybir as mybir
from concourse.bass_utils import run_bass_kernel_spmd

F32 = mybir.dt.float32
BF16 = mybir.dt.bfloat16
I32 = mybir.dt.int32
AF = mybir.ActivationFunctionType
ALU = mybir.AluOpType
AX = mybir.AxisListType

D = 1024
KT = 8
DFF = 2816
HT = 22
NADA = 9
EPS = 1e-6
NS = 4
LS = 8
NSTOK = NS * LS


class Sched:
    EPOCH = 8000
    NDMA = 32

    def __init__(self, nc, es):
        self.nc = nc
        self.es = es
        self.engs = {"pe": nc.tensor, "act": nc.scalar, "dve": nc.vector, "pool": nc.gpsimd, "sp": nc.sync}
        self.q = {e: [] for e in self.engs}
        self.cnt = {e: 0 for e in self.engs}
        self.esems = {e: [] for e in self.engs}
        self.dsems = [es.enter_context(nc.semaphore(f"dma{i}")) for i in range(2 * self.NDMA)]
        self.duse = [0] * (2 * self.NDMA)
        self.drr = {"sp": 0, "pool": 0}
        self.lastw = {}
        self.readers = {}
        self.seen = {e: {} for e in self.engs}
        self.nops = 0
        self.pbar = {e: [] for e in self.engs}
        self.lasttok = {}

    def barrier(self):
        toks = list(self.lasttok.values())
        for i, s_ in enumerate(self.dsems):
            if self.duse[i]:
                toks.append((s_, self.duse[i] * 16, "dma"))
        for e in self.engs:
            self.pbar[e] = list(toks)

    def _esem(self, e, epoch):
        while len(self.esems[e]) <= epoch:
            self.esems[e].append(self.es.enter_context(self.nc.semaphore(f"s_{e}{len(self.esems[e])}")))
        return self.esems[e][epoch]

    def op(self, e, fn, r=(), w=(), dma=False):
        r = list(r)
        w = list(w)
        for k in list(r):
            if len(k) == 3 and k.startswith("ps") and k[2].isdigit():
                r.remove(k)
                if k not in w:
                    w.append(k)
        deps = []
        for k in list(r) + list(w):
            t = self.lastw.get(k)
            if t is not None:
                deps.append(t)
        for k in w:
            deps.extend(self.readers.get(k, ()))
        if self.pbar[e]:
            deps.extend(self.pbar[e])
            self.pbar[e] = []
        waits = []
        seen = self.seen[e]

        def need(tok):
            sem, val, eng = tok
            if eng == "pe" and e == "pe" and not dma:
                return
            if seen.get(id(sem), 0) >= val:
                return
            seen[id(sem)] = val
            waits.append((sem, val))

        for t in deps:
            need(t)
        if dma:
            qn = "pool" if e == "pool" else "sp"
            i = self.drr[qn] + (self.NDMA if qn == "pool" else 0)
            self.drr[qn] = (self.drr[qn] + 1) % self.NDMA
            pv = self.duse[i] * 16
            if pv > 0:
                need((self.dsems[i], pv, "dma"))
            self.duse[i] += 1
            tok = (self.dsems[i], pv + 16, "dma")
            inc = (self.dsems[i], 16)
        else:
            c = self.cnt[e]
            self.cnt[e] += 1
            ep, v = divmod(c, self.EPOCH)
            sem = self._esem(e, ep)
            tok = (sem, v + 1, e)
            inc = (sem, 1)
        self.q[e].append((waits, fn, inc))
        if not dma:
            self.lasttok[e] = tok
        for k in w:
            self.lastw[k] = tok
            self.readers[k] = []
        for k in r:
            self.readers.setdefault(k, []).append(tok)
        self.nops += 1
        return tok

    def emit(self):
        nc = self.nc
        fin = []
        for i, s in enumerate(self.dsems):
            if self.duse[i]:
                fin.append((s, self.duse[i] * 16))
        for e in self.engs:
            c = self.cnt[e]
            if c:
                ep, v = divmod(c - 1, self.EPOCH)
                fin.append((self.esems[e][ep], v + 1))
        qs = self.q

        def replay(e, eng):
            for waits, fn, inc in qs[e]:
                for sem, val in waits:
                    eng.wait_ge(sem, val)
                ins = fn(eng)
                ins.then_inc(inc[0], inc[1])

        with nc.Block() as block:
            @block.tensor
            def _(eng):
                replay("pe", eng)

            @block.scalar
            def _(eng):
                replay("act", eng)

            @block.vector
            def _(eng):
                replay("dve", eng)

            @block.gpsimd
            def _(eng):
                replay("pool", eng)

            @block.sync
            def _(eng):
                replay("sp", eng)
                for sem, val in fin:
                    eng.wait_ge(sem, val)


class Ctx:
    pass


DEBUG = False
DBG5 = False
ASTOP = 99
SBCUT = 99
STAGE = 99


def build(T, PAST, NPHYS=2560):
    nc = bass.Bass("TRN2", target_bir_lowering=False)
    es = contextlib.ExitStack()
    S = Sched(nc, es)
    NB = T // 512
    NTOK = T + NSTOK
    blocks = [(i * 512, 512) for i in range(NB)] + [(T, NSTOK)]

    def dram_in(name, shape, dt=F32):
        return nc.dram_tensor(name, list(shape), dt, kind="ExternalInput").ap()

    def dram_out(name, shape, dt=F32):
        return nc.dram_tensor(name, list(shape), dt, kind="ExternalOutput").ap()

    def dram_tmp(name, shape, dt):
        if DEBUG:
            return nc.dram_tensor(name, list(shape), dt, kind="ExternalOutput").ap()
        return nc.dram_tensor(name, list(shape), dt).ap()

    def sb(name, shape, dt=F32):
        return es.enter_context(nc.sbuf_tensor(name, list(shape), dt))

    xT = dram_in("xT", [D, NTOK])
    cT = dram_in("cT", [D, 1 + NS])
    ada_w = dram_in("ada_w", [2, D, NADA * D])
    ada_bT = dram_in("ada_bT", [2, 128, NADA * KT])
    ffn_w_in = dram_in("ffn_w_in", [2, 2, D, 2 * DFF])
    ffn_w_out = dram_in("ffn_w_out", [2, 2, DFF, D])
    fnw = dram_in("fnw", [128, KT])
    yT = dram_out("yT", [D, NTOK])
    xs = dram_tmp("xs", [D, NTOK], F32)
    hbuf = dram_tmp("hbuf", [D, NTOK], BF16)
    hid = dram_tmp("hid", [DFF, NTOK], BF16)

    ps = [es.enter_context(nc.psum_tensor(f"ps{i}", [128, 512], F32)) for i in range(8)]
    psk = [f"ps{i}" for i in range(8)]
    psrr = [0]

    def nextps():
        i = psrr[0]
        psrr[0] = (i + 1) % 8
        return i

    ones_bf = sb("ones_bf", [128, 128], BF16)
    S.op("pool", lambda e: e.memset(ones_bf[:], 1.0), w=["ones_bf"])

    ARENA_W = 47104
    arena = sb("arena", [128, ARENA_W], F32)
    apos = [0]

    def aalloc(shape, dt=F32):
        n = int(np.prod(shape))
        words = n if dt == F32 or dt == I32 else (n + 1) // 2
        words = (words + 7) // 8 * 8
        a = apos[0]
        assert a + words <= ARENA_W, ("arena overflow", a, words)
        apos[0] = a + words
        v = arena[:, a:a + words]
        if dt != F32:
            v = v.bitcast(dt)
        v = v[:, 0:n]
        if len(shape) == 2:
            v = v.rearrange("p (a b) -> p a b", a=shape[0])
        elif len(shape) == 3:
            v = v.rearrange("p (a b c) -> p a b c", a=shape[0], b=shape[1])
        return v

    def areset():
        S.barrier()
        apos[0] = 0

    def token_bufs():
        areset()
        g = {}
        g["wbig"] = aalloc([25600], BF16)
        g["xb"] = [aalloc([KT, 512], F32) for i in range(2)]
        g["hb"] = [aalloc([HT, 512], BF16) for i in range(2)]
        g["ob"] = [aalloc([HT, 512], BF16)] * 2
        g["sq"] = aalloc([KT, 512], BF16)
        g["t1"] = [aalloc([512], F32) for i in range(2)]
        g["t2"] = [aalloc([512], F32) for i in range(2)]
        g["rstd"] = aalloc([512], F32)
        return g

    tb = token_bufs()
    wbig, xb, hb, ob, sq, t1, t2, rstd = (tb[k] for k in ["wbig", "xb", "hb", "ob", "sq", "t1", "t2", "rstd"])
    modv = [sb(f"modv{l}", [128, NADA * KT, 1 + NS], F32) for l in range(2)]
    s1v = [sb(f"s1v{l}", [128, NADA * KT, 1 + NS], F32) for l in range(2)]
    ghv = [sb(f"ghv{l}", [128, NADA * KT, 1 + NS], F32) for l in range(2)]
    modS = sb("modS", [128, 3, KT, NSTOK], F32)
    fnw_sb = sb("fnw_sb", [128, KT], F32)
    S.op("sp", lambda e: e.dma_start(out=fnw_sb[:], in_=fnw[:, :]), w=["fnw_sb"], dma=True)

    def ada_phase():
        csb = sb("csb", [128, KT, 1 + NS], F32)
        scs = sb("scs", [128, KT, 1 + NS], F32)
        adab = sb("adab", [128, NADA * KT], F32)
        S.op("sp", lambda e: e.dma_start(out=csb[:], in_=cT.rearrange("(k p) j -> p k j", p=128)), w=["csb"], dma=True)
        S.op("act", lambda e: e.activation(out=scs[:], in_=csb[:], func=AF.Silu), r=["csb"], w=["scs"])
        wst = wbig.bitcast(F32).rearrange("p (s k c) -> p s k c", s=2, k=KT)
        CW = 768
        ci = 0
        for l in range(2):
            S.op("sp", lambda e, l=l: e.dma_start(out=adab[:], in_=ada_bT[l]), w=["adab"], dma=True)
            for ch in range(NADA * D // CW):
                slot = ci % 2
                ci += 1
                S.op("sp", lambda e, l=l, ch=ch, slot=slot: e.dma_start(
                    out=wst[:, slot, :, 0:CW],
                    in_=ada_w[l, :, ch * CW:(ch + 1) * CW].rearrange("(k p) c -> p k c", p=128)),
                    w=[f"wst{slot}"], dma=True)
                for fl in range(CW // 128):
                    ft = ch * (CW // 128) + fl
                    pi = nextps()
                    for kt in range(KT):
                        S.op("pe", lambda e, pi=pi, slot=slot, kt=kt, fl=fl: e.matmul(
                            ps[pi][:, 0:1 + NS], lhsT=wst[:, slot, kt, fl * 128:(fl + 1) * 128],
                            rhs=scs[:, kt, :], start=(kt == 0), stop=(kt == KT - 1)),
                            r=[f"wst{slot}", "scs"], w=[psk[pi]])
                    S.op("dve", lambda e, pi=pi, ft=ft, l=l: e.tensor_scalar(
                        out=modv[l][:, ft, :], in0=ps[pi][:, 0:1 + NS], scalar1=adab[:, ft:ft + 1], scalar2=None,
                        op0=ALU.add), r=[psk[pi], "adab"], w=[f"modv{l}"])
            S.op("dve", lambda e, l=l: e.tensor_scalar(out=s1v[l][:], in0=modv[l][:], scalar1=1.0, scalar2=None,
                                                       op0=ALU.add), r=[f"modv{l}"], w=[f"s1v{l}"])
            S.op("dve", lambda e, l=l: e.tensor_scalar(out=ghv[l][:], in0=modv[l][:], scalar1=0.5, scalar2=None,
                                                       op0=ALU.mult), r=[f"modv{l}"], w=[f"ghv{l}"])

    def set_modS(which, src, chunk):
        for j in range(NS):
            S.op("dve", lambda e, j=j: e.tensor_copy(
                out=modS[:, which, :, j * LS:(j + 1) * LS],
                in_=src[:, chunk * KT:(chunk + 1) * KT, 1 + j:2 + j].to_broadcast([128, KT, LS])),
                r=[src.name if hasattr(src, "name") else "modsrc"], w=["modS"])

    def load_w(w_ap, K, cols, key="wbig"):
        kt_n = K // 128
        F = sum(c1 - c0 for c0, c1 in cols)
        view = wbig[:, 0:kt_n * F].rearrange("p (k f) -> p k f", k=kt_n)
        for kt in range(kt_n):
            o = 0
            for (c0, c1) in cols:
                for f0 in range(c0, c1, 2048):
                    f1 = min(c1, f0 + 2048)
                    S.op("pool", lambda e, kt=kt, f0=f0, f1=f1, o=o: e.dma_start(
                        out=view[:, kt, o:o + f1 - f0], in_=w_ap[kt * 128:(kt + 1) * 128, f0:f1]),
                        w=[key, "wst0", "wst1"], dma=True)
                    o += f1 - f0
        return view

    def norm_mod_block(xblk, xkey, bi, l, chunk, final=False):
        c0, n = blocks[bi]
        slot = bi % 2
        S.op("act", lambda e: e.activation(out=sq[:, :, 0:n], in_=xblk[:, :, 0:n], func=AF.Square),
             r=[xkey], w=["sq"])
        pi = nextps()
        for kt in range(KT):
            S.op("pe", lambda e, kt=kt: e.matmul(ps[pi][:, 0:n], lhsT=ones_bf[:], rhs=sq[:, kt, 0:n],
                                                 start=(kt == 0), stop=(kt == KT - 1)),
                 r=["ones_bf", "sq"], w=[psk[pi]])
        S.op("act", lambda e: e.activation(out=rstd[:, 0:n], in_=ps[pi][:, 0:n], func=AF.Sqrt, scale=1.0 / D,
                                           bias=epsb[:, 0:1]), r=[psk[pi], "epsb"], w=["rstd"])
        S.op("dve", lambda e: e.reciprocal(out=rstd[:, 0:n], in_=rstd[:, 0:n]), r=["rstd"], w=["rstd"])
        if final:
            for kt in range(KT):
                S.op("dve", lambda e, kt=kt: e.scalar_tensor_tensor(
                    out=xblk[:, kt, 0:n], in0=xblk[:, kt, 0:n], scalar=fnw_sb[:, kt:kt + 1], in1=rstd[:, 0:n],
                    op0=ALU.mult, op1=ALU.mult), r=[xkey, "rstd", "fnw_sb"], w=[xkey])
            S.op("sp", lambda e: e.dma_start(out=yT[:, c0:c0 + n].rearrange("(k p) t -> p k t", p=128),
                                             in_=xblk[:, :, 0:n]), r=[xkey], w=["yT"], dma=True)
            return
        o = ob[slot]
        okey = "ob0"
        prompt = n == 512
        for kt in range(KT):
            tt = t1[kt % 2]
            S.op("dve", lambda e, kt=kt, tt=tt: e.tensor_tensor(out=tt[:, 0:n], in0=xblk[:, kt, 0:n],
                                                                 in1=rstd[:, 0:n], op=ALU.mult),
                 r=[xkey, "rstd"], w=[f"t1_{kt % 2}"])
            ft = chunk * KT + kt
            if prompt:
                S.op("act", lambda e, kt=kt, tt=tt, ft=ft: e.activation(
                    out=o[:, kt, 0:n], in_=tt[:, 0:n], func=AF.Identity,
                    scale=s1v[l][:, ft + KT, 0:1], bias=modv[l][:, ft, 0:1]),
                    r=[f"t1_{kt % 2}", f"s1v{l}", f"modv{l}"], w=[okey])
            else:
                S.op("dve", lambda e, kt=kt, tt=tt: e.tensor_tensor(out=tt[:, 0:n], in0=tt[:, 0:n],
                                                                     in1=modS[:, 0, kt, :], op=ALU.mult),
                     r=[f"t1_{kt % 2}", "modS"], w=[f"t1_{kt % 2}"])
                S.op("dve", lambda e, kt=kt, tt=tt: e.tensor_tensor(out=o[:, kt, 0:n], in0=tt[:, 0:n],
                                                                     in1=modS[:, 1, kt, :], op=ALU.add),
                     r=[f"t1_{kt % 2}", "modS"], w=[okey])
        S.op("sp", lambda e: e.dma_start(out=hbuf[:, c0:c0 + n].rearrange("(k p) t -> p k t", p=128),
                                         in_=o[:, 0:KT, 0:n]), r=[okey], w=[f"hbuf{bi}"], dma=True)

    epsb = sb("epsb", [128, 1], F32)
    S.op("pool", lambda e: e.memset(epsb[:], EPS), w=["epsb"])

    def prep_modS(l, chunk):
        set_modS(0, s1v[l], chunk + 1)
        set_modS(1, modv[l], chunk)

    def pass_mod(src, l, chunk, store_x):
        prep_modS(l, chunk)
        for bi, (c0, n) in enumerate(blocks):
            slot = bi % 2
            S.op("sp", lambda e, c0=c0, n=n, slot=slot: e.dma_start(
                out=xb[slot][:, :, 0:n], in_=src[:, c0:c0 + n].rearrange("(k p) t -> p k t", p=128)),
                r=[f"{src.name}{bi}"], w=[f"xb{slot}"], dma=True)
            if store_x:
                S.op("sp", lambda e, c0=c0, n=n, slot=slot: e.dma_start(
                    out=xs[:, c0:c0 + n].rearrange("(k p) t -> p k t", p=128), in_=xb[slot][:, :, 0:n]),
                    r=[f"xb{slot}"], w=[f"xs{bi}"], dma=True)
            norm_mod_block(xb[slot], f"xb{slot}", bi, l, chunk)

    def pass_ffn_in(l, j):
      HH = HT // 2
      HW = HH * 128
      for hh in range(2):
        wv = load_w(ffn_w_in[l, j], D, [(hh * HW, (hh + 1) * HW), (DFF + hh * HW, DFF + (hh + 1) * HW)])
        for bi, (c0, n) in enumerate(blocks):
            slot = bi % 2
            h = hb[slot]
            S.op("sp", lambda e, c0=c0, n=n, h=h: e.dma_start(
                out=h[:, 0:KT, 0:n], in_=hbuf[:, c0:c0 + n].rearrange("(k p) t -> p k t", p=128)),
                r=[f"hbuf{bi}"], w=[f"hb{slot}"], dma=True)
            o = ob[slot]
            for i in range(HH):
                pg = nextps()
                pu = nextps()
                for kt in range(KT):
                    S.op("pe", lambda e, kt=kt, pg=pg, i=i, h=h, n=n: e.matmul(
                        ps[pg][:, 0:n], lhsT=wv[:, kt, i * 128:(i + 1) * 128], rhs=h[:, kt, 0:n],
                        start=(kt == 0), stop=(kt == KT - 1)), r=["wbig", f"hb{slot}"], w=[psk[pg]])
                for kt in range(KT):
                    S.op("pe", lambda e, kt=kt, pu=pu, i=i, h=h, n=n: e.matmul(
                        ps[pu][:, 0:n], lhsT=wv[:, kt, HW + i * 128:HW + (i + 1) * 128], rhs=h[:, kt, 0:n],
                        start=(kt == 0), stop=(kt == KT - 1)), r=["wbig", f"hb{slot}"], w=[psk[pu]])
                tt = t2[i % 2]
                S.op("act", lambda e, pg=pg, tt=tt, n=n: e.activation(out=tt[:, 0:n], in_=ps[pg][:, 0:n], func=AF.Silu),
                     r=[psk[pg]], w=[f"t2_{i % 2}"])
                S.op("dve", lambda e, pu=pu, tt=tt, i=i, o=o, n=n: e.tensor_tensor(
                    out=o[:, i, 0:n], in0=tt[:, 0:n], in1=ps[pu][:, 0:n], op=ALU.mult),
                    r=[psk[pu], f"t2_{i % 2}"], w=["ob0"])
            S.op("sp", lambda e, c0=c0, n=n, o=o, hh=hh: e.dma_start(
                out=hid[hh * HW:(hh + 1) * HW, c0:c0 + n].rearrange("(k p) t -> p k t", p=128), in_=o[:, 0:HH, 0:n]),
                r=["ob0"], w=[f"hid{bi}"], dma=True)

    def pass_out(src, srckey, K, w_ap, l, gchunk, half, nxt):
        ktn = K // 128
        wv = load_w(w_ap, K, [(0, D)])
        gsrc = ghv[l] if half else modv[l]
        set_modS(2, gsrc, gchunk)
        if nxt not in (None, "final"):
            prep_modS(nxt[0], nxt[1])
        for bi, (c0, n) in enumerate(blocks):
            slot = bi % 2
            h = hb[slot]
            S.op("sp", lambda e, c0=c0, n=n, h=h: e.dma_start(
                out=h[:, 0:ktn, 0:n], in_=src[:, c0:c0 + n].rearrange("(k p) t -> p k t", p=128)),
                r=[f"{srckey}{bi}"], w=[f"hb{slot}"], dma=True)
            S.op("sp", lambda e, c0=c0, n=n, slot=slot: e.dma_start(
                out=xb[slot][:, :, 0:n], in_=xs[:, c0:c0 + n].rearrange("(k p) t -> p k t", p=128)),
                r=[f"xs{bi}"], w=[f"xb{slot}"], dma=True)
            x = xb[slot]
            for ft in range(KT):
                pi = nextps()
                for kt in range(ktn):
                    S.op("pe", lambda e, kt=kt, pi=pi, ft=ft, h=h, n=n: e.matmul(
                        ps[pi][:, 0:n], lhsT=wv[:, kt, ft * 128:(ft + 1) * 128], rhs=h[:, kt, 0:n],
                        start=(kt == 0), stop=(kt == ktn - 1)), r=["wbig", f"hb{slot}"], w=[psk[pi]])
                if n == 512:
                    gi = gchunk * KT + ft
                    S.op("dve", lambda e, pi=pi, ft=ft, x=x, gi=gi, n=n: e.scalar_tensor_tensor(
                        out=x[:, ft, 0:n], in0=ps[pi][:, 0:n], scalar=gsrc[:, gi:gi + 1, 0:1].rearrange("p a b -> p (a b)"),
                        in1=x[:, ft, 0:n], op0=ALU.mult, op1=ALU.add),
                        r=[psk[pi], f"xb{slot}", f"modv{l}", f"ghv{l}"], w=[f"xb{slot}"])
                else:
                    tt = t1[ft % 2]
                    S.op("dve", lambda e, pi=pi, ft=ft, tt=tt, n=n: e.tensor_tensor(
                        out=tt[:, 0:n], in0=ps[pi][:, 0:n], in1=modS[:, 2, ft, :], op=ALU.mult),
                        r=[psk[pi], "modS"], w=[f"t1_{ft % 2}"])
                    S.op("dve", lambda e, ft=ft, tt=tt, x=x, n=n: e.tensor_tensor(
                        out=x[:, ft, 0:n], in0=x[:, ft, 0:n], in1=tt[:, 0:n], op=ALU.add),
                        r=[f"t1_{ft % 2}", f"xb{slot}"], w=[f"xb{slot}"])
            if nxt == "final":
                norm_mod_block(x, f"xb{slot}", bi, 0, 0, final=True)
            else:
                S.op("sp", lambda e, c0=c0, n=n, x=x: e.dma_start(
                    out=xs[:, c0:c0 + n].rearrange("(k p) t -> p k t", p=128), in_=x[:, :, 0:n]),
                    r=[f"xb{slot}"], w=[f"xs{bi}"], dma=True)
                if nxt is not None:
                    norm_mod_block(x, f"xb{slot}", bi, nxt[0], nxt[1])


    def TT(eng, out, a, b, op, r, w):
        S.op(eng, lambda e: e.tensor_tensor(out=out, in0=a, in1=b, op=op), r=r, w=w)

    def TS(eng, out, a, s1, op0, r, w, s2=None, op1=None):
        if op1 is None:
            S.op(eng, lambda e: e.tensor_scalar(out=out, in0=a, scalar1=s1, scalar2=None, op0=op0), r=r, w=w)
        else:
            S.op(eng, lambda e: e.tensor_scalar(out=out, in0=a, scalar1=s1, scalar2=s2, op0=op0, op1=op1), r=r, w=w)

    def STT(out, a, sc, b, op0, op1, r, w):
        S.op("dve", lambda e: e.scalar_tensor_tensor(out=out, in0=a, scalar=sc, in1=b, op0=op0, op1=op1), r=r, w=w)

    def ACT(out, in_, func, r, w, scale=1.0, bias=None):
        if bias is None:
            S.op("act", lambda e: e.activation(out=out, in_=in_, func=func, scale=scale), r=r, w=w)
        else:
            S.op("act", lambda e: e.activation(out=out, in_=in_, func=func, scale=scale, bias=bias), r=r, w=w)

    def MM(out, lhsT, rhs, start, stop, r, w):
        S.op("pe", lambda e: e.matmul(out, lhsT=lhsT, rhs=rhs, start=start, stop=stop), r=r, w=w)

    def TR(out, in_, ident, r, w):
        S.op("pe", lambda e: e.transpose(out, in_, ident), r=r, w=w)

    def DMA(eng, out, in_, r, w):
        S.op(eng, lambda e: e.dma_start(out=out, in_=in_), r=r, w=w, dma=True)

    def DMAX(eng, out, in_, r, w):
        S.op(eng, lambda e: e.dma_start(out=out, in_=in_, allow_slow_non_contiguous=True), r=r, w=w, dma=True)

    def CP(eng, out, in_, r, w):
        S.op(eng, lambda e: e.tensor_copy(out=out, in_=in_), r=r, w=w)

    def MS(eng, out, val, w):
        S.op(eng, lambda e: e.memset(out, val), w=w)

    ones_f = sb("ones_f", [128, 128], F32)
    MS("pool", ones_f[:], 1.0, ["ones_f"])
    ident = sb("ident", [128, 128], F32)
    S.op("pool", lambda e: e.affine_select(out=ident[:], in_=ones_f[:], pattern=[[-1, 128]], compare_op=ALU.is_equal,
                                           fill=0.0, base=0, channel_multiplier=1), r=["ones_f"], w=["ident"])
    halfpi = sb("halfpi", [128, 1], F32)
    MS("pool", halfpi[:], float(np.pi / 2), ["halfpi"])

    stg = [sb(f"stg{i}", [128, 512], F32) for i in range(4)]
    stgrr = [0]

    def evac(ps_ap, pkey, n, np_, scale=None):
        i = stgrr[0]
        stgrr[0] = (i + 1) % 4
        o = stg[i][0:np_, 0:n]
        if i % 2 == 0:
            ACT(o, ps_ap, AF.Copy, r=[pkey], w=[f"stg{i}"], scale=(1.0 if scale is None else scale))
        else:
            if scale is None:
                CP("dve", o, ps_ap, r=[pkey], w=[f"stg{i}"])
            else:
                TS("dve", o, ps_ap, float(scale), ALU.mult, r=[pkey], w=[f"stg{i}"])
        return o, f"stg{i}"

    def pass_proj(w_ap, F, fm, tm):
        wv = load_w(w_ap, D, [(0, F)])
        for bi, (c0, n) in enumerate(blocks):
            slot = bi % 2
            h = hb[slot]
            DMA("sp", h[:, 0:KT, 0:n], hbuf[:, c0:c0 + n].rearrange("(k p) t -> p k t", p=128),
                r=[f"hbuf{bi}"], w=[f"hb{slot}"])
            for (col0, ncols, dst, row0, scale) in fm:
                for ft in range(ncols // 128):
                    pi = nextps()
                    for kt in range(KT):
                        MM(ps[pi][:, 0:n], wv[:, kt, col0 + ft * 128:col0 + (ft + 1) * 128], h[:, kt, 0:n],
                           kt == 0, kt == KT - 1, r=["wbig", f"hb{slot}"], w=[psk[pi]])
                    o, ok = evac(ps[pi][:, 0:n], psk[pi], n, 128, scale)
                    for dst_ in (dst if isinstance(dst, (list, tuple)) else [dst]):
                        DMA("sp", dst_[row0 + ft * 128:row0 + (ft + 1) * 128, c0:c0 + n], o, r=[ok], w=[f"{dst_.name}{bi}"])
            for (col0, ncols, handler) in tm:
                for sub in range((n + 127) // 128):
                    nt = min(128, n - sub * 128)
                    pi = nextps()
                    for kt in range(KT):
                        MM(ps[pi][0:nt, 0:ncols], h[:, kt, sub * 128:sub * 128 + nt], wv[:, kt, col0:col0 + ncols],
                           kt == 0, kt == KT - 1, r=["wbig", f"hb{slot}"], w=[psk[pi]])
                    handler(pi, nt, c0 + sub * 128, bi)


    ab_w_in = dram_in("ab_w_in", [D, 2560])
    ab_w_out = dram_in("ab_w_out", [D, D])
    s5_st = dram_in("s5_st", [3, 128, 16])
    s5_ch = dram_in("s5_ch", [3, 4, 128, 512])
    s5_bblk = dram_in("s5_bblk", [2, 4, 128, 512])
    s5_cblk = dram_in("s5_cblk", [2, 16, 128, 128])
    s5_dfm = dram_in("s5_dfm", [128, 4])
    glu_w = dram_in("glu_w", [512, 512])
    glu_bfm = dram_in("glu_bfm", [128, 4])
    lbl_fm = dram_in("lbl_fm", [128, 4, 3])
    hnw = dram_in("hnw", [128, 1])
    s5s_in = dram_in("s5s_in", [128, 2, 16, NS])
    hgs_in = dram_in("hgs_in", [NS * 4, 128, 128])
    s5p_out = dram_out("s5p_out", [128, 2, 16])
    s5s_out = dram_out("s5s_out", [128, 2, 16, NS])
    hgp_out = dram_out("hgp_out", [4, 128, 128])
    hgs_out = dram_out("hgs_out", [NS * 4, 128, 128])
    pr0 = dram_tmp("pr0", [2560, NTOK], F32)
    vtok0 = dram_tmp("vtok0", [NTOK, 512], F32)
    mixed = dram_tmp("mixed", [D, NTOK], BF16)

    def proj0():
        def vh(pi, nt, tok0, bi):
            o, ok = evac(ps[pi][0:nt, 0:512], psk[pi], 512, nt)
            DMA("sp", vtok0[tok0:tok0 + nt, :], o, r=[ok], w=[f"vtok0{bi}"])
        pass_proj(ab_w_in, 2560, [(0, 2560, pr0, 0, None)], [(1536, 512, vh)])

    def s5_phase():
        areset()
        K5 = ["s5p"]
        scr = aalloc([8192])
        st = aalloc([3, 16])
        DMA("sp", st, s5_st.rearrange("a p j -> p a j"), r=[], w=K5)
        dt = aalloc([16]); mag = aalloc([16]); th = aalloc([16]); cs = aalloc([16]); sn = aalloc([16])
        ta16 = aalloc([16]); tb16 = aalloc([16])
        ACT(dt, st[:, 2, :], AF.Exp, r=K5, w=K5)
        TT("dve", ta16, st[:, 0, :], dt, ALU.mult, r=K5, w=K5)
        ACT(mag, ta16, AF.Exp, r=K5, w=K5)
        TT("dve", th, st[:, 1, :], dt, ALU.mult, r=K5, w=K5)
        ACT(cs, th, AF.Sin, r=K5 + ["halfpi"], w=K5, scale=1.0 / 64, bias=halfpi[:, 0:1])
        ACT(sn, th, AF.Sin, r=K5, w=K5, scale=1.0 / 64)

        def sq_cs(c_, s_, a_, b_):
            TT("dve", a_, c_, c_, ALU.mult, r=K5, w=K5)
            TT("dve", b_, s_, s_, ALU.mult, r=K5, w=K5)
            TT("dve", b_, a_, b_, ALU.subtract, r=K5, w=K5)
            TT("dve", a_, c_, s_, ALU.mult, r=K5, w=K5)
            TS("dve", s_, a_, 2.0, ALU.mult, r=K5, w=K5)
            CP("dve", c_, b_, r=K5, w=K5)
        for _ in range(6):
            sq_cs(cs, sn, ta16, tb16)
        Wbu = aalloc([2, 4, 512], BF16)
        chp = aalloc([3, 512]); bbl = aalloc([2, 512])
        w = [scr[:, i * 512:(i + 1) * 512] for i in range(8)]
        for c in range(4):
            DMA("sp", chp, s5_ch[:, c].rearrange("a p x -> p a x"), r=K5, w=K5)
            DMA("sp", bbl, s5_bblk[:, c].rearrange("a p x -> p a x"), r=K5, w=K5)
            dtc, lrd, magc, thc, cc, sc, t0, t1_ = w
            ACT(dtc, chp[:, 2, :], AF.Exp, r=K5, w=K5)
            TT("dve", lrd, chp[:, 0, :], dtc, ALU.mult, r=K5, w=K5)
            ACT(magc, lrd, AF.Exp, r=K5, w=K5)
            TT("dve", thc, chp[:, 1, :], dtc, ALU.mult, r=K5, w=K5)
            ACT(cc, thc, AF.Sin, r=K5 + ["halfpi"], w=K5, scale=1.0 / 64, bias=halfpi[:, 0:1])
            ACT(sc, thc, AF.Sin, r=K5, w=K5, scale=1.0 / 64)
            for _ in range(6):
                sq_cs(cc, sc, t0, t1_)
            TT("dve", cc, cc, magc, ALU.mult, r=K5, w=K5)
            TT("dve", sc, sc, magc, ALU.mult, r=K5, w=K5)
            TS("dve", cc, cc, -1.0, ALU.add, r=K5, w=K5)
            lr, li = chp[:, 0, :], chp[:, 1, :]
            TT("dve", t0, lr, lr, ALU.mult, r=K5, w=K5)
            TT("dve", t1_, li, li, ALU.mult, r=K5, w=K5)
            TT("dve", t0, t0, t1_, ALU.add, r=K5, w=K5)
            S.op("dve", lambda e, t0=t0: e.reciprocal(out=t0, in_=t0), r=K5, w=K5)
            TT("dve", dtc, cc, lr, ALU.mult, r=K5, w=K5)
            TT("dve", lrd, sc, li, ALU.mult, r=K5, w=K5)
            TT("dve", dtc, dtc, lrd, ALU.add, r=K5, w=K5)
            TT("dve", dtc, dtc, t0, ALU.mult, r=K5, w=K5)
            TT("dve", lrd, sc, lr, ALU.mult, r=K5, w=K5)
            TT("dve", magc, cc, li, ALU.mult, r=K5, w=K5)
            TT("dve", lrd, lrd, magc, ALU.subtract, r=K5, w=K5)
            TT("dve", lrd, lrd, t0, ALU.mult, r=K5, w=K5)
            TT("dve", magc, dtc, bbl[:, 0, :], ALU.mult, r=K5, w=K5)
            TT("dve", thc, lrd, bbl[:, 1, :], ALU.mult, r=K5, w=K5)
            TT("dve", Wbu[:, 0, c, :], magc, thc, ALU.subtract, r=K5, w=K5)
            TT("dve", magc, dtc, bbl[:, 1, :], ALU.mult, r=K5, w=K5)
            TT("dve", thc, lrd, bbl[:, 0, :], ALU.mult, r=K5, w=K5)
            TT("dve", Wbu[:, 1, c, :], magc, thc, ALU.add, r=K5, w=K5)
        Wc = aalloc([2, 16, 128])
        DMA("sp", Wc, s5_cblk.rearrange("a j p x -> p a j x"), r=[], w=K5)
        TS("dve", Wc[:, 1], Wc[:, 1], -1.0, ALU.mult, r=K5, w=K5)
        dfm = aalloc([4]); gbf = aalloc([4])
        DMA("sp", dfm, s5_dfm[:, :], r=[], w=K5)
        DMA("sp", gbf, glu_bfm[:, :], r=[], w=K5)
        Wg = aalloc([4, 512], BF16)
        DMA("pool", Wg, glu_w.rearrange("(k p) f -> p k f", p=128), r=[], w=K5)
        Ec = aalloc([16, 512]); Es = aalloc([16, 512])
        pc = aalloc([16]); psn = aalloc([16])
        tA = scr[:, 0:4096].rearrange("p (a b) -> p a b", a=16)
        tB = scr[:, 4096:8192].rearrange("p (a b) -> p a b", a=16)
        CP("dve", pc, cs, r=K5, w=K5)
        CP("dve", psn, sn, r=K5, w=K5)
        MS("dve", Ec[:, :, 0:1], 1.0, K5)
        MS("dve", Es[:, :, 0:1], 0.0, K5)
        L = 1
        while L < 512:
            pcB = pc.unsqueeze(2).to_broadcast([128, 16, L])
            psB = psn.unsqueeze(2).to_broadcast([128, 16, L])
            TT("dve", tA[:, :, 0:L], Ec[:, :, 0:L], pcB, ALU.mult, r=K5, w=K5)
            TT("dve", tB[:, :, 0:L], Es[:, :, 0:L], psB, ALU.mult, r=K5, w=K5)
            TT("dve", Ec[:, :, L:2 * L], tA[:, :, 0:L], tB[:, :, 0:L], ALU.subtract, r=K5, w=K5)
            TT("dve", tA[:, :, 0:L], Ec[:, :, 0:L], psB, ALU.mult, r=K5, w=K5)
            TT("dve", tB[:, :, 0:L], Es[:, :, 0:L], pcB, ALU.mult, r=K5, w=K5)
            TT("dve", Es[:, :, L:2 * L], tA[:, :, 0:L], tB[:, :, 0:L], ALU.add, r=K5, w=K5)
            sq_cs(pc, psn, ta16, tb16)
            L *= 2
        if DBG5:
            dbgE = dram_out("dbgE", [128, 2, 16, 512])
            DMA("sp", dbgE[:, 0], Ec, r=K5, w=["dbgE"])
            DMA("sp", dbgE[:, 1], Es, r=K5, w=["dbgE"])
            dbgP = dram_out("dbgP", [128, 3, 16])
            DMA("sp", dbgP[:, 0], mag, r=K5, w=["dbgP"])
            DMA("sp", dbgP[:, 1], cs, r=K5, w=["dbgP"])
            DMA("sp", dbgP[:, 2], sn, r=K5, w=["dbgP"])
        S.barrier()
        lbr = aalloc([16]); lbi = aalloc([16])
        TT("dve", lbr, mag, cs, ALU.mult, r=K5, w=K5)
        TT("dve", lbi, mag, sn, ALU.mult, r=K5, w=K5)

        ini = aalloc([2, 16])
        MS("dve", ini, 0.0, K5)
        hlast = aalloc([2, 16])
        ub = [aalloc([4, 512], BF16) for _ in range(2)]
        uf = [aalloc([4, 512]) for _ in range(2)]
        wk = [[scr[:, (a * 8 + i) * 512:(a * 8 + i + 1) * 512] for i in range(8)] for a in range(2)]
        yf = aalloc([4, 512]); ygb = aalloc([4, 512], BF16); outb = aalloc([4, 512], BF16)
        g1 = aalloc([512]); g2 = aalloc([512])
        Hs = scr[:, 0:1024].rearrange("p (a j s t) -> p a j s t", a=2, j=16, s=NS)
        bus = scr[:, 1024:2048].rearrange("p (a j s t) -> p a j s t", a=2, j=16, s=NS)
        h0 = scr[:, 2048:2176].rearrange("p (a j s) -> p a j s", a=2, j=16)

        def epilogue(bi, n, usl, uk):
            c0 = blocks[bi][0]
            for c in range(4):
                y = yf[:, c, 0:n]
                TT("dve", g1[:, 0:n], y, y, ALU.mult, r=["yf"], w=["g1"])
                TS("dve", g1[:, 0:n], g1[:, 0:n], 0.044715, ALU.mult, r=["g1"], w=["g1"], s2=1.0, op1=ALU.add)
                TT("dve", g1[:, 0:n], g1[:, 0:n], y, ALU.mult, r=["g1", "yf"], w=["g1"])
                ACT(g2[:, 0:n], g1[:, 0:n], AF.Sigmoid, r=["g1"], w=["g2"], scale=1.5957691216057308)
                TT("dve", y, y, g2[:, 0:n], ALU.mult, r=["yf", "g2"], w=["yf"])
                ACT(ygb[:, c, 0:n], y, AF.Copy, r=["yf"], w=["ygb"])
            for co in range(4):
                pz = nextps()
                for ci in range(4):
                    MM(ps[pz][:, 0:n], Wg[:, ci, co * 128:(co + 1) * 128], ygb[:, ci, 0:n], ci == 0, ci == 3,
                       r=K5 + ["ygb"], w=[psk[pz]])
                ACT(g2[:, 0:n], ps[pz][:, 0:n], AF.Sigmoid, r=[psk[pz]] + K5, w=["g2"], bias=gbf[:, co:co + 1])
                TT("dve", outb[:, co, 0:n], yf[:, co, 0:n], g2[:, 0:n], ALU.mult, r=["yf", "g2"], w=["outb"])
            DMA("sp", mixed[0:512, c0:c0 + n].rearrange("(k p) t -> p k t", p=128), outb[:, :, 0:n],
                r=["outb"], w=[f"mixed{bi}"])

        for bi in range(NB):
            c0, n = blocks[bi]
            sl = bi % 2
            DMA("sp", uf[sl][:, :, 0:n], pr0[0:512, c0:c0 + n].rearrange("(k p) t -> p k t", p=128),
                r=[f"pr0{bi}"], w=[f"uf{sl}"])
            DMA("pool", ub[sl][:, :, 0:n], pr0[0:512, c0:c0 + n].rearrange("(k p) t -> p k t", p=128),
                r=[f"pr0{bi}"], w=[f"ub{sl}"])
            for c in range(4):
                py = c % 2
                for jj in range(4):
                    j = 4 * c + jj
                    ws = (j % 2)
                    d_re, d_im, g_re, g_im, h_re, h_im, ta, tb = wk[ws]
                    kk = f"wk{ws}"
                    pr_ = 2 + (2 * j) % 6
                    pi_ = 2 + (2 * j + 1) % 6
                    MM(ps[pr_][:, 0:n], Wbu[:, 0, c, jj * 128:(jj + 1) * 128], ub[sl][:, c, 0:n], True, True,
                       r=K5 + [f"ub{sl}"], w=[psk[pr_]])
                    MM(ps[pi_][:, 0:n], Wbu[:, 1, c, jj * 128:(jj + 1) * 128], ub[sl][:, c, 0:n], True, True,
                       r=K5 + [f"ub{sl}"], w=[psk[pi_]])
                    ec, es_ = Ec[:, j, 0:n], Es[:, j, 0:n]
                    TT("dve", ta[:, 0:n], ps[pr_][:, 0:n], ec, ALU.mult, r=[psk[pr_]] + K5, w=[kk])
                    TT("dve", tb[:, 0:n], ps[pi_][:, 0:n], es_, ALU.mult, r=[psk[pi_]] + K5, w=[kk])
                    TT("dve", d_re[:, 0:n], ta[:, 0:n], tb[:, 0:n], ALU.add, r=[kk], w=[kk])
                    TT("dve", ta[:, 0:n], ps[pi_][:, 0:n], ec, ALU.mult, r=[psk[pi_]] + K5, w=[kk])
                    TT("dve", tb[:, 0:n], ps[pr_][:, 0:n], es_, ALU.mult, r=[psk[pr_]] + K5, w=[kk])
                    TT("dve", d_im[:, 0:n], ta[:, 0:n], tb[:, 0:n], ALU.subtract, r=[kk], w=[kk])
                    S.op("dve", lambda e, g_re=g_re, d_re=d_re, j=j, n=n: e.tensor_tensor_scan(
                        out=g_re[:, 0:n], data0=mag[:, j:j + 1].to_broadcast([128, n]), data1=d_re[:, 0:n], initial=ini[:, 0, j:j + 1],
                        op0=ALU.mult, op1=ALU.add), r=[kk, "ini"] + K5, w=[kk])
                    S.op("dve", lambda e, g_im=g_im, d_im=d_im, j=j, n=n: e.tensor_tensor_scan(
                        out=g_im[:, 0:n], data0=mag[:, j:j + 1].to_broadcast([128, n]), data1=d_im[:, 0:n], initial=ini[:, 1, j:j + 1],
                        op0=ALU.mult, op1=ALU.add), r=[kk, "ini"] + K5, w=[kk])
                    TT("dve", ta[:, 0:n], g_re[:, 0:n], ec, ALU.mult, r=[kk] + K5, w=[kk])
                    TT("dve", tb[:, 0:n], g_im[:, 0:n], es_, ALU.mult, r=[kk] + K5, w=[kk])
                    TT("dve", h_re[:, 0:n], ta[:, 0:n], tb[:, 0:n], ALU.subtract, r=[kk], w=[kk])
                    TT("dve", ta[:, 0:n], g_re[:, 0:n], es_, ALU.mult, r=[kk] + K5, w=[kk])
                    TT("dve", tb[:, 0:n], g_im[:, 0:n], ec, ALU.mult, r=[kk] + K5, w=[kk])
                    TT("dve", h_im[:, 0:n], ta[:, 0:n], tb[:, 0:n], ALU.add, r=[kk], w=[kk])
                    if DBG5 and bi == 0 and j in (0, 5):
                        dbgH = dram_out(f"dbgH{j}", [128, 6, 512])
                        for ii, tt_ in enumerate((d_re, d_im, g_re, g_im, h_re, h_im)):
                            DMA("sp", dbgH[:, ii, :], tt_[:, 0:n], r=[kk], w=[f"dbgH{j}"])
                        dbgB = dram_out(f"dbgB{j}", [128, 2, 512])
                        o_, ok_ = evac(ps[pr_][:, 0:n], psk[pr_], n, 128)
                        DMA("sp", dbgB[:, 0, :], o_, r=[ok_], w=[f"dbgB{j}"])
                        o_, ok_ = evac(ps[pi_][:, 0:n], psk[pi_], n, 128)
                        DMA("sp", dbgB[:, 1, :], o_, r=[ok_], w=[f"dbgB{j}"])
                    hrl, hil = h_re[:, n - 1:n], h_im[:, n - 1:n]
                    TT("dve", ta[:, 0:1], hil, sn[:, j:j + 1], ALU.mult, r=[kk] + K5, w=[kk])
                    STT(ini[:, 0, j:j + 1], hrl, cs[:, j:j + 1], ta[:, 0:1], ALU.mult, ALU.subtract, r=[kk] + K5, w=["ini"])
                    TT("dve", ta[:, 0:1], hil, cs[:, j:j + 1], ALU.mult, r=[kk] + K5, w=[kk])
                    STT(ini[:, 1, j:j + 1], hrl, sn[:, j:j + 1], ta[:, 0:1], ALU.mult, ALU.add, r=[kk] + K5, w=["ini"])
                    if bi == NB - 1:
                        CP("dve", hlast[:, 0, j:j + 1], hrl, r=[kk], w=["hlast"])
                        CP("dve", hlast[:, 1, j:j + 1], hil, r=[kk], w=["hlast"])
                    MM(ps[py][:, 0:n], Wc[:, 0, j, :], h_re[:, 0:n], jj == 0, False, r=K5 + [kk], w=[psk[py]])
                    MM(ps[py][:, 0:n], Wc[:, 1, j, :], h_im[:, 0:n], False, jj == 3, r=K5 + [kk], w=[psk[py]])
                STT(yf[:, c, 0:n], uf[sl][:, c, 0:n], dfm[:, c:c + 1], ps[py][:, 0:n], ALU.mult, ALU.add,
                    r=[f"uf{sl}", psk[py]] + K5, w=["yf"])
            epilogue(bi, n, sl, None)
        DMA("sp", s5p_out[:, :, :], hlast, r=["hlast"], w=["s5p_out"])

        S.barrier()
        DMA("sp", h0, s5s_in[:, :, :, :], r=[], w=["h0"])
        bi = NB
        c0, n = blocks[bi]
        DMA("sp", uf[0][:, :, 0:n], pr0[0:512, c0:c0 + n].rearrange("(k p) t -> p k t", p=128),
            r=[f"pr0{bi}"], w=["uf0"])
        DMA("pool", ub[0][:, :, 0:n], pr0[0:512, c0:c0 + n].rearrange("(k p) t -> p k t", p=128),
            r=[f"pr0{bi}"], w=["ub0"])
        pr_, pi_ = nextps(), nextps()
        for j in range(16):
            c, jj = divmod(j, 4)
            MM(ps[pr_][:, j * n:(j + 1) * n], Wbu[:, 0, c, jj * 128:(jj + 1) * 128], ub[0][:, c, 0:n], True, True,
               r=K5 + ["ub0"], w=[psk[pr_]])
            MM(ps[pi_][:, j * n:(j + 1) * n], Wbu[:, 1, c, jj * 128:(jj + 1) * 128], ub[0][:, c, 0:n], True, True,
               r=K5 + ["ub0"], w=[psk[pi_]])
        CP("dve", bus[:, 0].rearrange("p j s t -> p (j s t)"), ps[pr_][:, 0:16 * n], r=[psk[pr_]], w=["bus"])
        CP("dve", bus[:, 1].rearrange("p j s t -> p (j s t)"), ps[pi_][:, 0:16 * n], r=[psk[pi_]], w=["bus"])
        sA = scr[:, 2176:2240].rearrange("p (j s) -> p j s", j=16)
        sB = scr[:, 2240:2304].rearrange("p (j s) -> p j s", j=16)
        lbrB = lbr.unsqueeze(2).to_broadcast([128, 16, NS])
        lbiB = lbi.unsqueeze(2).to_broadcast([128, 16, NS])
        KH = ["Hs"]
        for t in range(LS):
            pr_re = h0[:, 0] if t == 0 else Hs[:, 0, :, :, t - 1]
            pr_im = h0[:, 1] if t == 0 else Hs[:, 1, :, :, t - 1]
            rr = KH + ["h0", "bus"] + K5
            TT("dve", sA, pr_re, lbrB, ALU.mult, r=rr, w=["sA"])
            TT("dve", sB, pr_im, lbiB, ALU.mult, r=rr, w=["sB"])
            TT("dve", sA, sA, sB, ALU.subtract, r=["sA", "sB"], w=["sA"])
            TT("dve", Hs[:, 0, :, :, t], sA, bus[:, 0, :, :, t], ALU.add, r=["sA", "bus"], w=KH)
            TT("dve", sA, pr_re, lbiB, ALU.mult, r=rr, w=["sA"])
            TT("dve", sB, pr_im, lbrB, ALU.mult, r=rr, w=["sB"])
            TT("dve", sA, sA, sB, ALU.add, r=["sA", "sB"], w=["sA"])
            TT("dve", Hs[:, 1, :, :, t], sA, bus[:, 1, :, :, t], ALU.add, r=["sA", "bus"], w=KH)
        hs_fin = scr[:, 2304:2432].rearrange("p (a j s) -> p a j s", a=2, j=16)
        CP("dve", hs_fin, Hs[:, :, :, :, LS - 1], r=KH, w=["hs_fin"])
        DMA("sp", s5s_out[:, :, :, :], hs_fin, r=["hs_fin"], w=["s5s_out"])
        for c in range(4):
            py = nextps()
            for jj in range(4):
                j = 4 * c + jj
                MM(ps[py][:, 0:n], Wc[:, 0, j, :], Hs[:, 0, j].rearrange("p s t -> p (s t)"), jj == 0, False,
                   r=K5 + KH, w=[psk[py]])
                MM(ps[py][:, 0:n], Wc[:, 1, j, :], Hs[:, 1, j].rearrange("p s t -> p (s t)"), False, jj == 3,
                   r=K5 + KH, w=[psk[py]])
            STT(yf[:, c, 0:n], uf[0][:, c, 0:n], dfm[:, c:c + 1], ps[py][:, 0:n], ALU.mult, ALU.add,
                r=["uf0", psk[py]] + K5, w=["yf"])
        epilogue(bi, n, 0, None)


    def hgrn_phase():
        areset()
        KP = ["hgp"]
        lbl = aalloc([4, 3]); ssum = aalloc([4]); lb = aalloc([4]); oml = aalloc([4]); nw = aalloc([1])
        DMA("sp", lbl, lbl_fm[:, :, :], r=[], w=KP)
        DMA("sp", nw, hnw[:, :], r=[], w=KP)
        ACT(lbl, lbl, AF.Exp, r=KP, w=KP)
        S.op("dve", lambda e: e.reduce_sum(out=ssum, in_=lbl, axis=AX.X), r=KP, w=KP)
        S.op("dve", lambda e: e.reciprocal(out=ssum, in_=ssum), r=KP, w=KP)
        TT("dve", lb, lbl[:, :, 0], ssum, ALU.mult, r=KP, w=KP)
        TS("dve", oml, lb, -1.0, ALU.mult, r=KP, w=KP, s2=1.0, op1=ALU.add)
        maskLE = aalloc([64])
        S.op("pool", lambda e: e.affine_select(out=maskLE[0:64, :], in_=ones_f[0:64, 0:64], pattern=[[1, 64]],
                                               compare_op=ALU.is_ge, fill=0.0, base=0, channel_multiplier=-1),
             r=["ones_f"], w=KP)
        m01 = {}
        for C_, n_ in ((64, 512), (8, 8)):
            m = aalloc([n_])
            MS("dve", m, 1.0, KP)
            MS("dve", m.rearrange("p (a c) -> p a c", c=C_)[:, :, 0:1], 0.0, KP)
            m01[C_] = m
        NU = 4
        U = []
        for u in range(NU):
            d = {}
            for nm in ("qr", "fr", "gr", "f", "b", "qin", "kin", "kdec", "tmp"):
                d[nm] = aalloc([512])
            d["v"] = aalloc([8, 128])
            d["kdT"] = aalloc([8, 128])
            d["a"] = aalloc([8])
            d["S"] = [aalloc([128]), aalloc([128])]
            d["scm"] = [aalloc([64]), aalloc([64])]
            d["osq"] = aalloc([512], BF16)
            d["ob"] = aalloc([512], BF16)
            U.append(d)
        rot = [4]

        def rps():
            i = rot[0]
            rot[0] = 4 + (i - 4 + 1) % 4
            return i

        def hg_block(units, n, C, si):
            nch = n // C
            for (u, head, col0) in units:
                d = U[u]; k = f"hu{u}"
                DMA("sp", d["qr"][:, 0:n], pr0[512 + head * 128:512 + (head + 1) * 128, col0:col0 + n], r=["pr0all"], w=[k])
                DMA("sp", d["fr"][:, 0:n], pr0[1024 + head * 128:1024 + (head + 1) * 128, col0:col0 + n], r=["pr0all"], w=[k])
                DMA("sp", d["gr"][:, 0:n], pr0[2048 + head * 128:2048 + (head + 1) * 128, col0:col0 + n], r=["pr0all"], w=[k])
                vv = d["v"].rearrange("p a d -> p (a d)")[0:C, 0:nch * 128].rearrange("p (a d) -> p a d", d=128)
                DMA("sp", vv, vtok0[col0:col0 + n, head * 128:(head + 1) * 128].rearrange("(a c) d -> c a d", c=C),
                    r=["vtok0all"], w=[k])
            for (u, head, col0) in units:
                d = U[u]; k = f"hu{u}"
                ACT(d["f"][:, 0:n], d["fr"][:, 0:n], AF.Sigmoid, r=[k], w=[k])
                ACT(d["gr"][:, 0:n], d["gr"][:, 0:n], AF.Silu, r=[k], w=[k])
                ACT(d["qr"][:, 0:n], d["qr"][:, 0:n], AF.Silu, r=[k], w=[k])
            for (u, head, col0) in units:
                d = U[u]; k = f"hu{u}"
                TS("dve", d["f"][:, 0:n], d["f"][:, 0:n], oml[:, head:head + 1], ALU.mult, r=[k] + KP, w=[k],
                   s2=lb[:, head:head + 1], op1=ALU.add)
                ACT(d["tmp"][:, 0:n], d["f"][:, 0:n], AF.Ln, r=[k], w=[k])
                S.op("dve", lambda e, d=d: e.tensor_tensor_scan(
                    out=d["b"][:, 0:n], data0=m01[C][:, 0:n], data1=d["tmp"][:, 0:n], initial=0.0,
                    op0=ALU.mult, op1=ALU.add), r=[k] + KP, w=[k])
                TS("dve", d["f"][:, 0:n], d["f"][:, 0:n], -1.0, ALU.mult, r=[k], w=[k], s2=1.0, op1=ALU.add)
                ACT(d["tmp"][:, 0:n], d["b"][:, 0:n], AF.Exp, r=[k], w=[k])
                TT("dve", d["qin"][:, 0:n], d["qr"][:, 0:n], d["tmp"][:, 0:n], ALU.mult, r=[k], w=[k])
                ACT(d["tmp"][:, 0:n], d["b"][:, 0:n], AF.Exp, r=[k], w=[k], scale=-1.0)
                TT("dve", d["kin"][:, 0:n], d["f"][:, 0:n], d["tmp"][:, 0:n], ALU.mult, r=[k], w=[k])
                b3 = d["b"][:, 0:n].rearrange("p (a c) -> p a c", c=C)
                t3 = d["tmp"][:, 0:n].rearrange("p (a c) -> p a c", c=C)
                TT("dve", t3, b3[:, :, C - 1:C].to_broadcast([128, nch, C]), b3, ALU.subtract, r=[k], w=[k])
                ACT(d["tmp"][:, 0:n], d["tmp"][:, 0:n], AF.Exp, r=[k], w=[k])
                TT("dve", d["kdec"][:, 0:n], d["f"][:, 0:n], d["tmp"][:, 0:n], ALU.mult, r=[k], w=[k])
                ACT(d["a"][:, 0:nch], b3[:, :, C - 1], AF.Exp, r=[k], w=[k])
                for g0 in range(0, nch, 4):
                    g1_ = min(nch, g0 + 4)
                    pt = rps()
                    for ch in range(g0, g1_):
                        TR(ps[pt][0:C, (ch - g0) * 128:(ch - g0 + 1) * 128], d["kdec"][:, ch * C:(ch + 1) * C], ident[:],
                           r=[k, "ident"], w=[psk[pt]])
                    CP("dve", d["kdT"].rearrange("p a d -> p (a d)")[0:C, g0 * 128:g1_ * 128],
                       ps[pt][0:C, 0:(g1_ - g0) * 128], r=[psk[pt]], w=[k])
            for ch in range(nch):
                for (u, head, col0) in units:
                    d = U[u]; k = f"hu{u}"; sk = f"hS{u}"
                    cols = slice(ch * C, (ch + 1) * C)
                    Sp = d["S"][si[u]]; Sn = d["S"][1 - si[u]]
                    vch = d["v"][0:C, ch, :] if False else d["v"].rearrange("p a d -> p (a d)")[0:C, ch * 128:(ch + 1) * 128]
                    kdch = d["kdT"].rearrange("p a d -> p (a d)")[0:C, ch * 128:(ch + 1) * 128]
                    pS = rps()
                    MM(ps[pS][0:C, 0:C], d["kin"][:, cols], d["qin"][:, cols], True, True, r=[k], w=[psk[pS]])
                    scm = d["scm"][ch % 2]
                    TT("dve", scm[0:C, 0:C], ps[pS][0:C, 0:C], maskLE[0:C, 0:C], ALU.mult, r=[psk[pS]] + KP, w=[k + f"scm{ch % 2}"])
                    MM(ps[u][:, cols], Sp, d["qin"][:, cols], True, False, r=[k, sk], w=[psk[u]])
                    MM(ps[u][:, cols], vch, scm[0:C, 0:C], False, True, r=[k, k + f"scm{ch % 2}"], w=[psk[u]])
                    pU = rps()
                    MM(ps[pU][:, 0:128], kdch, vch, True, True, r=[k], w=[psk[pU]])
                    STT(Sn, Sp, d["a"][:, ch:ch + 1], ps[pU][:, 0:128], ALU.mult, ALU.add, r=[k, sk, psk[pU]], w=[sk])
                    si[u] = 1 - si[u]
            for (u, head, col0) in units:
                d = U[u]; k = f"hu{u}"
                ACT(d["osq"][:, 0:n], ps[u][:, 0:n], AF.Square, r=[psk[u]], w=[k])
                pr_ = rps()
                MM(ps[pr_][:, 0:n], ones_bf[:], d["osq"][:, 0:n], True, True, r=[k, "ones_bf"], w=[psk[pr_]])
                ACT(d["tmp"][:, 0:n], ps[pr_][:, 0:n], AF.Sqrt, r=[psk[pr_], "epsb"], w=[k], scale=1.0 / 128, bias=epsb[:, 0:1])
                S.op("dve", lambda e, d=d: e.reciprocal(out=d["tmp"][:, 0:n], in_=d["tmp"][:, 0:n]), r=[k], w=[k])
                TT("dve", d["tmp"][:, 0:n], ps[u][:, 0:n], d["tmp"][:, 0:n], ALU.mult, r=[k, psk[u]], w=[k])
                STT(d["ob"][:, 0:n], d["tmp"][:, 0:n], nw[:, 0:1], d["gr"][:, 0:n], ALU.mult, ALU.mult, r=[k] + KP, w=[k])
                DMA("sp", mixed[512 + head * 128:512 + (head + 1) * 128, col0:col0 + n], d["ob"][:, 0:n],
                    r=[k], w=["mixedall"])

        si = [0] * NU
        for u in range(NU):
            MS("dve", U[u]["S"][0], 0.0, [f"hS{u}"])
        for bi in range(NB):
            hg_block([(u, u, bi * 512) for u in range(NU)], 512, 64, si)
        for u in range(NU):
            DMA("sp", hgp_out[u], U[u]["S"][si[u]], r=[f"hS{u}"], w=["hgp_out"])
        for sq_ in range(NS):
            for u in range(NU):
                DMA("sp", U[u]["S"][si[u]], hgs_in[sq_ * 4 + u], r=[], w=[f"hS{u}"])
            hg_block([(u, u, T + sq_ * LS) for u in range(NU)], LS, LS, si)
            for u in range(NU):
                DMA("sp", hgs_out[sq_ * 4 + u], U[u]["S"][si[u]], r=[f"hS{u}"], w=["hgs_out"])


    NP = PAST // 128
    NTt = T // 128
    NROWS = None
    cd_w_in = dram_in("cd_w_in", [D, 3080])
    cd_w_out = dram_in("cd_w_out", [D, D])
    bfb_in = dram_in("bfb_in", [128, 8])
    bsb_in = dram_in("bsb_in", [128, 8])
    pt_rep = dram_in("pt_rep", [128, NS * NP], I32)
    qk1 = dram_out("qk1", [2048, NTOK])
    fv_out = dram_out("fv_out", [NTOK, 512])
    sv_out = dram_out("sv_out", [NTOK, 512])
    logf_out = dram_out("logf_out", [NTOK, 8])
    qk1s = dram_tmp("qk1s", [2048, NTOK], F32)
    fv_s = dram_tmp("fv_s", [NTOK, 512], F32)
    sv_s = dram_tmp("sv_s", [NTOK, 512], F32)
    logf_s = dram_tmp("logf_s", [NTOK, 8], F32)

    def cache_in(name, w):
        return nc.dram_tensor(name, [NPHYS * 128, w], F32, kind="ExternalInput").ap()
    c_fk = cache_in("c_fk", 512); c_fv = cache_in("c_fv", 512); c_lf = cache_in("c_lf", 8)
    c_sk = cache_in("c_sk", 512); c_sv = cache_in("c_sv", 512)

    oneb = sb("oneb", [128, 1], F32)
    MS("pool", oneb[:], 1.0, ["oneb"])
    bfb = sb("bfb", [128, 8], F32)
    bsb = sb("bsb", [128, 8], F32)
    DMA("sp", bfb[:], bfb_in[:, :], r=[], w=["bfb"])
    DMA("sp", bsb[:], bsb_in[:, :], r=[], w=["bsb"])
    triS_f = sb("triS_f", [128, 128], F32)
    triI_f = sb("triI_f", [128, 128], F32)
    triLE_f = sb("triLE_f", [128, 128], F32)
    triI_b = sb("triI_b", [128, 128], BF16)
    for (tile_, base_, cm_, st_) in ((triS_f, -1, 1, -1), (triI_f, 0, 1, -1), (triLE_f, 0, -1, 1)):
        S.op("pool", lambda e, tile_=tile_, base_=base_, cm_=cm_, st_=st_: e.affine_select(
            out=tile_[:], in_=ones_f[:], pattern=[[st_, 128]], compare_op=ALU.is_ge, fill=0.0,
            base=base_, channel_multiplier=cm_), r=["ones_f"], w=[tile_.name])
    CP("dve", triI_b[:], triI_f[:], r=["triI_f"], w=["triI_b"])
    mLTn = sb("mLTn", [LS, LS], F32)
    S.op("pool", lambda e: e.affine_select(out=mLTn[:], in_=ones_f[0:LS, 0:LS], pattern=[[1, LS]], compare_op=ALU.is_ge,
                                           fill=0.0, base=-1, channel_multiplier=-1), r=["ones_f"], w=["asp"])

    def proj1():
        def fvh(dst, dst2):
            def h_(pi, nt, tok0, bi):
                o, ok = evac(ps[pi][0:nt, 0:512], psk[pi], 512, nt)
                DMA("sp", dst[tok0:tok0 + nt, :], o, r=[ok], w=[f"{dst.name}{bi}"])
                DMA("sp", dst2[tok0:tok0 + nt, :], o, r=[ok], w=[f"{dst2.name}{bi}"])
            return h_

        def lgh(pi, nt, tok0, bi):
            i = stgrr[0]
            stgrr[0] = (i + 1) % 4
            o = stg[i][0:nt, 0:8]
            k = f"stg{i}"
            TT("dve", o, ps[pi][0:nt, 0:8], bfb[0:nt, :], ALU.add, r=[psk[pi], "bfb"], w=[k])
            ACT(o, o, AF.Exp, r=[k], w=[k], scale=-1.0)
            ACT(o, o, AF.Ln, r=[k, "oneb"], w=[k], bias=oneb[0:nt, 0:1])
            TS("dve", o, o, -1.0, ALU.mult, r=[k], w=[k])
            DMA("sp", logf_out[tok0:tok0 + nt, :], o, r=[k], w=[f"logf_out{bi}"])
            DMA("sp", logf_s[tok0:tok0 + nt, :], o, r=[k], w=[f"logf_s{bi}"])
        pass_proj(cd_w_in, 3080,
                  [(0, 512, [qk1, qk1s], 0, 0.125), (512, 512, [qk1, qk1s], 512, None),
                   (1544, 512, [qk1, qk1s], 1024, 0.125), (2056, 512, [qk1, qk1s], 1536, None)],
                  [(1024, 512, fvh(fv_out, fv_s)), (2568, 512, fvh(sv_out, sv_s)), (1536, 8, lgh)])

    def attn_phase():
        areset()
        KA = ["attp"]
        mLE = aalloc([4, 512], BF16)
        mLT = aalloc([4, 512], BF16)
        onesb512 = aalloc([512], BF16)
        MS("pool", onesb512, 1.0, KA)
        for m in range(4):
            S.op("pool", lambda e, m=m: e.affine_select(out=mLE[:, m, :], in_=onesb512, pattern=[[1, 512]],
                 compare_op=ALU.is_ge, fill=0.0, base=-128 * m, channel_multiplier=-1), r=KA, w=KA)
            S.op("pool", lambda e, m=m: e.affine_select(out=mLT[:, m, :], in_=onesb512, pattern=[[1, 512]],
                 compare_op=ALU.is_ge, fill=0.0, base=-128 * m - 1, channel_multiplier=-1), r=KA, w=KA)
        if ASTOP <= 1:
            return
        lfall = aalloc([NTt, 8]); Fneg = aalloc([NTt, 8]); carF = aalloc([8])
        for a0_ in range(0, NTt, 16):
            a1_ = min(NTt, a0_ + 16)
            DMA("sp", lfall[:, a0_:a1_, :], logf_s[a0_ * 128:a1_ * 128, :].rearrange("(a p) h -> p a h", p=128), r=[], w=KA)
        MS("dve", carF, 0.0, KA)
        for t in range(NTt - 1, -1, -1):
            p1, p2 = 2 + (2 * t) % 6, 2 + (2 * t + 1) % 6
            MM(ps[p1][:, 0:8], triS_f[:], lfall[:, t, :], True, True, r=KA + ["triS_f"], w=[psk[p1]])
            MM(ps[p2][:, 0:8], ones_f[:], lfall[:, t, :], True, True, r=KA + ["ones_f"], w=[psk[p2]])
            TT("dve", Fneg[:, t, :], ps[p1][:, 0:8], carF, ALU.add, r=[psk[p1]] + KA, w=KA)
            TT("dve", carF, carF, ps[p2][:, 0:8], ALU.add, r=[psk[p2]] + KA, w=KA)
        if ASTOP <= 2:
            return
        kT = aalloc([T], BF16); qT = aalloc([T], BF16)
        Vf = aalloc([NTt, 2, 65], BF16)
        MS("dve", Vf[:, :, :, 64:65], 1.0, ["Vf"])
        e_t = [aalloc([512]) for _ in range(2)]
        zc_t = [aalloc([512]) for _ in range(2)]
        spb_t = [aalloc([512], BF16) for _ in range(2)]
        w_t = [aalloc([512], BF16) for _ in range(2)]
        carry = aalloc([512]); dsb = aalloc([512]); rden = aalloc([512])
        ob_ = [aalloc([512], BF16) for _ in range(2)]
        rot = [2]

        def rps():
            i = rot[0]
            rot[0] = 2 + (i - 2 + 1) % 6
            return i
        cnt = [0]
        for kind in range(2):
            if ASTOP == 4 and kind == 1:
                break
            if ASTOP == 5 and kind == 0:
                continue
            qrow, krow, vsrc = (0, 512, fv_s) if kind == 0 else (1024, 1536, sv_s)
            for hp in range(4):
                for t0_ in range(0, T, 2048):
                    t1_ = min(T, t0_ + 2048)
                    DMA("pool", kT[:, t0_:t1_], qk1s[krow + hp * 128:krow + (hp + 1) * 128, t0_:t1_], r=[], w=["kT"])
                    DMA("pool", qT[:, t0_:t1_], qk1s[qrow + hp * 128:qrow + (hp + 1) * 128, t0_:t1_], r=[], w=["qT"])
                for hh_ in range(2):
                    for a0_ in range(0, NTt, 16):
                        a1_ = min(NTt, a0_ + 16)
                        DMA("pool", Vf[:, a0_:a1_, hh_, 0:64],
                            vsrc[a0_ * 128:a1_ * 128, hp * 128 + hh_ * 64:hp * 128 + (hh_ + 1) * 64].rearrange("(a p) d -> p a d", p=128),
                            r=[], w=["Vf"])
                if ASTOP == 3:
                    continue
                for hh in range(2):
                    h = 2 * hp + hh
                    pr0_ = slice(hh * 64, (hh + 1) * 64)
                    for i in range(NB):
                        po = cnt[0] % 2
                        cnt[0] += 1
                        jl = list(range(4 * i + 3, -1, -1))
                        if kind == 1:
                            MS("dve", carry, 0.0, ["carry"])
                        st_ = {}

                        def stageA(idx):
                            j = jl[idx]
                            a = idx % 2
                            m = j - 4 * i
                            pz = rps()
                            MM(ps[pz][:, 0:512], kT[pr0_, j * 128:(j + 1) * 128], qT[pr0_, i * 512:(i + 1) * 512],
                               True, True, r=["kT", "qT"], w=[psk[pz]])
                            if kind == 0:
                                P = w_t[a]
                                ACT(P, ps[pz][:, 0:512], AF.Exp, r=[psk[pz]] + KA, w=[f"w{a}"], bias=Fneg[:, j, h:h + 1])
                                if m >= 0:
                                    TT("pool", P, P, mLE[:, m, :], ALU.mult, r=[f"w{a}"] + KA, w=[f"w{a}"])
                                st_[idx] = (pz,)
                            else:
                                e_, spb = e_t[a], spb_t[a]
                                ACT(e_, ps[pz][:, 0:512], AF.Exp, r=[psk[pz], "bsb"], w=[f"e{a}"], bias=bsb[:, h:h + 1])
                                ACT(spb, e_, AF.Ln, r=[f"e{a}", "oneb"], w=[f"spb{a}"], bias=oneb[:, 0:1])
                                if m >= 0:
                                    TT("pool", spb, spb, mLT[:, m, :], ALU.mult, r=[f"spb{a}"] + KA, w=[f"spb{a}"])
                                pc_, pt_ = rps(), rps()
                                MM(ps[pc_][:, 0:512], triI_b[:], spb, True, True, r=["triI_b", f"spb{a}"], w=[psk[pc_]])
                                MM(ps[pt_][:, 0:512], ones_bf[:], spb, True, True, r=["ones_bf", f"spb{a}"], w=[psk[pt_]])
                                st_[idx] = (pz, pc_, pt_)

                        def stageB(idx):
                            j = jl[idx]
                            a = idx % 2
                            m = j - 4 * i
                            first, last = idx == 0, idx == len(jl) - 1
                            if kind == 0:
                                MM(ps[po][0:65, 0:512], Vf[:, j, hh, :], w_t[a], first, last, r=["Vf", f"w{a}"], w=[psk[po]])
                            else:
                                pz, pc_, pt_ = st_[idx]
                                zc_, w_ = zc_t[a], w_t[a]
                                TT("dve", zc_, ps[pz][:, 0:512], carry, ALU.subtract, r=[psk[pz], "carry", f"e{a}"], w=[f"zc{a}"])
                                TT("dve", zc_, zc_, ps[pc_][:, 0:512], ALU.subtract, r=[f"zc{a}", psk[pc_]], w=[f"zc{a}"])
                                ACT(w_, zc_, AF.Exp, r=[f"zc{a}", "bsb"], w=[f"w{a}"], bias=bsb[:, h:h + 1])
                                if m >= 0:
                                    TT("pool", w_, w_, mLT[:, m, :], ALU.mult, r=[f"w{a}"] + KA, w=[f"w{a}"])
                                TT("dve", carry, carry, ps[pt_][:, 0:512], ALU.add, r=["carry", psk[pt_]], w=["carry"])
                                MM(ps[po][0:64, 0:512], Vf[:, j, hh, 0:64], w_, first, last, r=["Vf", f"w{a}"], w=[psk[po]])

                        stageA(0)
                        for idx in range(len(jl)):
                            if idx + 1 < len(jl):
                                stageA(idx + 1)
                            stageB(idx)
                        if SBCUT < 6 and kind == 1:
                            continue
                        oo = ob_[po]
                        if kind == 0:
                            ACT(dsb[64:65, :], ps[po][64:65, 0:512], AF.Copy, r=[psk[po]], w=["dsb"])
                            pd = rps()
                            MM(ps[pd][0:64, 0:512], ones_f[64:65, 0:64], dsb[64:65, :], True, True,
                               r=["dsb", "ones_f"], w=[psk[pd]])
                            S.op("dve", lambda e, pd=pd: e.reciprocal(out=rden[0:64, :], in_=ps[pd][0:64, 0:512]),
                                 r=[psk[pd]], w=["rden"])
                            TT("dve", oo[0:64, :], ps[po][0:64, 0:512], rden[0:64, :], ALU.mult, r=[psk[po], "rden"], w=[f"ob_{po}"])
                        else:
                            ACT(oo[0:64, :], ps[po][0:64, 0:512], AF.Copy, r=[psk[po]], w=[f"ob_{po}"])
                        DMA("sp", mixed[kind * 512 + h * 64:kind * 512 + (h + 1) * 64, i * 512:(i + 1) * 512], oo[0:64, :],
                            r=[f"ob_{po}"], w=["mixedall"])


    def attn_sample_phase():
        areset()
        KS = ["asp"]
        ptf = aalloc([NS * NP]); idx = aalloc([NS * NP], I32); pti = aalloc([NS * NP], I32)
        iop_i = aalloc([1], I32); iop = aalloc([1])
        DMA("sp", pti, pt_rep[:, :], r=[], w=KS)
        S.op("pool", lambda e: e.iota(out=iop_i, pattern=[[0, 1]], base=0, channel_multiplier=1), w=KS)
        CP("dve", iop, iop_i, r=KS, w=KS)
        CP("dve", ptf, pti, r=KS, w=KS)
        TS("dve", ptf, ptf, 128.0, ALU.mult, r=KS, w=KS, s2=iop[:, 0:1], op1=ALU.add)
        CP("dve", idx, ptf, r=KS, w=KS)
        bm = aalloc([8])
        S.op("pool", lambda e: e.affine_select(out=bm[0:64, :], in_=ones_f[0:64, 0:8], pattern=[[-8, 8]],
             compare_op=ALU.is_ge, fill=0.0, base=0, channel_multiplier=1), r=["ones_f"], w=KS)
        S.op("pool", lambda e: e.affine_select(out=bm[0:64, :], in_=bm[0:64, :], pattern=[[8, 8]],
             compare_op=ALU.is_ge, fill=0.0, base=7, channel_multiplier=-1), r=KS, w=KS)
        Qb = [aalloc([4, 64]) for _ in range(2)]
        qs = aalloc([4, LS]); kTn = [aalloc([4, LS]) for _ in range(2)]
        Vn = [aalloc([512]) for _ in range(2)]
        lfn = aalloc([8]); biasn = aalloc([8])
        kpg = [aalloc([512]) for _ in range(2)]
        vpg = [aalloc([512]) for _ in range(2)]
        kTp = [aalloc([4, 128]) for _ in range(2)]
        lfp = [aalloc([8]) for _ in range(2)]
        Fb = aalloc([8]); carF = aalloc([8])
        zb = [aalloc([64]) for _ in range(2)]
        e_ = [aalloc([64]) for _ in range(2)]
        sp_ = [aalloc([64]) for _ in range(2)]
        Pw = [aalloc([64]) for _ in range(2)]
        carry = aalloc([64])
        tmpo = aalloc([8, 64]); red = aalloc([64]); rd = aalloc([2]); o16 = aalloc([64], BF16)
        rot = [3]

        def rps():
            i = rot[0]
            rot[0] = 3 + (i - 3 + 1) % 5
            return i
        for sq_ in range(NS):
            col = T + sq_ * LS
            for kind in range(2):
                qrow, krow, vsrc, ck, cv = (0, 512, fv_s, c_fk, c_fv) if kind == 0 else (1024, 1536, sv_s, c_sk, c_sv)
                po, pd = 0, 1
                Q = Qb[kind]
                MS("dve", Q, 0.0, [f"Q{kind}"])
                DMA("sp", qs, qk1s[qrow:qrow + 512, col:col + LS].rearrange("(k p) t -> p k t", p=128), r=[], w=["qs"])
                for pr in range(4):
                    for hh in range(2):
                        hd = 2 * pr + hh
                        CP("dve", Q[hh * 64:(hh + 1) * 64, pr, hd * LS:(hd + 1) * LS], qs[hh * 64:(hh + 1) * 64, pr, :],
                           r=["qs"], w=[f"Q{kind}"])
                DMA("sp", kTn[kind], qk1s[krow:krow + 512, col:col + LS].rearrange("(k p) t -> p k t", p=128), r=[], w=[f"kTn{kind}"])
                DMA("sp", Vn[kind][0:LS, :], vsrc[col:col + LS, :], r=[], w=[f"Vn{kind}"])
                if kind == 0:
                    DMA("sp", lfn[0:LS, :], logf_s[col:col + LS, :], r=[], w=["lfn"])
                    MS("dve", carF, 0.0, ["carF"])
                else:
                    MS("dve", carry, 0.0, ["carry"])
                ntiles = NP + 1
                for ti in range(ntiles):
                    a = ti % 2
                    first, last = ti == 0, ti == ntiles - 1
                    new = ti == 0
                    ns = LS if new else 128
                    pz = rps()
                    if new:
                        for pr in range(4):
                            MM(ps[pz][0:ns, 0:64], kTn[kind][:, pr, :], Q[:, pr, :], pr == 0, pr == 3,
                               r=[f"kTn{kind}", f"Q{kind}"], w=[psk[pz]])
                        V = Vn[kind]
                        vkey = f"Vn{kind}"
                    else:
                        pg = NP - ti
                        ic = sq_ * NP + pg
                        S.op("pool", lambda e, a=a, ic=ic, ck=ck: e.indirect_dma_start(
                            out=kpg[a], out_offset=None, in_=ck,
                            in_offset=bass.IndirectOffsetOnAxis(ap=idx[:, ic:ic + 1], axis=0)),
                            r=KS, w=[f"kpg{a}"], dma=True)
                        S.op("pool", lambda e, a=a, ic=ic, cv=cv: e.indirect_dma_start(
                            out=vpg[a], out_offset=None, in_=cv,
                            in_offset=bass.IndirectOffsetOnAxis(ap=idx[:, ic:ic + 1], axis=0)),
                            r=KS, w=[f"vpg{a}"], dma=True)
                        ptr = rps()
                        for pr in range(4):
                            TR(ps[ptr][:, pr * 128:(pr + 1) * 128], kpg[a][:, pr * 128:(pr + 1) * 128], ident[:],
                               r=[f"kpg{a}", "ident"], w=[psk[ptr]])
                        ACT(kTp[a].rearrange("p a b -> p (a b)"), ps[ptr][:, 0:512], AF.Copy, r=[psk[ptr]], w=[f"kTp{a}"])
                        for pr in range(4):
                            MM(ps[pz][:, 0:64], kTp[a][:, pr, :], Q[:, pr, :], pr == 0, pr == 3,
                               r=[f"kTp{a}", f"Q{kind}"], w=[psk[pz]])
                        V = vpg[a]
                        vkey = f"vpg{a}"
                    z3 = ps[pz][0:ns, 0:64].rearrange("p (h t) -> p h t", h=8)
                    if kind == 0:
                        if new:
                            pb = rps()
                            MM(ps[pb][0:LS, 0:8], triLE_f[0:LS, 0:LS], lfn[0:LS, :], True, True, r=["lfn", "triLE_f"], w=[psk[pb]])
                            TS("dve", Fb[0:LS, :], ps[pb][0:LS, 0:8], -1.0, ALU.mult, r=[psk[pb]], w=["Fb"])
                        else:
                            S.op("pool", lambda e, a=a, ic=ic: e.indirect_dma_start(
                                out=lfp[a], out_offset=None, in_=c_lf,
                                in_offset=bass.IndirectOffsetOnAxis(ap=idx[:, ic:ic + 1], axis=0)),
                                r=KS, w=[f"lfp{a}"], dma=True)
                            pb, pb2 = rps(), rps()
                            MM(ps[pb][:, 0:8], triS_f[:], lfp[a], True, True, r=[f"lfp{a}", "triS_f"], w=[psk[pb]])
                            MM(ps[pb2][:, 0:8], ones_f[:], lfp[a], True, True, r=[f"lfp{a}", "ones_f"], w=[psk[pb2]])
                            TT("dve", Fb, ps[pb][:, 0:8], carF, ALU.add, r=[psk[pb], "carF"], w=["Fb"])
                            TT("dve", carF, carF, ps[pb2][:, 0:8], ALU.add, r=[psk[pb2], "carF"], w=["carF"])
                        zz = zb[a][0:ns, :].rearrange("p (h t) -> p h t", h=8)
                        TT("dve", zz, z3, Fb[0:ns, :].unsqueeze(2).to_broadcast([ns, 8, LS]), ALU.add,
                           r=[psk[pz], "Fb"], w=[f"zb{a}"])
                        P = Pw[a]
                        ACT(P[0:ns, :], zb[a][0:ns, :], AF.Exp, r=[f"zb{a}"], w=[f"Pw{a}"])
                        if new:
                            P3 = P[0:ns, :].rearrange("p (h t) -> p h t", h=8)
                            TT("dve", P3, P3, triLE_f[0:LS, 0:LS].unsqueeze(1).to_broadcast([LS, 8, LS]), ALU.mult,
                               r=[f"Pw{a}", "triLE_f"], w=[f"Pw{a}"])
                        MM(ps[po][0:64, 0:512], P[0:ns, :], V[0:ns, :], first, last, r=[f"Pw{a}", vkey], w=[psk[po]])
                        MM(ps[pd][0:64, 0:2], P[0:ns, :], ones_f[0:ns, 0:2], first, last, r=[f"Pw{a}", "ones_f"], w=[psk[pd]])
                    else:
                        zz = zb[a][0:ns, :].rearrange("p (h t) -> p h t", h=8)
                        TT("dve", zz, z3, bsb[0:ns, :].unsqueeze(2).to_broadcast([ns, 8, LS]), ALU.add,
                           r=[psk[pz], "bsb"], w=[f"zb{a}"])
                        ACT(e_[a][0:ns, :], zb[a][0:ns, :], AF.Exp, r=[f"zb{a}"], w=[f"e{a}"])
                        ACT(sp_[a][0:ns, :], e_[a][0:ns, :], AF.Ln, r=[f"e{a}", "oneb"], w=[f"sp{a}"], bias=oneb[0:ns, 0:1])
                        if new:
                            s3 = sp_[a][0:ns, :].rearrange("p (h t) -> p h t", h=8)
                            TT("dve", s3, s3, mLTn[0:LS, :].unsqueeze(1).to_broadcast([LS, 8, LS]), ALU.mult,
                               r=[f"sp{a}"] + KS, w=[f"sp{a}"])
                        pc_, pt_ = rps(), rps()
                        MM(ps[pc_][0:ns, 0:64], triI_f[0:ns, 0:ns], sp_[a][0:ns, :], True, True, r=["triI_f", f"sp{a}"], w=[psk[pc_]])
                        MM(ps[pt_][:, 0:64], ones_f[0:ns, :], sp_[a][0:ns, :], True, True, r=["ones_f", f"sp{a}"], w=[psk[pt_]])
                        TT("dve", zb[a][0:ns, :], zb[a][0:ns, :], carry[0:ns, :], ALU.subtract, r=[f"zb{a}", "carry"], w=[f"zb{a}"])
                        TT("dve", zb[a][0:ns, :], zb[a][0:ns, :], ps[pc_][0:ns, 0:64], ALU.subtract, r=[f"zb{a}", psk[pc_]], w=[f"zb{a}"])
                        P = Pw[a]
                        ACT(P[0:ns, :], zb[a][0:ns, :], AF.Exp, r=[f"zb{a}"], w=[f"Pw{a}"])
                        if new:
                            P3 = P[0:ns, :].rearrange("p (h t) -> p h t", h=8)
                            TT("dve", P3, P3, mLTn[0:LS, :].unsqueeze(1).to_broadcast([LS, 8, LS]), ALU.mult,
                               r=[f"Pw{a}"] + KS, w=[f"Pw{a}"])
                        TT("dve", carry, carry, ps[pt_][:, 0:64], ALU.add, r=["carry", psk[pt_]], w=["carry"])
                        MM(ps[po][0:64, 0:512], P[0:ns, :], V[0:ns, :], first, last, r=[f"Pw{a}", vkey], w=[psk[po]])
                t3 = tmpo[0:64].rearrange("p a b -> p (a b)")
                TT("dve", tmpo[0:64], ps[po][0:64, 0:512].rearrange("p (h d) -> p h d", h=8),
                   bm[0:64, :].unsqueeze(2).to_broadcast([64, 8, 64]), ALU.mult, r=[psk[po]] + KS, w=["tmpo"])
                S.op("dve", lambda e: e.reduce_sum(out=red[0:64, :], in_=tmpo[0:64].rearrange("p h d -> p d h"), axis=AX.X),
                     r=["tmpo"], w=["red"])
                if kind == 0:
                    S.op("dve", lambda e: e.reciprocal(out=rd[0:64, :], in_=ps[pd][0:64, 0:2]), r=[psk[pd]], w=["rd"])
                    TS("dve", o16[0:64, :], red[0:64, :], rd[0:64, 0:1], ALU.mult, r=["red", "rd"], w=["o16"])
                else:
                    CP("dve", o16[0:64, :], red[0:64, :], r=["red"], w=["o16"])
                for hd in range(8):
                    r0 = kind * 512 + hd * 64
                    DMAX("sp", mixed[r0:r0 + 64, col:col + LS].rearrange("d t -> t d"), o16[hd * LS:(hd + 1) * LS, :],
                         r=["o16"], w=["mixedall"])

    ctx = Ctx()
    ctx.__dict__.update(locals())
    return ctx


def program(T, PAST, mixers=True, NPHYS=2560):
    c = build(T, PAST, NPHYS)
    c.ada_phase()
    for l in range(2):
        if l == 0:
            c.pass_mod(c.xT, 0, 0, store_x=True)
        c.pass_ffn_in(l, 0)
        c.pass_out(c.hid, "hid", DFF, c.ffn_w_out[l, 0], l, 2, True, (l, 3))
        if mixers and l == 0:
            c.proj0()
            c.s5_phase()
            c.hgrn_phase()
            c.token_bufs()
            c.pass_out(c.mixed, "mixed", D, c.ab_w_out, l, 5, False, (l, 6))
        elif mixers and l == 1:
            c.proj1()
            if STAGE >= 2:
                c.attn_phase()
            if STAGE >= 3:
                c.attn_sample_phase()
            c.token_bufs()
            c.pass_out(c.mixed, "mixed", D, c.cd_w_out, l, 5, False, (l, 6))
        else:
            c.pass_mod(c.xs, l, 6, store_x=False)
        c.pass_ffn_in(l, 1)
        c.pass_out(c.hid, "hid", DFF, c.ffn_w_out[l, 1], l, 8, True, (1, 0) if l == 0 else "final")
    c.S.emit()
    return c


def host_inputs(inp, T, ncores=8, layers=2):
    maps = []
    xp = inp["x_prompt"]
    f32 = np.float32
    a_re, a_im, ld = inp["s5_a_re"], inp["s5_a_im"], inp["s5_log_dt"]
    s5_st = np.stack([a_re.reshape(16, 128).T, a_im.reshape(16, 128).T,
                      np.repeat(ld.reshape(16, 2, 1), 64, axis=2).reshape(16, 128).T]).astype(f32)
    s5_ch = np.zeros((3, 4, 128, 512), f32)
    s5_bblk = np.zeros((2, 4, 128, 512), f32)
    for c in range(4):
        s5_ch[0, c] = a_re[8 * c:8 * c + 8].reshape(1, 512)
        s5_ch[1, c] = a_im[8 * c:8 * c + 8].reshape(1, 512)
        s5_ch[2, c] = np.repeat(ld[8 * c:8 * c + 8], 64).reshape(1, 512)
        for ri, b in enumerate((inp["s5_b_re"], inp["s5_b_im"])):
            blk = np.zeros((8, 16, 8, 64), f32)
            for g in range(8):
                blk[g, :, g, :] = b[8 * c + g].T
            s5_bblk[ri, c] = blk.reshape(128, 512)
    s5_cblk = np.zeros((2, 16, 128, 128), f32)
    for j in range(16):
        for ri, cc in enumerate((inp["s5_c_re"], inp["s5_c_im"])):
            blk = np.zeros((2, 64, 8, 16), f32)
            for gg in range(2):
                blk[gg, :, 2 * (j % 4) + gg, :] = cc[2 * j + gg].T
            s5_cblk[ri, j] = blk.reshape(128, 128)
    common = {
        "ada_w": inp["ada_w"],
        "ada_bT": np.ascontiguousarray(inp["ada_b"].reshape(2, NADA * KT, 128).transpose(0, 2, 1)),
        "ffn_w_in": inp["ffn_w_in"], "ffn_w_out": inp["ffn_w_out"],
        "fnw": np.ascontiguousarray(inp["final_norm_w"].reshape(KT, 128).T),
        "ab_w_in": inp["ab_w_in"], "ab_w_out": inp["ab_w_out"],
        "s5_st": s5_st, "s5_ch": s5_ch, "s5_bblk": s5_bblk, "s5_cblk": s5_cblk,
        "s5_dfm": np.ascontiguousarray(inp["s5_d"].reshape(4, 128).T),
        "glu_w": inp["s5_glu_w"], "glu_bfm": np.ascontiguousarray(inp["s5_glu_b"].reshape(4, 128).T),
        "lbl_fm": np.ascontiguousarray(inp["hgrn_lb_logits"].reshape(3, 4, 128).transpose(2, 1, 0)),
        "hnw": np.ascontiguousarray(inp["hgrn_norm_w"].reshape(128, 1)),
        "cd_w_in": inp["cd_w_in"], "cd_w_out": inp["cd_w_out"],
        "bfb_in": np.ascontiguousarray(np.broadcast_to(inp["cd_b_f"].reshape(1, 8), (128, 8))).astype(f32),
        "bsb_in": np.ascontiguousarray(np.broadcast_to(inp["cd_b_sb"].reshape(1, 8), (128, 8))).astype(f32),
        "c_fk": inp["cache_fox_k"].reshape(-1, 512), "c_fv": inp["cache_fox_v"].reshape(-1, 512),
        "c_lf": inp["cache_fox_logf"].reshape(-1, 8),
        "c_sk": inp["cache_sb_k"].reshape(-1, 512), "c_sv": inp["cache_sb_v"].reshape(-1, 512),
    }
    for c in range(ncores):
        b = c % xp.shape[0]
        sl = slice(NS * c, NS * (c + 1))
        xs_ = inp["x_sample"][sl].reshape(NSTOK, D)
        xT = np.ascontiguousarray(np.concatenate([xp[b, :T], xs_], axis=0).T)
        cT = np.ascontiguousarray(np.concatenate([inp["c_prompt"][b:b + 1], inp["c_sample"][sl]], axis=0).T)
        s5s = np.stack([inp["state_s5_re"][sl].reshape(NS, 16, 128).transpose(2, 1, 0),
                        inp["state_s5_im"][sl].reshape(NS, 16, 128).transpose(2, 1, 0)], axis=1)
        m = dict(common)
        m.update({
            "xT": xT, "cT": cT,
            "s5s_in": np.ascontiguousarray(s5s.astype(f32)),
            "hgs_in": np.ascontiguousarray(inp["state_hgrn"][sl].reshape(NS * 4, 128, 128)),
            "pt_rep": np.ascontiguousarray(np.broadcast_to(inp["page_table"][sl].reshape(1, -1), (128, NS * inp["page_table"].shape[1]))).astype(np.int32),
        })
        maps.append(m)
    return maps


def assemble(res, T, nb=2):
    R_ = res
    ncores = len(R_)
    def prm(f):
        return np.stack([f(R_[b]) for b in range(nb)])
    def smp(f):
        return np.concatenate([f(R_[c]) for c in range(ncores)], axis=0)
    y_p = prm(lambda r: r["yT"][:, :T].T)
    y_s = smp(lambda r: r["yT"][:, T:].T.reshape(NS, LS, D))
    s5r_p = prm(lambda r: r["s5p_out"][:, 0, :].T.reshape(32, 64))
    s5i_p = prm(lambda r: r["s5p_out"][:, 1, :].T.reshape(32, 64))
    hg_p = prm(lambda r: r["hgp_out"])
    fk_p = prm(lambda r: r["qk1"][512:1024, :T].T.reshape(T, 8, 64))
    fv_p = prm(lambda r: r["fv_out"][:T].reshape(T, 8, 64))
    flf_p = prm(lambda r: r["logf_out"][:T])
    sk_p = prm(lambda r: r["qk1"][1536:2048, :T].T.reshape(T, 8, 64))
    sv_p = prm(lambda r: r["sv_out"][:T].reshape(T, 8, 64))
    s5r_s = smp(lambda r: r["s5s_out"][:, 0].transpose(2, 1, 0).reshape(NS, 32, 64))
    s5i_s = smp(lambda r: r["s5s_out"][:, 1].transpose(2, 1, 0).reshape(NS, 32, 64))
    hg_s = smp(lambda r: r["hgs_out"].reshape(NS, 4, 128, 128))
    fk_s = smp(lambda r: r["qk1"][512:1024, T:].T.reshape(NS, LS, 8, 64))
    fv_s = smp(lambda r: r["fv_out"][T:].reshape(NS, LS, 8, 64))
    flf_s = smp(lambda r: r["logf_out"][T:].reshape(NS, LS, 8))
    sk_s = smp(lambda r: r["qk1"][1536:2048, T:].T.reshape(NS, LS, 8, 64))
    sv_s = smp(lambda r: r["sv_out"][T:].reshape(NS, LS, 8, 64))
    outs = (y_p, y_s, s5r_p, s5i_p, hg_p, fk_p, fv_p, flf_p, sk_p, sv_p,
            s5r_s, s5i_s, hg_s, fk_s, fv_s, flf_s, sk_s, sv_s)
    return tuple(np.ascontiguousarray(o, dtype=np.float32) for o in outs)


def kernel(**inputs):
    inp = {k: np.asarray(v) for k, v in inputs.items()}
    T = inp["x_prompt"].shape[1]
    PAST = inp["page_table"].shape[1] * 128
    NPHYS = inp["cache_fox_k"].shape[0]
    c = program(T, PAST, True, NPHYS)
    maps = host_inputs(inp, T)
    res = run_bass_kernel_spmd(c.nc, maps, core_ids=list(range(8)))
    return assemble(res.results, T, inp["x_prompt"].shape[0])
```

```python
import contextlib
import numpy as np
import concourse.bass as bass
import concourse.mybir as mybir
from concourse.bass_utils import run_bass_kernel_spmd

F32 = mybir.dt.float32
BF16 = mybir.dt.bfloat16
I32 = mybir.dt.int32
AF = mybir.ActivationFunctionType
ALU = mybir.AluOpType
AX = mybir.AxisListType

D = 1024
KT = 8
DFF = 2816
HT = 22
NADA = 9
EPS = 1e-6
NS = 4
LS = 8
NSTOK = NS * LS


class Sched:
    EPOCH = 8000
    NDMA = 32

    def __init__(self, nc, es):
        self.nc = nc
        self.es = es
        self.engs = {"pe": nc.tensor, "act": nc.scalar, "dve": nc.vector, "pool": nc.gpsimd, "sp": nc.sync}
        self.q = {e: [] for e in self.engs}
        self.cnt = {e: 0 for e in self.engs}
        self.esems = {e: [] for e in self.engs}
        self.dsems = [es.enter_context(nc.semaphore(f"dma{i}")) for i in range(2 * self.NDMA)]
        self.duse = [0] * (2 * self.NDMA)
        self.drr = {"sp": 0, "pool": 0}
        self.lastw = {}
        self.readers = {}
        self.seen = {e: {} for e in self.engs}
        self.nops = 0
        self.pbar = {e: [] for e in self.engs}
        self.lasttok = {}

    def barrier(self):
        toks = list(self.lasttok.values())
        for i, s_ in enumerate(self.dsems):
            if self.duse[i]:
                toks.append((s_, self.duse[i] * 16, "dma"))
        for e in self.engs:
            self.pbar[e] = list(toks)

    def _esem(self, e, epoch):
        while len(self.esems[e]) <= epoch:
            self.esems[e].append(self.es.enter_context(self.nc.semaphore(f"s_{e}{len(self.esems[e])}")))
        return self.esems[e][epoch]

    def op(self, e, fn, r=(), w=(), dma=False):
        r = list(r)
        w = list(w)
        for k in list(r):
            if len(k) == 3 and k.startswith("ps") and k[2].isdigit():
                r.remove(k)
                if k not in w:
                    w.append(k)
        deps = []
        for k in list(r) + list(w):
            t = self.lastw.get(k)
            if t is not None:
                deps.append(t)
        for k in w:
            deps.extend(self.readers.get(k, ()))
        if self.pbar[e]:
            deps.extend(self.pbar[e])
            self.pbar[e] = []
        waits = []
        seen = self.seen[e]

        def need(tok):
            sem, val, eng = tok
            if eng == "pe" and e == "pe" and not dma:
                return
            if seen.get(id(sem), 0) >= val:
                return
            seen[id(sem)] = val
            waits.append((sem, val))

        for t in deps:
            need(t)
        if dma:
            qn = "pool" if e == "pool" else "sp"
            i = self.drr[qn] + (self.NDMA if qn == "pool" else 0)
            self.drr[qn] = (self.drr[qn] + 1) % self.NDMA
            pv = self.duse[i] * 16
            if pv > 0:
                need((self.dsems[i], pv, "dma"))
            self.duse[i] += 1
            tok = (self.dsems[i], pv + 16, "dma")
            inc = (self.dsems[i], 16)
        else:
            c = self.cnt[e]
            self.cnt[e] += 1
            ep, v = divmod(c, self.EPOCH)
            sem = self._esem(e, ep)
            tok = (sem, v + 1, e)
            inc = (sem, 1)
        self.q[e].append((waits, fn, inc))
        if not dma:
            self.lasttok[e] = tok
        for k in w:
            self.lastw[k] = tok
            self.readers[k] = []
        for k in r:
            self.readers.setdefault(k, []).append(tok)
        self.nops += 1
        return tok

    def emit(self):
        nc = self.nc
        fin = []
        for i, s in enumerate(self.dsems):
            if self.duse[i]:
                fin.append((s, self.duse[i] * 16))
        for e in self.engs:
            c = self.cnt[e]
            if c:
                ep, v = divmod(c - 1, self.EPOCH)
                fin.append((self.esems[e][ep], v + 1))
        qs = self.q

        def replay(e, eng):
            for waits, fn, inc in qs[e]:
                for sem, val in waits:
                    eng.wait_ge(sem, val)
                ins = fn(eng)
                ins.then_inc(inc[0], inc[1])

        with nc.Block() as block:
            @block.tensor
            def _(eng):
                replay("pe", eng)

            @block.scalar
            def _(eng):
                replay("act", eng)

            @block.vector
            def _(eng):
                replay("dve", eng)

            @block.gpsimd
            def _(eng):
                replay("pool", eng)

            @block.sync
            def _(eng):
                replay("sp", eng)
                for sem, val in fin:
                    eng.wait_ge(sem, val)


class Ctx:
    pass


DEBUG = False
DBG5 = False
ASTOP = 99
SBCUT = 99
STAGE = 99


def build(T, PAST, NPHYS=2560):
    nc = bass.Bass("TRN2", target_bir_lowering=False)
    es = contextlib.ExitStack()
    S = Sched(nc, es)
    NB = T // 512
    NTOK = T + NSTOK
    blocks = [(i * 512, 512) for i in range(NB)] + [(T, NSTOK)]

    def dram_in(name, shape, dt=F32):
        return nc.dram_tensor(name, list(shape), dt, kind="ExternalInput").ap()

    def dram_out(name, shape, dt=F32):
        return nc.dram_tensor(name, list(shape), dt, kind="ExternalOutput").ap()

    def dram_tmp(name, shape, dt):
        if DEBUG:
            return nc.dram_tensor(name, list(shape), dt, kind="ExternalOutput").ap()
        return nc.dram_tensor(name, list(shape), dt).ap()

    def sb(name, shape, dt=F32):
        return es.enter_context(nc.sbuf_tensor(name, list(shape), dt))

    xT = dram_in("xT", [D, NTOK])
    cT = dram_in("cT", [D, 1 + NS])
    ada_w = dram_in("ada_w", [2, D, NADA * D])
    ada_bT = dram_in("ada_bT", [2, 128, NADA * KT])
    ffn_w_in = dram_in("ffn_w_in", [2, 2, D, 2 * DFF])
    ffn_w_out = dram_in("ffn_w_out", [2, 2, DFF, D])
    fnw = dram_in("fnw", [128, KT])
    yT = dram_out("yT", [D, NTOK])
    xs = dram_tmp("xs", [D, NTOK], F32)
    hbuf = dram_tmp("hbuf", [D, NTOK], BF16)
    hid = dram_tmp("hid", [DFF, NTOK], BF16)

    ps = [es.enter_context(nc.psum_tensor(f"ps{i}", [128, 512], F32)) for i in range(8)]
    psk = [f"ps{i}" for i in range(8)]
    psrr = [0]

    def nextps():
        i = psrr[0]
        psrr[0] = (i + 1) % 8
        return i

    ones_bf = sb("ones_bf", [128, 128], BF16)
    S.op("pool", lambda e: e.memset(ones_bf[:], 1.0), w=["ones_bf"])

    ARENA_W = 47104
    arena = sb("arena", [128, ARENA_W], F32)
    apos = [0]

    def aalloc(shape, dt=F32):
        n = int(np.prod(shape))
        words = n if dt == F32 or dt == I32 else (n + 1) // 2
        words = (words + 7) // 8 * 8
        a = apos[0]
        assert a + words <= ARENA_W, ("arena overflow", a, words)
        apos[0] = a + words
        v = arena[:, a:a + words]
        if dt != F32:
            v = v.bitcast(dt)
        v = v[:, 0:n]
        if len(shape) == 2:
            v = v.rearrange("p (a b) -> p a b", a=shape[0])
        elif len(shape) == 3:
            v = v.rearrange("p (a b c) -> p a b c", a=shape[0], b=shape[1])
        return v

    def areset():
        S.barrier()
        apos[0] = 0

    def token_bufs():
        areset()
        g = {}
        g["wbig"] = aalloc([25600], BF16)
        g["xb"] = [aalloc([KT, 512], F32) for i in range(2)]
        g["hb"] = [aalloc([HT, 512], BF16) for i in range(2)]
        g["ob"] = [aalloc([HT, 512], BF16)] * 2
        g["sq"] = aalloc([KT, 512], BF16)
        g["t1"] = [aalloc([512], F32) for i in range(2)]
        g["t2"] = [aalloc([512], F32) for i in range(2)]
        g["rstd"] = aalloc([512], F32)
        return g

    tb = token_bufs()
    wbig, xb, hb, ob, sq, t1, t2, rstd = (tb[k] for k in ["wbig", "xb", "hb", "ob", "sq", "t1", "t2", "rstd"])
    modv = [sb(f"modv{l}", [128, NADA * KT, 1 + NS], F32) for l in range(2)]
    s1v = [sb(f"s1v{l}", [128, NADA * KT, 1 + NS], F32) for l in range(2)]
    ghv = [sb(f"ghv{l}", [128, NADA * KT, 1 + NS], F32) for l in range(2)]
    modS = sb("modS", [128, 3, KT, NSTOK], F32)
    fnw_sb = sb("fnw_sb", [128, KT], F32)
    S.op("sp", lambda e: e.dma_start(out=fnw_sb[:], in_=fnw[:, :]), w=["fnw_sb"], dma=True)

    def ada_phase():
        csb = sb("csb", [128, KT, 1 + NS], F32)
        scs = sb("scs", [128, KT, 1 + NS], F32)
        adab = sb("adab", [128, NADA * KT], F32)
        S.op("sp", lambda e: e.dma_start(out=csb[:], in_=cT.rearrange("(k p) j -> p k j", p=128)), w=["csb"], dma=True)
        S.op("act", lambda e: e.activation(out=scs[:], in_=csb[:], func=AF.Silu), r=["csb"], w=["scs"])
        wst = wbig.bitcast(F32).rearrange("p (s k c) -> p s k c", s=2, k=KT)
        CW = 768
        ci = 0
        for l in range(2):
            S.op("sp", lambda e, l=l: e.dma_start(out=adab[:], in_=ada_bT[l]), w=["adab"], dma=True)
            for ch in range(NADA * D // CW):
                slot = ci % 2
                ci += 1
                S.op("sp", lambda e, l=l, ch=ch, slot=slot: e.dma_start(
                    out=wst[:, slot, :, 0:CW],
                    in_=ada_w[l, :, ch * CW:(ch + 1) * CW].rearrange("(k p) c -> p k c", p=128)),
                    w=[f"wst{slot}"], dma=True)
                for fl in range(CW // 128):
                    ft = ch * (CW // 128) + fl
                    pi = nextps()
                    for kt in range(KT):
                        S.op("pe", lambda e, pi=pi, slot=slot, kt=kt, fl=fl: e.matmul(
                            ps[pi][:, 0:1 + NS], lhsT=wst[:, slot, kt, fl * 128:(fl + 1) * 128],
                            rhs=scs[:, kt, :], start=(kt == 0), stop=(kt == KT - 1)),
                            r=[f"wst{slot}", "scs"], w=[psk[pi]])
                    S.op("dve", lambda e, pi=pi, ft=ft, l=l: e.tensor_scalar(
                        out=modv[l][:, ft, :], in0=ps[pi][:, 0:1 + NS], scalar1=adab[:, ft:ft + 1], scalar2=None,
                        op0=ALU.add), r=[psk[pi], "adab"], w=[f"modv{l}"])
            S.op("dve", lambda e, l=l: e.tensor_scalar(out=s1v[l][:], in0=modv[l][:], scalar1=1.0, scalar2=None,
                                                       op0=ALU.add), r=[f"modv{l}"], w=[f"s1v{l}"])
            S.op("dve", lambda e, l=l: e.tensor_scalar(out=ghv[l][:], in0=modv[l][:], scalar1=0.5, scalar2=None,
                                                       op0=ALU.mult), r=[f"modv{l}"], w=[f"ghv{l}"])

    def set_modS(which, src, chunk):
        for j in range(NS):
            S.op("dve", lambda e, j=j: e.tensor_copy(
                out=modS[:, which, :, j * LS:(j + 1) * LS],
                in_=src[:, chunk * KT:(chunk + 1) * KT, 1 + j:2 + j].to_broadcast([128, KT, LS])),
                r=[src.name if hasattr(src, "name") else "modsrc"], w=["modS"])

    def load_w(w_ap, K, cols, key="wbig"):
        kt_n = K // 128
        F = sum(c1 - c0 for c0, c1 in cols)
        view = wbig[:, 0:kt_n * F].rearrange("p (k f) -> p k f", k=kt_n)
        for kt in range(kt_n):
            o = 0
            for (c0, c1) in cols:
                for f0 in range(c0, c1, 2048):
                    f1 = min(c1, f0 + 2048)
                    S.op("pool", lambda e, kt=kt, f0=f0, f1=f1, o=o: e.dma_start(
                        out=view[:, kt, o:o + f1 - f0], in_=w_ap[kt * 128:(kt + 1) * 128, f0:f1]),
                        w=[key, "wst0", "wst1"], dma=True)
                    o += f1 - f0
        return view

    def norm_mod_block(xblk, xkey, bi, l, chunk, final=False):
        c0, n = blocks[bi]
        slot = bi % 2
        S.op("act", lambda e: e.activation(out=sq[:, :, 0:n], in_=xblk[:, :, 0:n], func=AF.Square),
             r=[xkey], w=["sq"])
        pi = nextps()
        for kt in range(KT):
            S.op("pe", lambda e, kt=kt: e.matmul(ps[pi][:, 0:n], lhsT=ones_bf[:], rhs=sq[:, kt, 0:n],
                                                 start=(kt == 0), stop=(kt == KT - 1)),
                 r=["ones_bf", "sq"], w=[psk[pi]])
        S.op("act", lambda e: e.activation(out=rstd[:, 0:n], in_=ps[pi][:, 0:n], func=AF.Sqrt, scale=1.0 / D,
                                           bias=epsb[:, 0:1]), r=[psk[pi], "epsb"], w=["rstd"])
        S.op("dve", lambda e: e.reciprocal(out=rstd[:, 0:n], in_=rstd[:, 0:n]), r=["rstd"], w=["rstd"])
        if final:
            for kt in range(KT):
                S.op("dve", lambda e, kt=kt: e.scalar_tensor_tensor(
                    out=xblk[:, kt, 0:n], in0=xblk[:, kt, 0:n], scalar=fnw_sb[:, kt:kt + 1], in1=rstd[:, 0:n],
                    op0=ALU.mult, op1=ALU.mult), r=[xkey, "rstd", "fnw_sb"], w=[xkey])
            S.op("pool", lambda e: e.dma_start(out=yT[:, c0:c0 + n].rearrange("(k p) t -> p k t", p=128),
                                               in_=xblk[:, :, 0:n]), r=[xkey], w=["yT"], dma=True)
            return
        o = ob[slot]
        okey = "ob0"
        prompt = n == 512
        for kt in range(KT):
            tt = t1[kt % 2]
            S.op("dve", lambda e, kt=kt, tt=tt: e.tensor_tensor(out=tt[:, 0:n], in0=xblk[:, kt, 0:n],
                                                                 in1=rstd[:, 0:n], op=ALU.mult),
                 r=[xkey, "rstd"], w=[f"t1_{kt % 2}"])
            ft = chunk * KT + kt
            if prompt:
                S.op("act", lambda e, kt=kt, tt=tt, ft=ft: e.activation(
                    out=o[:, kt, 0:n], in_=tt[:, 0:n], func=AF.Identity,
                    scale=s1v[l][:, ft + KT, 0:1], bias=modv[l][:, ft, 0:1]),
                    r=[f"t1_{kt % 2}", f"s1v{l}", f"modv{l}"], w=[okey])
            else:
                S.op("dve", lambda e, kt=kt, tt=tt: e.tensor_tensor(out=tt[:, 0:n], in0=tt[:, 0:n],
                                                                     in1=modS[:, 0, kt, :], op=ALU.mult),
                     r=[f"t1_{kt % 2}", "modS"], w=[f"t1_{kt % 2}"])
                S.op("dve", lambda e, kt=kt, tt=tt: e.tensor_tensor(out=o[:, kt, 0:n], in0=tt[:, 0:n],
                                                                     in1=modS[:, 1, kt, :], op=ALU.add),
                     r=[f"t1_{kt % 2}", "modS"], w=[okey])
        S.op("pool", lambda e: e.dma_start(out=hbuf[:, c0:c0 + n].rearrange("(k p) t -> p k t", p=128),
                                           in_=o[:, 0:KT, 0:n]), r=[okey], w=[f"hbuf{bi}"], dma=True)

    epsb = sb("epsb", [128, 1], F32)
    S.op("pool", lambda e: e.memset(epsb[:], EPS), w=["epsb"])

    def prep_modS(l, chunk):
        set_modS(0, s1v[l], chunk + 1)
        set_modS(1, modv[l], chunk)

    def pass_mod(src, l, chunk, store_x):
        prep_modS(l, chunk)
        for bi, (c0, n) in enumerate(blocks):
            slot = bi % 2
            S.op("sp", lambda e, c0=c0, n=n, slot=slot: e.dma_start(
                out=xb[slot][:, :, 0:n], in_=src[:, c0:c0 + n].rearrange("(k p) t -> p k t", p=128)),
                r=[f"{src.name}{bi}"], w=[f"xb{slot}"], dma=True)
            if store_x:
                S.op("pool", lambda e, c0=c0, n=n, slot=slot: e.dma_start(
                    out=xs[:, c0:c0 + n].rearrange("(k p) t -> p k t", p=128), in_=xb[slot][:, :, 0:n]),
                    r=[f"xb{slot}"], w=[f"xs{bi}"], dma=True)
            norm_mod_block(xb[slot], f"xb{slot}", bi, l, chunk)

    def pass_ffn_in(l, j):
      HH = HT // 2
      HW = HH * 128
      for hh in range(2):
        wv = load_w(ffn_w_in[l, j], D, [(hh * HW, (hh + 1) * HW), (DFF + hh * HW, DFF + (hh + 1) * HW)])
        for bi, (c0, n) in enumerate(blocks):
            slot = bi % 2
            h = hb[slot]
            S.op("sp", lambda e, c0=c0, n=n, h=h: e.dma_start(
                out=h[:, 0:KT, 0:n], in_=hbuf[:, c0:c0 + n].rearrange("(k p) t -> p k t", p=128)),
                r=[f"hbuf{bi}"], w=[f"hb{slot}"], dma=True)
            o = ob[slot]
            for i in range(HH):
                pg = nextps()
                pu = nextps()
                for kt in range(KT):
                    S.op("pe", lambda e, kt=kt, pg=pg, i=i, h=h, n=n: e.matmul(
                        ps[pg][:, 0:n], lhsT=wv[:, kt, i * 128:(i + 1) * 128], rhs=h[:, kt, 0:n],
                        start=(kt == 0), stop=(kt == KT - 1)), r=["wbig", f"hb{slot}"], w=[psk[pg]])
                for kt in range(KT):
                    S.op("pe", lambda e, kt=kt, pu=pu, i=i, h=h, n=n: e.matmul(
                        ps[pu][:, 0:n], lhsT=wv[:, kt, HW + i * 128:HW + (i + 1) * 128], rhs=h[:, kt, 0:n],
                        start=(kt == 0), stop=(kt == KT - 1)), r=["wbig", f"hb{slot}"], w=[psk[pu]])
                tt = t2[i % 2]
                S.op("act", lambda e, pg=pg, tt=tt, n=n: e.activation(out=tt[:, 0:n], in_=ps[pg][:, 0:n], func=AF.Silu),
                     r=[psk[pg]], w=[f"t2_{i % 2}"])
                S.op("dve", lambda e, pu=pu, tt=tt, i=i, o=o, n=n: e.tensor_tensor(
                    out=o[:, i, 0:n], in0=tt[:, 0:n], in1=ps[pu][:, 0:n], op=ALU.mult),
                    r=[psk[pu], f"t2_{i % 2}"], w=["ob0"])
            S.op("pool", lambda e, c0=c0, n=n, o=o, hh=hh: e.dma_start(
                out=hid[hh * HW:(hh + 1) * HW, c0:c0 + n].rearrange("(k p) t -> p k t", p=128), in_=o[:, 0:HH, 0:n]),
                r=["ob0"], w=[f"hid{bi}"], dma=True)

    def pass_out(src, srckey, K, w_ap, l, gchunk, half, nxt):
        ktn = K // 128
        wv = load_w(w_ap, K, [(0, D)])
        gsrc = ghv[l] if half else modv[l]
        set_modS(2, gsrc, gchunk)
        if nxt not in (None, "final"):
            prep_modS(nxt[0], nxt[1])
        for bi, (c0, n) in enumerate(blocks):
            slot = bi % 2
            h = hb[slot]
            S.op("sp", lambda e, c0=c0, n=n, h=h: e.dma_start(
                out=h[:, 0:ktn, 0:n], in_=src[:, c0:c0 + n].rearrange("(k p) t -> p k t", p=128)),
                r=[f"{srckey}{bi}"], w=[f"hb{slot}"], dma=True)
            S.op("sp", lambda e, c0=c0, n=n, slot=slot: e.dma_start(
                out=xb[slot][:, :, 0:n], in_=xs[:, c0:c0 + n].rearrange("(k p) t -> p k t", p=128)),
                r=[f"xs{bi}"], w=[f"xb{slot}"], dma=True)
            x = xb[slot]
            for ft in range(KT):
                pi = nextps()
                for kt in range(ktn):
                    S.op("pe", lambda e, kt=kt, pi=pi, ft=ft, h=h, n=n: e.matmul(
                        ps[pi][:, 0:n], lhsT=wv[:, kt, ft * 128:(ft + 1) * 128], rhs=h[:, kt, 0:n],
                        start=(kt == 0), stop=(kt == ktn - 1)), r=["wbig", f"hb{slot}"], w=[psk[pi]])
                if n == 512:
                    gi = gchunk * KT + ft
                    S.op("dve", lambda e, pi=pi, ft=ft, x=x, gi=gi, n=n: e.scalar_tensor_tensor(
                        out=x[:, ft, 0:n], in0=ps[pi][:, 0:n], scalar=gsrc[:, gi:gi + 1, 0:1].rearrange("p a b -> p (a b)"),
                        in1=x[:, ft, 0:n], op0=ALU.mult, op1=ALU.add),
                        r=[psk[pi], f"xb{slot}", f"modv{l}", f"ghv{l}"], w=[f"xb{slot}"])
                else:
                    tt = t1[ft % 2]
                    S.op("dve", lambda e, pi=pi, ft=ft, tt=tt, n=n: e.tensor_tensor(
                        out=tt[:, 0:n], in0=ps[pi][:, 0:n], in1=modS[:, 2, ft, :], op=ALU.mult),
                        r=[psk[pi], "modS"], w=[f"t1_{ft % 2}"])
                    S.op("dve", lambda e, ft=ft, tt=tt, x=x, n=n: e.tensor_tensor(
                        out=x[:, ft, 0:n], in0=x[:, ft, 0:n], in1=tt[:, 0:n], op=ALU.add),
                        r=[f"t1_{ft % 2}", f"xb{slot}"], w=[f"xb{slot}"])
            if nxt == "final":
                norm_mod_block(x, f"xb{slot}", bi, 0, 0, final=True)
            else:
                S.op("pool", lambda e, c0=c0, n=n, x=x: e.dma_start(
                    out=xs[:, c0:c0 + n].rearrange("(k p) t -> p k t", p=128), in_=x[:, :, 0:n]),
                    r=[f"xb{slot}"], w=[f"xs{bi}"], dma=True)
                if nxt is not None:
                    norm_mod_block(x, f"xb{slot}", bi, nxt[0], nxt[1])


    def TT(eng, out, a, b, op, r, w):
        S.op(eng, lambda e: e.tensor_tensor(out=out, in0=a, in1=b, op=op), r=r, w=w)

    def TS(eng, out, a, s1, op0, r, w, s2=None, op1=None):
        if op1 is None:
            S.op(eng, lambda e: e.tensor_scalar(out=out, in0=a, scalar1=s1, scalar2=None, op0=op0), r=r, w=w)
        else:
            S.op(eng, lambda e: e.tensor_scalar(out=out, in0=a, scalar1=s1, scalar2=s2, op0=op0, op1=op1), r=r, w=w)

    def STT(out, a, sc, b, op0, op1, r, w):
        S.op("dve", lambda e: e.scalar_tensor_tensor(out=out, in0=a, scalar=sc, in1=b, op0=op0, op1=op1), r=r, w=w)

    def ACT(out, in_, func, r, w, scale=1.0, bias=None):
        if bias is None:
            S.op("act", lambda e: e.activation(out=out, in_=in_, func=func, scale=scale), r=r, w=w)
        else:
            S.op("act", lambda e: e.activation(out=out, in_=in_, func=func, scale=scale, bias=bias), r=r, w=w)

    def MM(out, lhsT, rhs, start, stop, r, w):
        S.op("pe", lambda e: e.matmul(out, lhsT=lhsT, rhs=rhs, start=start, stop=stop), r=r, w=w)

    def TR(out, in_, ident, r, w):
        S.op("pe", lambda e: e.transpose(out, in_, ident), r=r, w=w)

    def DMA(eng, out, in_, r, w):
        S.op(eng, lambda e: e.dma_start(out=out, in_=in_), r=r, w=w, dma=True)

    def DMAX(eng, out, in_, r, w):
        S.op(eng, lambda e: e.dma_start(out=out, in_=in_, allow_slow_non_contiguous=True), r=r, w=w, dma=True)

    def CP(eng, out, in_, r, w):
        S.op(eng, lambda e: e.tensor_copy(out=out, in_=in_), r=r, w=w)

    def MS(eng, out, val, w):
        S.op(eng, lambda e: e.memset(out, val), w=w)

    ones_f = sb("ones_f", [128, 128], F32)
    MS("pool", ones_f[:], 1.0, ["ones_f"])
    ident = sb("ident", [128, 128], F32)
    S.op("pool", lambda e: e.affine_select(out=ident[:], in_=ones_f[:], pattern=[[-1, 128]], compare_op=ALU.is_equal,
                                           fill=0.0, base=0, channel_multiplier=1), r=["ones_f"], w=["ident"])
    halfpi = sb("halfpi", [128, 1], F32)
    MS("pool", halfpi[:], float(np.pi / 2), ["halfpi"])

    stg = [sb(f"stg{i}", [128, 512], F32) for i in range(4)]
    stgrr = [0]

    def evac(ps_ap, pkey, n, np_, scale=None):
        i = stgrr[0]
        stgrr[0] = (i + 1) % 4
        o = stg[i][0:np_, 0:n]
        if i % 2 == 0:
            ACT(o, ps_ap, AF.Copy, r=[pkey], w=[f"stg{i}"], scale=(1.0 if scale is None else scale))
        else:
            if scale is None:
                CP("dve", o, ps_ap, r=[pkey], w=[f"stg{i}"])
            else:
                TS("dve", o, ps_ap, float(scale), ALU.mult, r=[pkey], w=[f"stg{i}"])
        return o, f"stg{i}"

    def pass_proj(w_ap, F, fm, tm):
        wv = load_w(w_ap, D, [(0, F)])
        for bi, (c0, n) in enumerate(blocks):
            slot = bi % 2
            h = hb[slot]
            DMA("sp", h[:, 0:KT, 0:n], hbuf[:, c0:c0 + n].rearrange("(k p) t -> p k t", p=128),
                r=[f"hbuf{bi}"], w=[f"hb{slot}"])
            for (col0, ncols, dst, row0, scale) in fm:
                for ft in range(ncols // 128):
                    pi = nextps()
                    for kt in range(KT):
                        MM(ps[pi][:, 0:n], wv[:, kt, col0 + ft * 128:col0 + (ft + 1) * 128], h[:, kt, 0:n],
                           kt == 0, kt == KT - 1, r=["wbig", f"hb{slot}"], w=[psk[pi]])
                    o, ok = evac(ps[pi][:, 0:n], psk[pi], n, 128, scale)
                    for dst_ in (dst if isinstance(dst, (list, tuple)) else [dst]):
                        DMA("pool", dst_[row0 + ft * 128:row0 + (ft + 1) * 128, c0:c0 + n], o, r=[ok], w=[f"{dst_.name}{bi}"])
            for (col0, ncols, handler) in tm:
                for sub in range((n + 127) // 128):
                    nt = min(128, n - sub * 128)
                    pi = nextps()
                    for kt in range(KT):
                        MM(ps[pi][0:nt, 0:ncols], h[:, kt, sub * 128:sub * 128 + nt], wv[:, kt, col0:col0 + ncols],
                           kt == 0, kt == KT - 1, r=["wbig", f"hb{slot}"], w=[psk[pi]])
                    handler(pi, nt, c0 + sub * 128, bi)


    ab_w_in = dram_in("ab_w_in", [D, 2560])
    ab_w_out = dram_in("ab_w_out", [D, D])
    s5_st = dram_in("s5_st", [3, 128, 16])
    s5_ch = dram_in("s5_ch", [3, 4, 128, 512])
    s5_bblk = dram_in("s5_bblk", [2, 4, 128, 512])
    s5_cblk = dram_in("s5_cblk", [2, 16, 128, 128])
    s5_dfm = dram_in("s5_dfm", [128, 4])
    glu_w = dram_in("glu_w", [512, 512])
    glu_bfm = dram_in("glu_bfm", [128, 4])
    lbl_fm = dram_in("lbl_fm", [128, 4, 3])
    hnw = dram_in("hnw", [128, 1])
    s5s_in = dram_in("s5s_in", [128, 2, 16, NS])
    hgs_in = dram_in("hgs_in", [NS * 4, 128, 128])
    s5p_out = dram_out("s5p_out", [128, 2, 16])
    s5s_out = dram_out("s5s_out", [128, 2, 16, NS])
    hgp_out = dram_out("hgp_out", [4, 128, 128])
    hgs_out = dram_out("hgs_out", [NS * 4, 128, 128])
    pr0 = dram_tmp("pr0", [2560, NTOK], F32)
    vtok0 = dram_tmp("vtok0", [NTOK, 512], F32)
    mixed = dram_tmp("mixed", [D, NTOK], BF16)

    def proj0():
        def vh(pi, nt, tok0, bi):
            o, ok = evac(ps[pi][0:nt, 0:512], psk[pi], 512, nt)
            DMA("pool", vtok0[tok0:tok0 + nt, :], o, r=[ok], w=[f"vtok0{bi}"])
        pass_proj(ab_w_in, 2560, [(0, 2560, pr0, 0, None)], [(1536, 512, vh)])

    def s5_phase():
        areset()
        K5 = ["s5p"]
        scr = aalloc([8192])
        st = aalloc([3, 16])
        DMA("sp", st, s5_st.rearrange("a p j -> p a j"), r=[], w=K5)
        dt = aalloc([16]); mag = aalloc([16]); th = aalloc([16]); cs = aalloc([16]); sn = aalloc([16])
        ta16 = aalloc([16]); tb16 = aalloc([16])
        ACT(dt, st[:, 2, :], AF.Exp, r=K5, w=K5)
        TT("dve", ta16, st[:, 0, :], dt, ALU.mult, r=K5, w=K5)
        ACT(mag, ta16, AF.Exp, r=K5, w=K5)
        TT("dve", th, st[:, 1, :], dt, ALU.mult, r=K5, w=K5)
        ACT(cs, th, AF.Sin, r=K5 + ["halfpi"], w=K5, scale=1.0 / 64, bias=halfpi[:, 0:1])
        ACT(sn, th, AF.Sin, r=K5, w=K5, scale=1.0 / 64)

        def sq_cs(c_, s_, a_, b_):
            TT("dve", a_, c_, c_, ALU.mult, r=K5, w=K5)
            TT("dve", b_, s_, s_, ALU.mult, r=K5, w=K5)
            TT("dve", b_, a_, b_, ALU.subtract, r=K5, w=K5)
            TT("dve", a_, c_, s_, ALU.mult, r=K5, w=K5)
            TS("dve", s_, a_, 2.0, ALU.mult, r=K5, w=K5)
            CP("dve", c_, b_, r=K5, w=K5)
        for _ in range(6):
            sq_cs(cs, sn, ta16, tb16)
        Wbu = aalloc([2, 4, 512], BF16)
        chp = aalloc([3, 512]); bbl = aalloc([2, 512])
        w = [scr[:, i * 512:(i + 1) * 512] for i in range(8)]
        for c in range(4):
            DMA("sp", chp, s5_ch[:, c].rearrange("a p x -> p a x"), r=K5, w=K5)
            DMA("sp", bbl, s5_bblk[:, c].rearrange("a p x -> p a x"), r=K5, w=K5)
            dtc, lrd, magc, thc, cc, sc, t0, t1_ = w
            ACT(dtc, chp[:, 2, :], AF.Exp, r=K5, w=K5)
            TT("dve", lrd, chp[:, 0, :], dtc, ALU.mult, r=K5, w=K5)
            ACT(magc, lrd, AF.Exp, r=K5, w=K5)
            TT("dve", thc, chp[:, 1, :], dtc, ALU.mult, r=K5, w=K5)
            ACT(cc, thc, AF.Sin, r=K5 + ["halfpi"], w=K5, scale=1.0 / 64, bias=halfpi[:, 0:1])
            ACT(sc, thc, AF.Sin, r=K5, w=K5, scale=1.0 / 64)
            for _ in range(6):
                sq_cs(cc, sc, t0, t1_)
            TT("dve", cc, cc, magc, ALU.mult, r=K5, w=K5)
            TT("dve", sc, sc, magc, ALU.mult, r=K5, w=K5)
            TS("dve", cc, cc, -1.0, ALU.add, r=K5, w=K5)
            lr, li = chp[:, 0, :], chp[:, 1, :]
            TT("dve", t0, lr, lr, ALU.mult, r=K5, w=K5)
            TT("dve", t1_, li, li, ALU.mult, r=K5, w=K5)
            TT("dve", t0, t0, t1_, ALU.add, r=K5, w=K5)
            S.op("dve", lambda e, t0=t0: e.reciprocal(out=t0, in_=t0), r=K5, w=K5)
            TT("dve", dtc, cc, lr, ALU.mult, r=K5, w=K5)
            TT("dve", lrd, sc, li, ALU.mult, r=K5, w=K5)
            TT("dve", dtc, dtc, lrd, ALU.add, r=K5, w=K5)
            TT("dve", dtc, dtc, t0, ALU.mult, r=K5, w=K5)
            TT("dve", lrd, sc, lr, ALU.mult, r=K5, w=K5)
            TT("dve", magc, cc, li, ALU.mult, r=K5, w=K5)
            TT("dve", lrd, lrd, magc, ALU.subtract, r=K5, w=K5)
            TT("dve", lrd, lrd, t0, ALU.mult, r=K5, w=K5)
            TT("dve", magc, dtc, bbl[:, 0, :], ALU.mult, r=K5, w=K5)
            TT("dve", thc, lrd, bbl[:, 1, :], ALU.mult, r=K5, w=K5)
            TT("dve", Wbu[:, 0, c, :], magc, thc, ALU.subtract, r=K5, w=K5)
            TT("dve", magc, dtc, bbl[:, 1, :], ALU.mult, r=K5, w=K5)
            TT("dve", thc, lrd, bbl[:, 0, :], ALU.mult, r=K5, w=K5)
            TT("dve", Wbu[:, 1, c, :], magc, thc, ALU.add, r=K5, w=K5)
        Wc = aalloc([2, 16, 128])
        DMA("sp", Wc, s5_cblk.rearrange("a j p x -> p a j x"), r=[], w=K5)
        TS("dve", Wc[:, 1], Wc[:, 1], -1.0, ALU.mult, r=K5, w=K5)
        dfm = aalloc([4]); gbf = aalloc([4])
        DMA("sp", dfm, s5_dfm[:, :], r=[], w=K5)
        DMA("sp", gbf, glu_bfm[:, :], r=[], w=K5)
        Wg = aalloc([4, 512], BF16)
        DMA("pool", Wg, glu_w.rearrange("(k p) f -> p k f", p=128), r=[], w=K5)
        Ec = aalloc([16, 512]); Es = aalloc([16, 512])
        pc = aalloc([16]); psn = aalloc([16])
        tA = scr[:, 0:4096].rearrange("p (a b) -> p a b", a=16)
        tB = scr[:, 4096:8192].rearrange("p (a b) -> p a b", a=16)
        CP("dve", pc, cs, r=K5, w=K5)
        CP("dve", psn, sn, r=K5, w=K5)
        MS("dve", Ec[:, :, 0:1], 1.0, K5)
        MS("dve", Es[:, :, 0:1], 0.0, K5)
        L = 1
        while L < 512:
            pcB = pc.unsqueeze(2).to_broadcast([128, 16, L])
            psB = psn.unsqueeze(2).to_broadcast([128, 16, L])
            TT("dve", tA[:, :, 0:L], Ec[:, :, 0:L], pcB, ALU.mult, r=K5, w=K5)
            TT("dve", tB[:, :, 0:L], Es[:, :, 0:L], psB, ALU.mult, r=K5, w=K5)
            TT("dve", Ec[:, :, L:2 * L], tA[:, :, 0:L], tB[:, :, 0:L], ALU.subtract, r=K5, w=K5)
            TT("dve", tA[:, :, 0:L], Ec[:, :, 0:L], psB, ALU.mult, r=K5, w=K5)
            TT("dve", tB[:, :, 0:L], Es[:, :, 0:L], pcB, ALU.mult, r=K5, w=K5)
            TT("dve", Es[:, :, L:2 * L], tA[:, :, 0:L], tB[:, :, 0:L], ALU.add, r=K5, w=K5)
            sq_cs(pc, psn, ta16, tb16)
            L *= 2
        if DBG5:
            dbgE = dram_out("dbgE", [128, 2, 16, 512])
            DMA("sp", dbgE[:, 0], Ec, r=K5, w=["dbgE"])
            DMA("sp", dbgE[:, 1], Es, r=K5, w=["dbgE"])
            dbgP = dram_out("dbgP", [128, 3, 16])
            DMA("sp", dbgP[:, 0], mag, r=K5, w=["dbgP"])
            DMA("sp", dbgP[:, 1], cs, r=K5, w=["dbgP"])
            DMA("sp", dbgP[:, 2], sn, r=K5, w=["dbgP"])
        S.barrier()
        lbr = aalloc([16]); lbi = aalloc([16])
        TT("dve", lbr, mag, cs, ALU.mult, r=K5, w=K5)
        TT("dve", lbi, mag, sn, ALU.mult, r=K5, w=K5)

        ini = aalloc([2, 16])
        MS("dve", ini, 0.0, K5)
        hlast = aalloc([2, 16])
        ub = [aalloc([4, 512], BF16) for _ in range(2)]
        uf = [aalloc([4, 512]) for _ in range(2)]
        wk = [[scr[:, (a * 8 + i) * 512:(a * 8 + i + 1) * 512] for i in range(8)] for a in range(2)]
        yf = aalloc([4, 512]); ygb = aalloc([4, 512], BF16); outb = aalloc([4, 512], BF16)
        g1 = aalloc([512]); g2 = aalloc([512])
        Hs = scr[:, 0:1024].rearrange("p (a j s t) -> p a j s t", a=2, j=16, s=NS)
        bus = scr[:, 1024:2048].rearrange("p (a j s t) -> p a j s t", a=2, j=16, s=NS)
        h0 = scr[:, 2048:2176].rearrange("p (a j s) -> p a j s", a=2, j=16)

        def epilogue(bi, n, usl, uk):
            c0 = blocks[bi][0]
            for c in range(4):
                y = yf[:, c, 0:n]
                TT("dve", g1[:, 0:n], y, y, ALU.mult, r=["yf"], w=["g1"])
                TS("dve", g1[:, 0:n], g1[:, 0:n], 0.044715, ALU.mult, r=["g1"], w=["g1"], s2=1.0, op1=ALU.add)
                TT("dve", g1[:, 0:n], g1[:, 0:n], y, ALU.mult, r=["g1", "yf"], w=["g1"])
                ACT(g2[:, 0:n], g1[:, 0:n], AF.Sigmoid, r=["g1"], w=["g2"], scale=1.5957691216057308)
                TT("dve", y, y, g2[:, 0:n], ALU.mult, r=["yf", "g2"], w=["yf"])
                ACT(ygb[:, c, 0:n], y, AF.Copy, r=["yf"], w=["ygb"])
            for co in range(4):
                pz = nextps()
                for ci in range(4):
                    MM(ps[pz][:, 0:n], Wg[:, ci, co * 128:(co + 1) * 128], ygb[:, ci, 0:n], ci == 0, ci == 3,
                       r=K5 + ["ygb"], w=[psk[pz]])
                ACT(g2[:, 0:n], ps[pz][:, 0:n], AF.Sigmoid, r=[psk[pz]] + K5, w=["g2"], bias=gbf[:, co:co + 1])
                TT("dve", outb[:, co, 0:n], yf[:, co, 0:n], g2[:, 0:n], ALU.mult, r=["yf", "g2"], w=["outb"])
            DMA("sp", mixed[0:512, c0:c0 + n].rearrange("(k p) t -> p k t", p=128), outb[:, :, 0:n],
                r=["outb"], w=[f"mixed{bi}"])

        for bi in range(NB):
            c0, n = blocks[bi]
            sl = bi % 2
            DMA("pool", uf[sl][:, :, 0:n], pr0[0:512, c0:c0 + n].rearrange("(k p) t -> p k t", p=128),
                r=[f"pr0{bi}"], w=[f"uf{sl}"])
            DMA("pool", ub[sl][:, :, 0:n], pr0[0:512, c0:c0 + n].rearrange("(k p) t -> p k t", p=128),
                r=[f"pr0{bi}"], w=[f"ub{sl}"])
            for c in range(4):
                py = c % 2
                for jj in range(4):
                    j = 4 * c + jj
                    ws = (j % 2)
                    d_re, d_im, g_re, g_im, h_re, h_im, ta, tb = wk[ws]
                    kk = f"wk{ws}"
                    pr_ = 2 + (2 * j) % 6
                    pi_ = 2 + (2 * j + 1) % 6
                    MM(ps[pr_][:, 0:n], Wbu[:, 0, c, jj * 128:(jj + 1) * 128], ub[sl][:, c, 0:n], True, True,
                       r=K5 + [f"ub{sl}"], w=[psk[pr_]])
                    MM(ps[pi_][:, 0:n], Wbu[:, 1, c, jj * 128:(jj + 1) * 128], ub[sl][:, c, 0:n], True, True,
                       r=K5 + [f"ub{sl}"], w=[psk[pi_]])
                    ec, es_ = Ec[:, j, 0:n], Es[:, j, 0:n]
                    TT("dve", ta[:, 0:n], ps[pr_][:, 0:n], ec, ALU.mult, r=[psk[pr_]] + K5, w=[kk])
                    TT("dve", tb[:, 0:n], ps[pi_][:, 0:n], es_, ALU.mult, r=[psk[pi_]] + K5, w=[kk])
                    TT("dve", d_re[:, 0:n], ta[:, 0:n], tb[:, 0:n], ALU.add, r=[kk], w=[kk])
                    TT("dve", ta[:, 0:n], ps[pi_][:, 0:n], ec, ALU.mult, r=[psk[pi_]] + K5, w=[kk])
                    TT("dve", tb[:, 0:n], ps[pr_][:, 0:n], es_, ALU.mult, r=[psk[pr_]] + K5, w=[kk])
                    TT("dve", d_im[:, 0:n], ta[:, 0:n], tb[:, 0:n], ALU.subtract, r=[kk], w=[kk])
                    S.op("dve", lambda e, g_re=g_re, d_re=d_re, j=j, n=n: e.tensor_tensor_scan(
                        out=g_re[:, 0:n], data0=mag[:, j:j + 1].to_broadcast([128, n]), data1=d_re[:, 0:n], initial=ini[:, 0, j:j + 1],
                        op0=ALU.mult, op1=ALU.add), r=[kk, "ini"] + K5, w=[kk])
                    S.op("dve", lambda e, g_im=g_im, d_im=d_im, j=j, n=n: e.tensor_tensor_scan(
                        out=g_im[:, 0:n], data0=mag[:, j:j + 1].to_broadcast([128, n]), data1=d_im[:, 0:n], initial=ini[:, 1, j:j + 1],
                        op0=ALU.mult, op1=ALU.add), r=[kk, "ini"] + K5, w=[kk])
                    TT("dve", ta[:, 0:n], g_re[:, 0:n], ec, ALU.mult, r=[kk] + K5, w=[kk])
                    TT("dve", tb[:, 0:n], g_im[:, 0:n], es_, ALU.mult, r=[kk] + K5, w=[kk])
                    TT("dve", h_re[:, 0:n], ta[:, 0:n], tb[:, 0:n], ALU.subtract, r=[kk], w=[kk])
                    TT("dve", ta[:, 0:n], g_re[:, 0:n], es_, ALU.mult, r=[kk] + K5, w=[kk])
                    TT("dve", tb[:, 0:n], g_im[:, 0:n], ec, ALU.mult, r=[kk] + K5, w=[kk])
                    TT("dve", h_im[:, 0:n], ta[:, 0:n], tb[:, 0:n], ALU.add, r=[kk], w=[kk])
                    if DBG5 and bi == 0 and j in (0, 5):
                        dbgH = dram_out(f"dbgH{j}", [128, 6, 512])
                        for ii, tt_ in enumerate((d_re, d_im, g_re, g_im, h_re, h_im)):
                            DMA("sp", dbgH[:, ii, :], tt_[:, 0:n], r=[kk], w=[f"dbgH{j}"])
                        dbgB = dram_out(f"dbgB{j}", [128, 2, 512])
                        o_, ok_ = evac(ps[pr_][:, 0:n], psk[pr_], n, 128)
                        DMA("sp", dbgB[:, 0, :], o_, r=[ok_], w=[f"dbgB{j}"])
                        o_, ok_ = evac(ps[pi_][:, 0:n], psk[pi_], n, 128)
                        DMA("sp", dbgB[:, 1, :], o_, r=[ok_], w=[f"dbgB{j}"])
                    hrl, hil = h_re[:, n - 1:n], h_im[:, n - 1:n]
                    TT("dve", ta[:, 0:1], hil, sn[:, j:j + 1], ALU.mult, r=[kk] + K5, w=[kk])
                    STT(ini[:, 0, j:j + 1], hrl, cs[:, j:j + 1], ta[:, 0:1], ALU.mult, ALU.subtract, r=[kk] + K5, w=["ini"])
                    TT("dve", ta[:, 0:1], hil, cs[:, j:j + 1], ALU.mult, r=[kk] + K5, w=[kk])
                    STT(ini[:, 1, j:j + 1], hrl, sn[:, j:j + 1], ta[:, 0:1], ALU.mult, ALU.add, r=[kk] + K5, w=["ini"])
                    if bi == NB - 1:
                        CP("dve", hlast[:, 0, j:j + 1], hrl, r=[kk], w=["hlast"])
                        CP("dve", hlast[:, 1, j:j + 1], hil, r=[kk], w=["hlast"])
                    MM(ps[py][:, 0:n], Wc[:, 0, j, :], h_re[:, 0:n], jj == 0, False, r=K5 + [kk], w=[psk[py]])
                    MM(ps[py][:, 0:n], Wc[:, 1, j, :], h_im[:, 0:n], False, jj == 3, r=K5 + [kk], w=[psk[py]])
                STT(yf[:, c, 0:n], uf[sl][:, c, 0:n], dfm[:, c:c + 1], ps[py][:, 0:n], ALU.mult, ALU.add,
                    r=[f"uf{sl}", psk[py]] + K5, w=["yf"])
            epilogue(bi, n, sl, None)
        DMA("sp", s5p_out[:, :, :], hlast, r=["hlast"], w=["s5p_out"])

        S.barrier()
        DMA("sp", h0, s5s_in[:, :, :, :], r=[], w=["h0"])
        bi = NB
        c0, n = blocks[bi]
        DMA("sp", uf[0][:, :, 0:n], pr0[0:512, c0:c0 + n].rearrange("(k p) t -> p k t", p=128),
            r=[f"pr0{bi}"], w=["uf0"])
        DMA("pool", ub[0][:, :, 0:n], pr0[0:512, c0:c0 + n].rearrange("(k p) t -> p k t", p=128),
            r=[f"pr0{bi}"], w=["ub0"])
        pr_, pi_ = nextps(), nextps()
        for j in range(16):
            c, jj = divmod(j, 4)
            MM(ps[pr_][:, j * n:(j + 1) * n], Wbu[:, 0, c, jj * 128:(jj + 1) * 128], ub[0][:, c, 0:n], True, True,
               r=K5 + ["ub0"], w=[psk[pr_]])
            MM(ps[pi_][:, j * n:(j + 1) * n], Wbu[:, 1, c, jj * 128:(jj + 1) * 128], ub[0][:, c, 0:n], True, True,
               r=K5 + ["ub0"], w=[psk[pi_]])
        CP("dve", bus[:, 0].rearrange("p j s t -> p (j s t)"), ps[pr_][:, 0:16 * n], r=[psk[pr_]], w=["bus"])
        CP("dve", bus[:, 1].rearrange("p j s t -> p (j s t)"), ps[pi_][:, 0:16 * n], r=[psk[pi_]], w=["bus"])
        sA = scr[:, 2176:2240].rearrange("p (j s) -> p j s", j=16)
        sB = scr[:, 2240:2304].rearrange("p (j s) -> p j s", j=16)
        lbrB = lbr.unsqueeze(2).to_broadcast([128, 16, NS])
        lbiB = lbi.unsqueeze(2).to_broadcast([128, 16, NS])
        KH = ["Hs"]
        for t in range(LS):
            pr_re = h0[:, 0] if t == 0 else Hs[:, 0, :, :, t - 1]
            pr_im = h0[:, 1] if t == 0 else Hs[:, 1, :, :, t - 1]
            rr = KH + ["h0", "bus"] + K5
            TT("dve", sA, pr_re, lbrB, ALU.mult, r=rr, w=["sA"])
            TT("dve", sB, pr_im, lbiB, ALU.mult, r=rr, w=["sB"])
            TT("dve", sA, sA, sB, ALU.subtract, r=["sA", "sB"], w=["sA"])
            TT("dve", Hs[:, 0, :, :, t], sA, bus[:, 0, :, :, t], ALU.add, r=["sA", "bus"], w=KH)
            TT("dve", sA, pr_re, lbiB, ALU.mult, r=rr, w=["sA"])
            TT("dve", sB, pr_im, lbrB, ALU.mult, r=rr, w=["sB"])
            TT("dve", sA, sA, sB, ALU.add, r=["sA", "sB"], w=["sA"])
            TT("dve", Hs[:, 1, :, :, t], sA, bus[:, 1, :, :, t], ALU.add, r=["sA", "bus"], w=KH)
        hs_fin = scr[:, 2304:2432].rearrange("p (a j s) -> p a j s", a=2, j=16)
        CP("dve", hs_fin, Hs[:, :, :, :, LS - 1], r=KH, w=["hs_fin"])
        DMA("sp", s5s_out[:, :, :, :], hs_fin, r=["hs_fin"], w=["s5s_out"])
        for c in range(4):
            py = nextps()
            for jj in range(4):
                j = 4 * c + jj
                MM(ps[py][:, 0:n], Wc[:, 0, j, :], Hs[:, 0, j].rearrange("p s t -> p (s t)"), jj == 0, False,
                   r=K5 + KH, w=[psk[py]])
                MM(ps[py][:, 0:n], Wc[:, 1, j, :], Hs[:, 1, j].rearrange("p s t -> p (s t)"), False, jj == 3,
                   r=K5 + KH, w=[psk[py]])
            STT(yf[:, c, 0:n], uf[0][:, c, 0:n], dfm[:, c:c + 1], ps[py][:, 0:n], ALU.mult, ALU.add,
                r=["uf0", psk[py]] + K5, w=["yf"])
        epilogue(bi, n, 0, None)


    def hgrn_phase():
        areset()
        KP = ["hgp"]
        lbl = aalloc([4, 3]); ssum = aalloc([4]); lb = aalloc([4]); oml = aalloc([4]); nw = aalloc([1])
        DMA("sp", lbl, lbl_fm[:, :, :], r=[], w=KP)
        DMA("sp", nw, hnw[:, :], r=[], w=KP)
        ACT(lbl, lbl, AF.Exp, r=KP, w=KP)
        S.op("dve", lambda e: e.reduce_sum(out=ssum, in_=lbl, axis=AX.X), r=KP, w=KP)
        S.op("dve", lambda e: e.reciprocal(out=ssum, in_=ssum), r=KP, w=KP)
        TT("dve", lb, lbl[:, :, 0], ssum, ALU.mult, r=KP, w=KP)
        TS("dve", oml, lb, -1.0, ALU.mult, r=KP, w=KP, s2=1.0, op1=ALU.add)
        maskLE = aalloc([64])
        S.op("pool", lambda e: e.affine_select(out=maskLE[0:64, :], in_=ones_f[0:64, 0:64], pattern=[[1, 64]],
                                               compare_op=ALU.is_ge, fill=0.0, base=0, channel_multiplier=-1),
             r=["ones_f"], w=KP)
        m01 = {}
        for C_, n_ in ((64, 512), (8, 8)):
            m = aalloc([n_])
            MS("dve", m, 1.0, KP)
            MS("dve", m.rearrange("p (a c) -> p a c", c=C_)[:, :, 0:1], 0.0, KP)
            m01[C_] = m
        NU = 4
        U = []
        for u in range(NU):
            d = {}
            for nm in ("qr", "fr", "gr", "f", "b", "qin", "kin", "kdec", "tmp"):
                d[nm] = aalloc([512])
            d["v"] = aalloc([8, 128])
            d["kdT"] = aalloc([8, 128])
            d["a"] = aalloc([8])
            d["S"] = [aalloc([128]), aalloc([128])]
            d["scm"] = [aalloc([64]), aalloc([64])]
            d["osq"] = aalloc([512], BF16)
            d["ob"] = aalloc([512], BF16)
            U.append(d)
        rot = [4]

        def rps():
            i = rot[0]
            rot[0] = 4 + (i - 4 + 1) % 4
            return i

        def hg_block(units, n, C, si):
            nch = n // C
            for (u, head, col0) in units:
                d = U[u]; k = f"hu{u}"
                DMA("sp", d["qr"][:, 0:n], pr0[512 + head * 128:512 + (head + 1) * 128, col0:col0 + n], r=["pr0all"], w=[k])
                DMA("sp", d["fr"][:, 0:n], pr0[1024 + head * 128:1024 + (head + 1) * 128, col0:col0 + n], r=["pr0all"], w=[k])
                DMA("sp", d["gr"][:, 0:n], pr0[2048 + head * 128:2048 + (head + 1) * 128, col0:col0 + n], r=["pr0all"], w=[k])
                vv = d["v"].rearrange("p a d -> p (a d)")[0:C, 0:nch * 128].rearrange("p (a d) -> p a d", d=128)
                DMA("sp", vv, vtok0[col0:col0 + n, head * 128:(head + 1) * 128].rearrange("(a c) d -> c a d", c=C),
                    r=["vtok0all"], w=[k])
            for (u, head, col0) in units:
                d = U[u]; k = f"hu{u}"
                ACT(d["f"][:, 0:n], d["fr"][:, 0:n], AF.Sigmoid, r=[k], w=[k])
                ACT(d["gr"][:, 0:n], d["gr"][:, 0:n], AF.Silu, r=[k], w=[k])
                ACT(d["qr"][:, 0:n], d["qr"][:, 0:n], AF.Silu, r=[k], w=[k])
            for (u, head, col0) in units:
                d = U[u]; k = f"hu{u}"
                TS("dve", d["f"][:, 0:n], d["f"][:, 0:n], oml[:, head:head + 1], ALU.mult, r=[k] + KP, w=[k],
                   s2=lb[:, head:head + 1], op1=ALU.add)
                ACT(d["tmp"][:, 0:n], d["f"][:, 0:n], AF.Ln, r=[k], w=[k])
                S.op("dve", lambda e, d=d: e.tensor_tensor_scan(
                    out=d["b"][:, 0:n], data0=m01[C][:, 0:n], data1=d["tmp"][:, 0:n], initial=0.0,
                    op0=ALU.mult, op1=ALU.add), r=[k] + KP, w=[k])
                TS("dve", d["f"][:, 0:n], d["f"][:, 0:n], -1.0, ALU.mult, r=[k], w=[k], s2=1.0, op1=ALU.add)
                ACT(d["tmp"][:, 0:n], d["b"][:, 0:n], AF.Exp, r=[k], w=[k])
                TT("dve", d["qin"][:, 0:n], d["qr"][:, 0:n], d["tmp"][:, 0:n], ALU.mult, r=[k], w=[k])
                ACT(d["tmp"][:, 0:n], d["b"][:, 0:n], AF.Exp, r=[k], w=[k], scale=-1.0)
                TT("dve", d["kin"][:, 0:n], d["f"][:, 0:n], d["tmp"][:, 0:n], ALU.mult, r=[k], w=[k])
                b3 = d["b"][:, 0:n].rearrange("p (a c) -> p a c", c=C)
                t3 = d["tmp"][:, 0:n].rearrange("p (a c) -> p a c", c=C)
                TT("dve", t3, b3[:, :, C - 1:C].to_broadcast([128, nch, C]), b3, ALU.subtract, r=[k], w=[k])
                ACT(d["tmp"][:, 0:n], d["tmp"][:, 0:n], AF.Exp, r=[k], w=[k])
                TT("dve", d["kdec"][:, 0:n], d["f"][:, 0:n], d["tmp"][:, 0:n], ALU.mult, r=[k], w=[k])
                ACT(d["a"][:, 0:nch], b3[:, :, C - 1], AF.Exp, r=[k], w=[k])
                for g0 in range(0, nch, 4):
                    g1_ = min(nch, g0 + 4)
                    pt = rps()
                    for ch in range(g0, g1_):
                        TR(ps[pt][0:C, (ch - g0) * 128:(ch - g0 + 1) * 128], d["kdec"][:, ch * C:(ch + 1) * C], ident[:],
                           r=[k, "ident"], w=[psk[pt]])
                    CP("dve", d["kdT"].rearrange("p a d -> p (a d)")[0:C, g0 * 128:g1_ * 128],
                       ps[pt][0:C, 0:(g1_ - g0) * 128], r=[psk[pt]], w=[k])
            for ch in range(nch):
                for (u, head, col0) in units:
                    d = U[u]; k = f"hu{u}"; sk = f"hS{u}"
                    cols = slice(ch * C, (ch + 1) * C)
                    Sp = d["S"][si[u]]; Sn = d["S"][1 - si[u]]
                    vch = d["v"][0:C, ch, :] if False else d["v"].rearrange("p a d -> p (a d)")[0:C, ch * 128:(ch + 1) * 128]
                    kdch = d["kdT"].rearrange("p a d -> p (a d)")[0:C, ch * 128:(ch + 1) * 128]
                    pS = rps()
                    MM(ps[pS][0:C, 0:C], d["kin"][:, cols], d["qin"][:, cols], True, True, r=[k], w=[psk[pS]])
                    scm = d["scm"][ch % 2]
                    TT("dve", scm[0:C, 0:C], ps[pS][0:C, 0:C], maskLE[0:C, 0:C], ALU.mult, r=[psk[pS]] + KP, w=[k + f"scm{ch % 2}"])
                    MM(ps[u][:, cols], Sp, d["qin"][:, cols], True, False, r=[k, sk], w=[psk[u]])
                    MM(ps[u][:, cols], vch, scm[0:C, 0:C], False, True, r=[k, k + f"scm{ch % 2}"], w=[psk[u]])
                    pU = rps()
                    MM(ps[pU][:, 0:128], kdch, vch, True, True, r=[k], w=[psk[pU]])
                    STT(Sn, Sp, d["a"][:, ch:ch + 1], ps[pU][:, 0:128], ALU.mult, ALU.add, r=[k, sk, psk[pU]], w=[sk])
                    si[u] = 1 - si[u]
            for (u, head, col0) in units:
                d = U[u]; k = f"hu{u}"
                ACT(d["osq"][:, 0:n], ps[u][:, 0:n], AF.Square, r=[psk[u]], w=[k])
                pr_ = rps()
                MM(ps[pr_][:, 0:n], ones_bf[:], d["osq"][:, 0:n], True, True, r=[k, "ones_bf"], w=[psk[pr_]])
                ACT(d["tmp"][:, 0:n], ps[pr_][:, 0:n], AF.Sqrt, r=[psk[pr_], "epsb"], w=[k], scale=1.0 / 128, bias=epsb[:, 0:1])
                S.op("dve", lambda e, d=d: e.reciprocal(out=d["tmp"][:, 0:n], in_=d["tmp"][:, 0:n]), r=[k], w=[k])
                TT("dve", d["tmp"][:, 0:n], ps[u][:, 0:n], d["tmp"][:, 0:n], ALU.mult, r=[k, psk[u]], w=[k])
                STT(d["ob"][:, 0:n], d["tmp"][:, 0:n], nw[:, 0:1], d["gr"][:, 0:n], ALU.mult, ALU.mult, r=[k] + KP, w=[k])
                DMA("pool", mixed[512 + head * 128:512 + (head + 1) * 128, col0:col0 + n], d["ob"][:, 0:n],
                    r=[k], w=["mixedall"])

        si = [0] * NU
        for u in range(NU):
            MS("dve", U[u]["S"][0], 0.0, [f"hS{u}"])
        for bi in range(NB):
            hg_block([(u, u, bi * 512) for u in range(NU)], 512, 64, si)
        for u in range(NU):
            DMA("sp", hgp_out[u], U[u]["S"][si[u]], r=[f"hS{u}"], w=["hgp_out"])
        for sq_ in range(NS):
            for u in range(NU):
                DMA("sp", U[u]["S"][si[u]], hgs_in[sq_ * 4 + u], r=[], w=[f"hS{u}"])
            hg_block([(u, u, T + sq_ * LS) for u in range(NU)], LS, LS, si)
            for u in range(NU):
                DMA("sp", hgs_out[sq_ * 4 + u], U[u]["S"][si[u]], r=[f"hS{u}"], w=["hgs_out"])


    NP = PAST // 128
    NTt = T // 128
    NROWS = None
    cd_w_in = dram_in("cd_w_in", [D, 3080])
    cd_w_out = dram_in("cd_w_out", [D, D])
    bfb_in = dram_in("bfb_in", [128, 8])
    bsb_in = dram_in("bsb_in", [128, 8])
    pt_rep = dram_in("pt_rep", [128, NS * NP], I32)
    qk1 = dram_out("qk1", [2048, NTOK])
    fv_out = dram_out("fv_out", [NTOK, 512])
    sv_out = dram_out("sv_out", [NTOK, 512])
    logf_out = dram_out("logf_out", [NTOK, 8])
    qk1s = dram_tmp("qk1s", [2048, NTOK], F32)
    fv_s = dram_tmp("fv_s", [NTOK, 512], F32)
    sv_s = dram_tmp("sv_s", [NTOK, 512], F32)
    logf_s = dram_tmp("logf_s", [NTOK, 8], F32)

    def cache_in(name, w):
        return nc.dram_tensor(name, [NPHYS * 128, w], F32, kind="ExternalInput").ap()
    c_fk = cache_in("c_fk", 512); c_fv = cache_in("c_fv", 512); c_lf = cache_in("c_lf", 8)
    c_sk = cache_in("c_sk", 512); c_sv = cache_in("c_sv", 512)

    oneb = sb("oneb", [128, 1], F32)
    MS("pool", oneb[:], 1.0, ["oneb"])
    bfb = sb("bfb", [128, 8], F32)
    bsb = sb("bsb", [128, 8], F32)
    DMA("sp", bfb[:], bfb_in[:, :], r=[], w=["bfb"])
    DMA("sp", bsb[:], bsb_in[:, :], r=[], w=["bsb"])
    triS_f = sb("triS_f", [128, 128], F32)
    triI_f = sb("triI_f", [128, 128], F32)
    triLE_f = sb("triLE_f", [128, 128], F32)
    triI_b = sb("triI_b", [128, 128], BF16)
    for (tile_, base_, cm_, st_) in ((triS_f, -1, 1, -1), (triI_f, 0, 1, -1), (triLE_f, 0, -1, 1)):
        S.op("pool", lambda e, tile_=tile_, base_=base_, cm_=cm_, st_=st_: e.affine_select(
            out=tile_[:], in_=ones_f[:], pattern=[[st_, 128]], compare_op=ALU.is_ge, fill=0.0,
            base=base_, channel_multiplier=cm_), r=["ones_f"], w=[tile_.name])
    CP("dve", triI_b[:], triI_f[:], r=["triI_f"], w=["triI_b"])
    mLTn = sb("mLTn", [LS, LS], F32)
    S.op("pool", lambda e: e.affine_select(out=mLTn[:], in_=ones_f[0:LS, 0:LS], pattern=[[1, LS]], compare_op=ALU.is_ge,
                                           fill=0.0, base=-1, channel_multiplier=-1), r=["ones_f"], w=["asp"])

    def proj1():
        def fvh(dst, dst2):
            def h_(pi, nt, tok0, bi):
                o, ok = evac(ps[pi][0:nt, 0:512], psk[pi], 512, nt)
                DMA("pool", dst[tok0:tok0 + nt, :], o, r=[ok], w=[f"{dst.name}{bi}"])
                DMA("pool", dst2[tok0:tok0 + nt, :], o, r=[ok], w=[f"{dst2.name}{bi}"])
            return h_

        def lgh(pi, nt, tok0, bi):
            i = stgrr[0]
            stgrr[0] = (i + 1) % 4
            o = stg[i][0:nt, 0:8]
            k = f"stg{i}"
            TT("dve", o, ps[pi][0:nt, 0:8], bfb[0:nt, :], ALU.add, r=[psk[pi], "bfb"], w=[k])
            ACT(o, o, AF.Exp, r=[k], w=[k], scale=-1.0)
            ACT(o, o, AF.Ln, r=[k, "oneb"], w=[k], bias=oneb[0:nt, 0:1])
            TS("dve", o, o, -1.0, ALU.mult, r=[k], w=[k])
            DMA("pool", logf_out[tok0:tok0 + nt, :], o, r=[k], w=[f"logf_out{bi}"])
            DMA("pool", logf_s[tok0:tok0 + nt, :], o, r=[k], w=[f"logf_s{bi}"])
        pass_proj(cd_w_in, 3080,
                  [(0, 512, [qk1, qk1s], 0, 0.125), (512, 512, [qk1, qk1s], 512, None),
                   (1544, 512, [qk1, qk1s], 1024, 0.125), (2056, 512, [qk1, qk1s], 1536, None)],
                  [(1024, 512, fvh(fv_out, fv_s)), (2568, 512, fvh(sv_out, sv_s)), (1536, 8, lgh)])

    def attn_phase():
        areset()
        KA = ["attp"]
        mLE = aalloc([4, 512], BF16)
        mLT = aalloc([4, 512], BF16)
        onesb512 = aalloc([512], BF16)
        MS("pool", onesb512, 1.0, KA)
        for m in range(4):
            S.op("pool", lambda e, m=m: e.affine_select(out=mLE[:, m, :], in_=onesb512, pattern=[[1, 512]],
                 compare_op=ALU.is_ge, fill=0.0, base=-128 * m, channel_multiplier=-1), r=KA, w=KA)
            S.op("pool", lambda e, m=m: e.affine_select(out=mLT[:, m, :], in_=onesb512, pattern=[[1, 512]],
                 compare_op=ALU.is_ge, fill=0.0, base=-128 * m - 1, channel_multiplier=-1), r=KA, w=KA)
        if ASTOP <= 1:
            return
        lfall = aalloc([NTt, 8]); Fneg = aalloc([NTt, 8]); carF = aalloc([8])
        for a0_ in range(0, NTt, 16):
            a1_ = min(NTt, a0_ + 16)
            DMA("sp", lfall[:, a0_:a1_, :], logf_s[a0_ * 128:a1_ * 128, :].rearrange("(a p) h -> p a h", p=128), r=[], w=KA)
        MS("dve", carF, 0.0, KA)
        for t in range(NTt - 1, -1, -1):
            p1, p2 = 2 + (2 * t) % 6, 2 + (2 * t + 1) % 6
            MM(ps[p1][:, 0:8], triS_f[:], lfall[:, t, :], True, True, r=KA + ["triS_f"], w=[psk[p1]])
            MM(ps[p2][:, 0:8], ones_f[:], lfall[:, t, :], True, True, r=KA + ["ones_f"], w=[psk[p2]])
            TT("dve", Fneg[:, t, :], ps[p1][:, 0:8], carF, ALU.add, r=[psk[p1]] + KA, w=KA)
            TT("dve", carF, carF, ps[p2][:, 0:8], ALU.add, r=[psk[p2]] + KA, w=KA)
        if ASTOP <= 2:
            return
        kT = aalloc([T], BF16); qT = aalloc([T], BF16)
        Vf = aalloc([NTt, 2, 65], BF16)
        MS("dve", Vf[:, :, :, 64:65], 1.0, ["Vf"])
        e_t = [aalloc([512]) for _ in range(2)]
        zc_t = [aalloc([512]) for _ in range(2)]
        spb_t = [aalloc([512], BF16) for _ in range(2)]
        w_t = [aalloc([512], BF16) for _ in range(2)]
        carry = aalloc([512]); dsb = aalloc([512]); rden = aalloc([512])
        ob_ = [aalloc([512], BF16) for _ in range(2)]
        rot = [2]

        def rps():
            i = rot[0]
            rot[0] = 2 + (i - 2 + 1) % 6
            return i
        cnt = [0]
        for kind in range(2):
            if ASTOP == 4 and kind == 1:
                break
            if ASTOP == 5 and kind == 0:
                continue
            qrow, krow, vsrc = (0, 512, fv_s) if kind == 0 else (1024, 1536, sv_s)
            for hp in range(4):
                for t0_ in range(0, T, 2048):
                    t1_ = min(T, t0_ + 2048)
                    DMA("pool", kT[:, t0_:t1_], qk1s[krow + hp * 128:krow + (hp + 1) * 128, t0_:t1_], r=[], w=["kT"])
                    DMA("pool", qT[:, t0_:t1_], qk1s[qrow + hp * 128:qrow + (hp + 1) * 128, t0_:t1_], r=[], w=["qT"])
                for hh_ in range(2):
                    for a0_ in range(0, NTt, 16):
                        a1_ = min(NTt, a0_ + 16)
                        DMA("pool", Vf[:, a0_:a1_, hh_, 0:64],
                            vsrc[a0_ * 128:a1_ * 128, hp * 128 + hh_ * 64:hp * 128 + (hh_ + 1) * 64].rearrange("(a p) d -> p a d", p=128),
                            r=[], w=["Vf"])
                if ASTOP == 3:
                    continue
                for hh in range(2):
                    h = 2 * hp + hh
                    pr0_ = slice(hh * 64, (hh + 1) * 64)
                    for i in range(NB):
                        po = cnt[0] % 2
                        cnt[0] += 1
                        jl = list(range(4 * i + 3, -1, -1))
                        if kind == 1:
                            MS("dve", carry, 0.0, ["carry"])
                        st_ = {}

                        def stageA(idx):
                            j = jl[idx]
                            a = idx % 2
                            m = j - 4 * i
                            pz = rps()
                            MM(ps[pz][:, 0:512], kT[pr0_, j * 128:(j + 1) * 128], qT[pr0_, i * 512:(i + 1) * 512],
                               True, True, r=["kT", "qT"], w=[psk[pz]])
                            if kind == 0:
                                P = w_t[a]
                                ACT(P, ps[pz][:, 0:512], AF.Exp, r=[psk[pz]] + KA, w=[f"w{a}"], bias=Fneg[:, j, h:h + 1])
                                if m >= 0:
                                    TT("pool", P, P, mLE[:, m, :], ALU.mult, r=[f"w{a}"] + KA, w=[f"w{a}"])
                                st_[idx] = (pz,)
                            else:
                                e_, spb = e_t[a], spb_t[a]
                                ACT(e_, ps[pz][:, 0:512], AF.Exp, r=[psk[pz], "bsb"], w=[f"e{a}"], bias=bsb[:, h:h + 1])
                                ACT(spb, e_, AF.Ln, r=[f"e{a}", "oneb"], w=[f"spb{a}"], bias=oneb[:, 0:1])
                                if m >= 0:
                                    TT("pool", spb, spb, mLT[:, m, :], ALU.mult, r=[f"spb{a}"] + KA, w=[f"spb{a}"])
                                pc_, pt_ = rps(), rps()
                                MM(ps[pc_][:, 0:512], triI_b[:], spb, True, True, r=["triI_b", f"spb{a}"], w=[psk[pc_]])
                                MM(ps[pt_][:, 0:512], ones_bf[:], spb, True, True, r=["ones_bf", f"spb{a}"], w=[psk[pt_]])
                                st_[idx] = (pz, pc_, pt_)

                        def stageB(idx):
                            j = jl[idx]
                            a = idx % 2
                            m = j - 4 * i
                            first, last = idx == 0, idx == len(jl) - 1
                            if kind == 0:
                                MM(ps[po][0:65, 0:512], Vf[:, j, hh, :], w_t[a], first, last, r=["Vf", f"w{a}"], w=[psk[po]])
                            else:
                                pz, pc_, pt_ = st_[idx]
                                zc_, w_ = zc_t[a], w_t[a]
                                TT("dve", zc_, ps[pz][:, 0:512], carry, ALU.subtract, r=[psk[pz], "carry", f"e{a}"], w=[f"zc{a}"])
                                TT("dve", zc_, zc_, ps[pc_][:, 0:512], ALU.subtract, r=[f"zc{a}", psk[pc_]], w=[f"zc{a}"])
                                ACT(w_, zc_, AF.Exp, r=[f"zc{a}", "bsb"], w=[f"w{a}"], bias=bsb[:, h:h + 1])
                                if m >= 0:
                                    TT("pool", w_, w_, mLT[:, m, :], ALU.mult, r=[f"w{a}"] + KA, w=[f"w{a}"])
                                TT("dve", carry, carry, ps[pt_][:, 0:512], ALU.add, r=["carry", psk[pt_]], w=["carry"])
                                MM(ps[po][0:64, 0:512], Vf[:, j, hh, 0:64], w_, first, last, r=["Vf", f"w{a}"], w=[psk[po]])

                        stageA(0)
                        for idx in range(len(jl)):
                            if idx + 1 < len(jl):
                                stageA(idx + 1)
                            stageB(idx)
                        if SBCUT < 6 and kind == 1:
                            continue
                        oo = ob_[po]
                        if kind == 0:
                            ACT(dsb[64:65, :], ps[po][64:65, 0:512], AF.Copy, r=[psk[po]], w=["dsb"])
                            pd = rps()
                            MM(ps[pd][0:64, 0:512], ones_f[64:65, 0:64], dsb[64:65, :], True, True,
                               r=["dsb", "ones_f"], w=[psk[pd]])
                            S.op("dve", lambda e, pd=pd: e.reciprocal(out=rden[0:64, :], in_=ps[pd][0:64, 0:512]),
                                 r=[psk[pd]], w=["rden"])
                            TT("dve", oo[0:64, :], ps[po][0:64, 0:512], rden[0:64, :], ALU.mult, r=[psk[po], "rden"], w=[f"ob_{po}"])
                        else:
                            ACT(oo[0:64, :], ps[po][0:64, 0:512], AF.Copy, r=[psk[po]], w=[f"ob_{po}"])
                        DMA("sp", mixed[kind * 512 + h * 64:kind * 512 + (h + 1) * 64, i * 512:(i + 1) * 512], oo[0:64, :],
                            r=[f"ob_{po}"], w=["mixedall"])


    def attn_sample_phase():
        areset()
        KS = ["asp"]
        ptf = aalloc([NS * NP]); idx = aalloc([NS * NP], I32); pti = aalloc([NS * NP], I32)
        iop_i = aalloc([1], I32); iop = aalloc([1])
        DMA("sp", pti, pt_rep[:, :], r=[], w=KS)
        S.op("pool", lambda e: e.iota(out=iop_i, pattern=[[0, 1]], base=0, channel_multiplier=1), w=KS)
        CP("dve", iop, iop_i, r=KS, w=KS)
        CP("dve", ptf, pti, r=KS, w=KS)
        TS("dve", ptf, ptf, 128.0, ALU.mult, r=KS, w=KS, s2=iop[:, 0:1], op1=ALU.add)
        CP("dve", idx, ptf, r=KS, w=KS)
        bm = aalloc([8])
        S.op("pool", lambda e: e.affine_select(out=bm[0:64, :], in_=ones_f[0:64, 0:8], pattern=[[-8, 8]],
             compare_op=ALU.is_ge, fill=0.0, base=0, channel_multiplier=1), r=["ones_f"], w=KS)
        S.op("pool", lambda e: e.affine_select(out=bm[0:64, :], in_=bm[0:64, :], pattern=[[8, 8]],
             compare_op=ALU.is_ge, fill=0.0, base=7, channel_multiplier=-1), r=KS, w=KS)
        Qb = [aalloc([4, 64]) for _ in range(2)]
        qs = aalloc([4, LS]); kTn = [aalloc([4, LS]) for _ in range(2)]
        Vn = [aalloc([512]) for _ in range(2)]
        lfn = aalloc([8]); biasn = aalloc([8])
        kpg = [aalloc([512]) for _ in range(2)]
        vpg = [aalloc([512]) for _ in range(2)]
        kTp = [aalloc([4, 128]) for _ in range(2)]
        lfp = [aalloc([8]) for _ in range(2)]
        Fb = aalloc([8]); carF = aalloc([8])
        zb = [aalloc([64]) for _ in range(2)]
        e_ = [aalloc([64]) for _ in range(2)]
        sp_ = [aalloc([64]) for _ in range(2)]
        Pw = [aalloc([64]) for _ in range(2)]
        carry = aalloc([64])
        tmpo = aalloc([8, 64]); red = aalloc([64]); rd = aalloc([2]); o16 = aalloc([64], BF16)
        rot = [3]

        def rps():
            i = rot[0]
            rot[0] = 3 + (i - 3 + 1) % 5
            return i
        for sq_ in range(NS):
            col = T + sq_ * LS
            for kind in range(2):
                qrow, krow, vsrc, ck, cv = (0, 512, fv_s, c_fk, c_fv) if kind == 0 else (1024, 1536, sv_s, c_sk, c_sv)
                po, pd = 0, 1
                Q = Qb[kind]
                MS("dve", Q, 0.0, [f"Q{kind}"])
                DMA("sp", qs, qk1s[qrow:qrow + 512, col:col + LS].rearrange("(k p) t -> p k t", p=128), r=[], w=["qs"])
                for pr in range(4):
                    for hh in range(2):
                        hd = 2 * pr + hh
                        CP("dve", Q[hh * 64:(hh + 1) * 64, pr, hd * LS:(hd + 1) * LS], qs[hh * 64:(hh + 1) * 64, pr, :],
                           r=["qs"], w=[f"Q{kind}"])
                DMA("sp", kTn[kind], qk1s[krow:krow + 512, col:col + LS].rearrange("(k p) t -> p k t", p=128), r=[], w=[f"kTn{kind}"])
                DMA("sp", Vn[kind][0:LS, :], vsrc[col:col + LS, :], r=[], w=[f"Vn{kind}"])
                if kind == 0:
                    DMA("sp", lfn[0:LS, :], logf_s[col:col + LS, :], r=[], w=["lfn"])
                    MS("dve", carF, 0.0, ["carF"])
                else:
                    MS("dve", carry, 0.0, ["carry"])
                ntiles = NP + 1
                for ti in range(ntiles):
                    a = ti % 2
                    first, last = ti == 0, ti == ntiles - 1
                    new = ti == 0
                    ns = LS if new else 128
                    pz = rps()
                    if new:
                        for pr in range(4):
                            MM(ps[pz][0:ns, 0:64], kTn[kind][:, pr, :], Q[:, pr, :], pr == 0, pr == 3,
                               r=[f"kTn{kind}", f"Q{kind}"], w=[psk[pz]])
                        V = Vn[kind]
                        vkey = f"Vn{kind}"
                    else:
                        pg = NP - ti
                        ic = sq_ * NP + pg
                        S.op("pool", lambda e, a=a, ic=ic, ck=ck: e.indirect_dma_start(
                            out=kpg[a], out_offset=None, in_=ck,
                            in_offset=bass.IndirectOffsetOnAxis(ap=idx[:, ic:ic + 1], axis=0)),
                            r=KS, w=[f"kpg{a}"], dma=True)
                        S.op("pool", lambda e, a=a, ic=ic, cv=cv: e.indirect_dma_start(
                            out=vpg[a], out_offset=None, in_=cv,
                            in_offset=bass.IndirectOffsetOnAxis(ap=idx[:, ic:ic + 1], axis=0)),
                            r=KS, w=[f"vpg{a}"], dma=True)
                        ptr = rps()
                        for pr in range(4):
                            TR(ps[ptr][:, pr * 128:(pr + 1) * 128], kpg[a][:, pr * 128:(pr + 1) * 128], ident[:],
                               r=[f"kpg{a}", "ident"], w=[psk[ptr]])
                        ACT(kTp[a].rearrange("p a b -> p (a b)"), ps[ptr][:, 0:512], AF.Copy, r=[psk[ptr]], w=[f"kTp{a}"])
                        for pr in range(4):
                            MM(ps[pz][:, 0:64], kTp[a][:, pr, :], Q[:, pr, :], pr == 0, pr == 3,
                               r=[f"kTp{a}", f"Q{kind}"], w=[psk[pz]])
                        V = vpg[a]
                        vkey = f"vpg{a}"
                    z3 = ps[pz][0:ns, 0:64].rearrange("p (h t) -> p h t", h=8)
                    if kind == 0:
                        if new:
                            pb = rps()
                            MM(ps[pb][0:LS, 0:8], triLE_f[0:LS, 0:LS], lfn[0:LS, :], True, True, r=["lfn", "triLE_f"], w=[psk[pb]])
                            TS("dve", Fb[0:LS, :], ps[pb][0:LS, 0:8], -1.0, ALU.mult, r=[psk[pb]], w=["Fb"])
                        else:
                            S.op("pool", lambda e, a=a, ic=ic: e.indirect_dma_start(
                                out=lfp[a], out_offset=None, in_=c_lf,
                                in_offset=bass.IndirectOffsetOnAxis(ap=idx[:, ic:ic + 1], axis=0)),
                                r=KS, w=[f"lfp{a}"], dma=True)
                            pb, pb2 = rps(), rps()
                            MM(ps[pb][:, 0:8], triS_f[:], lfp[a], True, True, r=[f"lfp{a}", "triS_f"], w=[psk[pb]])
                            MM(ps[pb2][:, 0:8], ones_f[:], lfp[a], True, True, r=[f"lfp{a}", "ones_f"], w=[psk[pb2]])
                            TT("dve", Fb, ps[pb][:, 0:8], carF, ALU.add, r=[psk[pb], "carF"], w=["Fb"])
                            TT("dve", carF, carF, ps[pb2][:, 0:8], ALU.add, r=[psk[pb2], "carF"], w=["carF"])
                        zz = zb[a][0:ns, :].rearrange("p (h t) -> p h t", h=8)
                        TT("dve", zz, z3, Fb[0:ns, :].unsqueeze(2).to_broadcast([ns, 8, LS]), ALU.add,
                           r=[psk[pz], "Fb"], w=[f"zb{a}"])
                        P = Pw[a]
                        ACT(P[0:ns, :], zb[a][0:ns, :], AF.Exp, r=[f"zb{a}"], w=[f"Pw{a}"])
                        if new:
                            P3 = P[0:ns, :].rearrange("p (h t) -> p h t", h=8)
                            TT("dve", P3, P3, triLE_f[0:LS, 0:LS].unsqueeze(1).to_broadcast([LS, 8, LS]), ALU.mult,
                               r=[f"Pw{a}", "triLE_f"], w=[f"Pw{a}"])
                        MM(ps[po][0:64, 0:512], P[0:ns, :], V[0:ns, :], first, last, r=[f"Pw{a}", vkey], w=[psk[po]])
                        MM(ps[pd][0:64, 0:2], P[0:ns, :], ones_f[0:ns, 0:2], first, last, r=[f"Pw{a}", "ones_f"], w=[psk[pd]])
                    else:
                        zz = zb[a][0:ns, :].rearrange("p (h t) -> p h t", h=8)
                        TT("dve", zz, z3, bsb[0:ns, :].unsqueeze(2).to_broadcast([ns, 8, LS]), ALU.add,
                           r=[psk[pz], "bsb"], w=[f"zb{a}"])
                        ACT(e_[a][0:ns, :], zb[a][0:ns, :], AF.Exp, r=[f"zb{a}"], w=[f"e{a}"])
                        ACT(sp_[a][0:ns, :], e_[a][0:ns, :], AF.Ln, r=[f"e{a}", "oneb"], w=[f"sp{a}"], bias=oneb[0:ns, 0:1])
                        if new:
                            s3 = sp_[a][0:ns, :].rearrange("p (h t) -> p h t", h=8)
                            TT("dve", s3, s3, mLTn[0:LS, :].unsqueeze(1).to_broadcast([LS, 8, LS]), ALU.mult,
                               r=[f"sp{a}"] + KS, w=[f"sp{a}"])
                        pc_, pt_ = rps(), rps()
                        MM(ps[pc_][0:ns, 0:64], triI_f[0:ns, 0:ns], sp_[a][0:ns, :], True, True, r=["triI_f", f"sp{a}"], w=[psk[pc_]])
                        MM(ps[pt_][:, 0:64], ones_f[0:ns, :], sp_[a][0:ns, :], True, True, r=["ones_f", f"sp{a}"], w=[psk[pt_]])
                        TT("dve", zb[a][0:ns, :], zb[a][0:ns, :], carry[0:ns, :], ALU.subtract, r=[f"zb{a}", "carry"], w=[f"zb{a}"])
                        TT("dve", zb[a][0:ns, :], zb[a][0:ns, :], ps[pc_][0:ns, 0:64], ALU.subtract, r=[f"zb{a}", psk[pc_]], w=[f"zb{a}"])
                        P = Pw[a]
                        ACT(P[0:ns, :], zb[a][0:ns, :], AF.Exp, r=[f"zb{a}"], w=[f"Pw{a}"])
                        if new:
                            P3 = P[0:ns, :].rearrange("p (h t) -> p h t", h=8)
                            TT("dve", P3, P3, mLTn[0:LS, :].unsqueeze(1).to_broadcast([LS, 8, LS]), ALU.mult,
                               r=[f"Pw{a}"] + KS, w=[f"Pw{a}"])
                        TT("dve", carry, carry, ps[pt_][:, 0:64], ALU.add, r=["carry", psk[pt_]], w=["carry"])
                        MM(ps[po][0:64, 0:512], P[0:ns, :], V[0:ns, :], first, last, r=[f"Pw{a}", vkey], w=[psk[po]])
                t3 = tmpo[0:64].rearrange("p a b -> p (a b)")
                TT("dve", tmpo[0:64], ps[po][0:64, 0:512].rearrange("p (h d) -> p h d", h=8),
                   bm[0:64, :].unsqueeze(2).to_broadcast([64, 8, 64]), ALU.mult, r=[psk[po]] + KS, w=["tmpo"])
                S.op("dve", lambda e: e.reduce_sum(out=red[0:64, :], in_=tmpo[0:64].rearrange("p h d -> p d h"), axis=AX.X),
                     r=["tmpo"], w=["red"])
                if kind == 0:
                    S.op("dve", lambda e: e.reciprocal(out=rd[0:64, :], in_=ps[pd][0:64, 0:2]), r=[psk[pd]], w=["rd"])
                    TS("dve", o16[0:64, :], red[0:64, :], rd[0:64, 0:1], ALU.mult, r=["red", "rd"], w=["o16"])
                else:
                    CP("dve", o16[0:64, :], red[0:64, :], r=["red"], w=["o16"])
                for hd in range(8):
                    r0 = kind * 512 + hd * 64
                    DMAX("sp", mixed[r0:r0 + 64, col:col + LS].rearrange("d t -> t d"), o16[hd * LS:(hd + 1) * LS, :],
                         r=["o16"], w=["mixedall"])

    ctx = Ctx()
    ctx.__dict__.update(locals())
    return ctx


def program(T, PAST, mixers=True, NPHYS=2560):
    c = build(T, PAST, NPHYS)
    c.ada_phase()
    for l in range(2):
        if l == 0:
            c.pass_mod(c.xT, 0, 0, store_x=True)
        c.pass_ffn_in(l, 0)
        c.pass_out(c.hid, "hid", DFF, c.ffn_w_out[l, 0], l, 2, True, (l, 3))
        if mixers and l == 0:
            c.proj0()
            c.s5_phase()
            c.hgrn_phase()
            c.token_bufs()
            c.pass_out(c.mixed, "mixed", D, c.ab_w_out, l, 5, False, (l, 6))
        elif mixers and l == 1:
            c.proj1()
            if STAGE >= 2:
                c.attn_phase()
            if STAGE >= 3:
                c.attn_sample_phase()
            c.token_bufs()
            c.pass_out(c.mixed, "mixed", D, c.cd_w_out, l, 5, False, (l, 6))
        else:
            c.pass_mod(c.xs, l, 6, store_x=False)
        c.pass_ffn_in(l, 1)
        c.pass_out(c.hid, "hid", DFF, c.ffn_w_out[l, 1], l, 8, True, (1, 0) if l == 0 else "final")
    c.S.emit()
    return c


def host_inputs(inp, T, ncores=8, layers=2):
    maps = []
    xp = inp["x_prompt"]
    f32 = np.float32
    a_re, a_im, ld = inp["s5_a_re"], inp["s5_a_im"], inp["s5_log_dt"]
    s5_st = np.stack([a_re.reshape(16, 128).T, a_im.reshape(16, 128).T,
                      np.repeat(ld.reshape(16, 2, 1), 64, axis=2).reshape(16, 128).T]).astype(f32)
    s5_ch = np.zeros((3, 4, 128, 512), f32)
    s5_bblk = np.zeros((2, 4, 128, 512), f32)
    for c in range(4):
        s5_ch[0, c] = a_re[8 * c:8 * c + 8].reshape(1, 512)
        s5_ch[1, c] = a_im[8 * c:8 * c + 8].reshape(1, 512)
        s5_ch[2, c] = np.repeat(ld[8 * c:8 * c + 8], 64).reshape(1, 512)
        for ri, b in enumerate((inp["s5_b_re"], inp["s5_b_im"])):
            blk = np.zeros((8, 16, 8, 64), f32)
            for g in range(8):
                blk[g, :, g, :] = b[8 * c + g].T
            s5_bblk[ri, c] = blk.reshape(128, 512)
    s5_cblk = np.zeros((2, 16, 128, 128), f32)
    for j in range(16):
        for ri, cc in enumerate((inp["s5_c_re"], inp["s5_c_im"])):
            blk = np.zeros((2, 64, 8, 16), f32)
            for gg in range(2):
                blk[gg, :, 2 * (j % 4) + gg, :] = cc[2 * j + gg].T
            s5_cblk[ri, j] = blk.reshape(128, 128)
    common = {
        "ada_w": inp["ada_w"],
        "ada_bT": np.ascontiguousarray(inp["ada_b"].reshape(2, NADA * KT, 128).transpose(0, 2, 1)),
        "ffn_w_in": inp["ffn_w_in"], "ffn_w_out": inp["ffn_w_out"],
        "fnw": np.ascontiguousarray(inp["final_norm_w"].reshape(KT, 128).T),
        "ab_w_in": inp["ab_w_in"], "ab_w_out": inp["ab_w_out"],
        "s5_st": s5_st, "s5_ch": s5_ch, "s5_bblk": s5_bblk, "s5_cblk": s5_cblk,
        "s5_dfm": np.ascontiguousarray(inp["s5_d"].reshape(4, 128).T),
        "glu_w": inp["s5_glu_w"], "glu_bfm": np.ascontiguousarray(inp["s5_glu_b"].reshape(4, 128).T),
        "lbl_fm": np.ascontiguousarray(inp["hgrn_lb_logits"].reshape(3, 4, 128).transpose(2, 1, 0)),
        "hnw": np.ascontiguousarray(inp["hgrn_norm_w"].reshape(128, 1)),
        "cd_w_in": inp["cd_w_in"], "cd_w_out": inp["cd_w_out"],
        "bfb_in": np.ascontiguousarray(np.broadcast_to(inp["cd_b_f"].reshape(1, 8), (128, 8))).astype(f32),
        "bsb_in": np.ascontiguousarray(np.broadcast_to(inp["cd_b_sb"].reshape(1, 8), (128, 8))).astype(f32),
        "c_fk": inp["cache_fox_k"].reshape(-1, 512), "c_fv": inp["cache_fox_v"].reshape(-1, 512),
        "c_lf": inp["cache_fox_logf"].reshape(-1, 8),
        "c_sk": inp["cache_sb_k"].reshape(-1, 512), "c_sv": inp["cache_sb_v"].reshape(-1, 512),
    }
    for c in range(ncores):
        b = c % xp.shape[0]
        sl = slice(NS * c, NS * (c + 1))
        xs_ = inp["x_sample"][sl].reshape(NSTOK, D)
        xT = np.ascontiguousarray(np.concatenate([xp[b, :T], xs_], axis=0).T)
        cT = np.ascontiguousarray(np.concatenate([inp["c_prompt"][b:b + 1], inp["c_sample"][sl]], axis=0).T)
        s5s = np.stack([inp["state_s5_re"][sl].reshape(NS, 16, 128).transpose(2, 1, 0),
                        inp["state_s5_im"][sl].reshape(NS, 16, 128).transpose(2, 1, 0)], axis=1)
        m = dict(common)
        m.update({
            "xT": xT, "cT": cT,
            "s5s_in": np.ascontiguousarray(s5s.astype(f32)),
            "hgs_in": np.ascontiguousarray(inp["state_hgrn"][sl].reshape(NS * 4, 128, 128)),
            "pt_rep": np.ascontiguousarray(np.broadcast_to(inp["page_table"][sl].reshape(1, -1), (128, NS * inp["page_table"].shape[1]))).astype(np.int32),
        })
        maps.append(m)
    return maps


def assemble(res, T, nb=2):
    R_ = res
    ncores = len(R_)
    def prm(f):
        return np.stack([f(R_[b]) for b in range(nb)])
    def smp(f):
        return np.concatenate([f(R_[c]) for c in range(ncores)], axis=0)
    y_p = prm(lambda r: r["yT"][:, :T].T)
    y_s = smp(lambda r: r["yT"][:, T:].T.reshape(NS, LS, D))
    s5r_p = prm(lambda r: r["s5p_out"][:, 0, :].T.reshape(32, 64))
    s5i_p = prm(lambda r: r["s5p_out"][:, 1, :].T.reshape(32, 64))
    hg_p = prm(lambda r: r["hgp_out"])
    fk_p = prm(lambda r: r["qk1"][512:1024, :T].T.reshape(T, 8, 64))
    fv_p = prm(lambda r: r["fv_out"][:T].reshape(T, 8, 64))
    flf_p = prm(lambda r: r["logf_out"][:T])
    sk_p = prm(lambda r: r["qk1"][1536:2048, :T].T.reshape(T, 8, 64))
    sv_p = prm(lambda r: r["sv_out"][:T].reshape(T, 8, 64))
    s5r_s = smp(lambda r: r["s5s_out"][:, 0].transpose(2, 1, 0).reshape(NS, 32, 64))
    s5i_s = smp(lambda r: r["s5s_out"][:, 1].transpose(2, 1, 0).reshape(NS, 32, 64))
    hg_s = smp(lambda r: r["hgs_out"].reshape(NS, 4, 128, 128))
    fk_s = smp(lambda r: r["qk1"][512:1024, T:].T.reshape(NS, LS, 8, 64))
    fv_s = smp(lambda r: r["fv_out"][T:].reshape(NS, LS, 8, 64))
    flf_s = smp(lambda r: r["logf_out"][T:].reshape(NS, LS, 8))
    sk_s = smp(lambda r: r["qk1"][1536:2048, T:].T.reshape(NS, LS, 8, 64))
    sv_s = smp(lambda r: r["sv_out"][T:].reshape(NS, LS, 8, 64))
    outs = (y_p, y_s, s5r_p, s5i_p, hg_p, fk_p, fv_p, flf_p, sk_p, sv_p,
            s5r_s, s5i_s, hg_s, fk_s, fv_s, flf_s, sk_s, sv_s)
    return tuple(np.ascontiguousarray(o, dtype=np.float32) for o in outs)


def kernel(**inputs):
    inp = {k: np.asarray(v) for k, v in inputs.items()}
    T = inp["x_prompt"].shape[1]
    PAST = inp["page_table"].shape[1] * 128
    NPHYS = inp["cache_fox_k"].shape[0]
    c = program(T, PAST, True, NPHYS)
    maps = host_inputs(inp, T)
    res = run_bass_kernel_spmd(c.nc, maps, core_ids=list(range(8)))
    return assemble(res.results, T, inp["x_prompt"].shape[0])
```
